# Optimizing a Trainium2 kernel written in Bass

```python
import math
import jax, jax.numpy as jnp
from jax import lax
import numpy as np

D_MODEL = 2048
BATCH = 4
SEQ = 8192
DEPTH = 1
DEC_BATCH = 16
DEC_SEQ = 32
PAST_LEN = 2048

CHUNK = 64
Q_BLOCK = 128
D_MIX = D_MODEL
DA_WIDTH = D_MIX // 2
DA_HEADS = 8
DA_V_DIM = DA_WIDTH // DA_HEADS
DA_QK_DIM = DA_V_DIM // 2
RW_WIDTH = D_MIX - DA_WIDTH
RW_HEAD = 64
RW_HEADS = RW_WIDTH // RW_HEAD
RW_LORA_W = 64
RW_LORA_A = 64
ROPE_THETA = 10000.0
NORM_EPS = 1e-6
LNX_EPS = 64e-5
RW_SHIFT_COLS = 3 * RW_WIDTH + RW_LORA_W + RW_LORA_A
IN_COLS = 4 * DA_WIDTH + RW_SHIFT_COLS + RW_WIDTH
NEG_INF = -1e30

kernel_name = "hymba_diffattn_rwkv7_stream_step"


def rms_norm(x, g):
    xf = x.astype(jnp.float32)
    xf = xf * lax.rsqrt(jnp.mean(xf * xf, axis=-1, keepdims=True) + NORM_EPS)
    return (xf * g.astype(jnp.float32)).astype(x.dtype)


def rope(x, pos):
    half = x.shape[-1] // 2
    inv_freq = ROPE_THETA ** (-jnp.arange(half, dtype=jnp.float32) / half)
    ang = pos.astype(jnp.float32)[:, None] * inv_freq[None, :]
    cos = jnp.cos(ang)[None, :, None, None, :]
    sin = jnp.sin(ang)[None, :, None, None, :]
    xf = x.astype(jnp.float32)
    x1, x2 = xf[..., :half], xf[..., half:]
    return jnp.concatenate([x1 * cos - x2 * sin, x2 * cos + x1 * sin], axis=-1).astype(x.dtype)


def diff_attention(q, k, v, q_pos, k_pos, lam):
    s = jnp.einsum('bqhcd,bkhcd->bhcqk', q, k, preferred_element_type=jnp.float32) * (DA_QK_DIM ** -0.5)
    mask = (k_pos[None, :] // CHUNK) <= (q_pos[:, None] // CHUNK)
    s = jnp.where(mask, s, NEG_INF)
    p = jax.nn.softmax(s, axis=-1)
    attn = p[:, :, 0] - lam * p[:, :, 1]
    return jnp.einsum('bhqk,bkhd->bqhd', attn.astype(v.dtype), v)


def wkv_scan(r, decay, k, v, a_vec, b_vec, s0):
    def step(S, inp):
        r_t, w_t, k_t, v_t, a_t, b_t = inp
        sa = jnp.einsum('bhij,bhj->bhi', S, a_t)
        S = S * w_t[:, :, None, :] + sa[..., None] * b_t[:, :, None, :] + v_t[..., None] * k_t[:, :, None, :]
        y = jnp.einsum('bhij,bhj->bhi', S, r_t)
        return S, y
    xs = tuple(jnp.moveaxis(t, 1, 0) for t in (r, decay, k, v, a_vec, b_vec))
    s_final, ys = lax.scan(step, s0, xs)
    return jnp.moveaxis(ys, 0, 1), s_final


def trunk_layer(x, past_k, past_v, wkv0, shift0, layer_idx, norm_g, w_in, q_norm_g, k_norm_g,
                lambda_q1, lambda_k1, lambda_q2, lambda_k2, subln_g, shift_mu, w0, w_up, a0, a_up,
                k_k, k_a, r_k, lnx_g, lnx_b, w_out):
    B, T, _ = x.shape
    P = past_k.shape[1]
    pos = P + jnp.arange(T)
    xn = rms_norm(x, norm_g)
    proj = xn @ w_in
    q, k, v, g_att, p_rw, g_rw = jnp.split(
        proj, [DA_WIDTH, 2 * DA_WIDTH, 3 * DA_WIDTH, 4 * DA_WIDTH, 4 * DA_WIDTH + RW_SHIFT_COLS], axis=-1)

    q = rope(rms_norm(q.reshape(B, T, DA_HEADS, 2, DA_QK_DIM), q_norm_g), pos)
    k = rope(rms_norm(k.reshape(B, T, DA_HEADS, 2, DA_QK_DIM), k_norm_g), pos)
    v = v.reshape(B, T, DA_HEADS, DA_V_DIM)
    keys = jnp.concatenate([past_k, k], axis=1)
    vals = jnp.concatenate([past_v, v], axis=1)
    k_pos = jnp.arange(P + T)
    lam_init = 0.8 - 0.6 * math.exp(-0.3 * layer_idx)
    f32 = jnp.float32
    lam = (jnp.exp(jnp.sum(lambda_q1.astype(f32) * lambda_k1.astype(f32)))
           - jnp.exp(jnp.sum(lambda_q2.astype(f32) * lambda_k2.astype(f32))) + lam_init)
    if T > Q_BLOCK and T % Q_BLOCK == 0:
        nb = T // Q_BLOCK
        qb = q.reshape(B, nb, Q_BLOCK, DA_HEADS, 2, DA_QK_DIM).swapaxes(0, 1)
        pb = pos.reshape(nb, Q_BLOCK)
        ob = lax.map(lambda qp: diff_attention(qp[0], keys, vals, qp[1], k_pos, lam), (qb, pb))
        o = ob.swapaxes(0, 1).reshape(B, T, DA_HEADS, DA_V_DIM)
    else:
        o = diff_attention(q, keys, vals, pos, k_pos, lam)
    o = rms_norm(o, subln_g) * (1.0 - lam_init)
    att_out = o.reshape(B, T, DA_WIDTH) * jax.nn.silu(g_att)

    prev = jnp.concatenate([shift0, p_rw[:, :-1]], axis=1)
    xs = p_rw + shift_mu * (prev - p_rw)
    r, kr, vr, wl, al = jnp.split(xs, [RW_WIDTH, 2 * RW_WIDTH, 3 * RW_WIDTH, 3 * RW_WIDTH + RW_LORA_W], axis=-1)
    w = -jax.nn.softplus(-(w0 + jnp.tanh(wl) @ w_up)) - 0.5
    decay = jnp.exp(-jnp.exp(w.astype(f32)))
    a = jax.nn.sigmoid((a0 + al @ a_up).astype(f32))
    kk = (kr * k_k).astype(f32).reshape(B, T, RW_HEADS, RW_HEAD)
    kk = kk / jnp.maximum(jnp.linalg.norm(kk, axis=-1, keepdims=True), 1e-12)
    kmod = kr.astype(f32) * (1.0 + (a - 1.0) * k_a.astype(f32))
    hs = (B, T, RW_HEADS, RW_HEAD)
    r_h = r.astype(f32).reshape(hs)
    k_h = kmod.reshape(hs)
    v_h = vr.astype(f32).reshape(hs)
    a_h = a.reshape(hs)
    y, s_final = wkv_scan(r_h, decay.reshape(hs), k_h, v_h, -kk, kk * a_h, wkv0.astype(f32))
    mu = jnp.mean(y, axis=-1, keepdims=True)
    var = jnp.mean(jnp.square(y - mu), axis=-1, keepdims=True)
    y = ((y - mu) * lax.rsqrt(var + LNX_EPS)).reshape(B, T, RW_WIDTH) * lnx_g.astype(f32) + lnx_b.astype(f32)
    bonus = jnp.sum(r_h * k_h * r_k.astype(f32), axis=-1, keepdims=True) * v_h
    y = y + bonus.reshape(B, T, RW_WIDTH)
    rw_out = y.astype(x.dtype) * jax.nn.silu(g_rw)

    out = jnp.concatenate([att_out, rw_out], axis=-1) @ w_out
    return x + out, k, v, s_final.astype(wkv0.dtype), p_rw[:, -1:]


def setup_inputs(seed: int = 0) -> dict:
    key = jax.random.key(seed)
    ks = jax.random.split(key, 32)
    n = lambda i, shape: jax.random.normal(ks[i], shape, jnp.float32)
    return {
        "x_prompt": n(0, (BATCH, SEQ, D_MODEL)),
        "x_sample": n(1, (DEC_BATCH, DEC_SEQ, D_MODEL)),
        "cache_attn_k": n(2, (DEPTH, DEC_BATCH, PAST_LEN, DA_HEADS, 2, DA_QK_DIM)),
        "cache_attn_v": n(3, (DEPTH, DEC_BATCH, PAST_LEN, DA_HEADS, DA_V_DIM)),
        "state_rwkv_wkv": 0.3 * n(4, (DEPTH, DEC_BATCH, RW_HEADS, RW_HEAD, RW_HEAD)),
        "state_rwkv_shift": n(5, (DEPTH, DEC_BATCH, 1, RW_SHIFT_COLS)),
        "norm_g": 1.0 + 0.02 * n(6, (DEPTH, D_MODEL)),
        "w_in": n(7, (DEPTH, D_MODEL, IN_COLS)) * D_MODEL ** -0.5,
        "q_norm_g": 1.0 + 0.02 * n(8, (DEPTH, DA_QK_DIM)),
        "k_norm_g": 1.0 + 0.02 * n(9, (DEPTH, DA_QK_DIM)),
        "lambda_q1": 0.1 * n(10, (DEPTH, DA_QK_DIM)),
        "lambda_k1": 0.1 * n(11, (DEPTH, DA_QK_DIM)),
        "lambda_q2": 0.1 * n(12, (DEPTH, DA_QK_DIM)),
        "lambda_k2": 0.1 * n(13, (DEPTH, DA_QK_DIM)),
        "subln_g": 1.0 + 0.02 * n(14, (DEPTH, DA_V_DIM)),
        "shift_mu": jax.random.uniform(ks[15], (DEPTH, RW_SHIFT_COLS), jnp.float32),
        "w0": 0.3 * n(16, (DEPTH, RW_WIDTH)),
        "w_up": 0.5 * n(17, (DEPTH, RW_LORA_W, RW_WIDTH)) * RW_LORA_W ** -0.5,
        "a0": 0.1 * n(18, (DEPTH, RW_WIDTH)),
        "a_up": 0.5 * n(19, (DEPTH, RW_LORA_A, RW_WIDTH)) * RW_LORA_A ** -0.5,
        "k_k": 0.85 + 0.02 * n(20, (DEPTH, RW_WIDTH)),
        "k_a": 1.0 + 0.02 * n(21, (DEPTH, RW_WIDTH)),
        "r_k": 0.1 * n(22, (DEPTH, RW_HEADS, RW_HEAD)),
        "lnx_g": 1.0 + 0.02 * n(23, (DEPTH, RW_WIDTH)),
        "lnx_b": 0.02 * n(24, (DEPTH, RW_WIDTH)),
        "w_out": n(25, (DEPTH, D_MIX, D_MODEL)) * D_MIX ** -0.5,
    }


def reference(x_prompt, x_sample, cache_attn_k, cache_attn_v, state_rwkv_wkv, state_rwkv_shift,
              norm_g, w_in, q_norm_g, k_norm_g, lambda_q1, lambda_k1, lambda_q2, lambda_k2, subln_g,
              shift_mu, w0, w_up, a0, a_up, k_k, k_a, r_k, lnx_g, lnx_b, w_out):
    yp, ys = x_prompt, x_sample
    B = x_prompt.shape[0]
    dt = x_prompt.dtype
    kp_l, vp_l, sp_l, hp_l, ks_l, vs_l, ss_l, hs_l = [], [], [], [], [], [], [], []
    for l in range(DEPTH):
        lw = (norm_g[l], w_in[l], q_norm_g[l], k_norm_g[l], lambda_q1[l], lambda_k1[l], lambda_q2[l],
              lambda_k2[l], subln_g[l], shift_mu[l], w0[l], w_up[l], a0[l], a_up[l], k_k[l], k_a[l],
              r_k[l], lnx_g[l], lnx_b[l], w_out[l])
        yp, kp, vp, sp, hp = trunk_layer(
            yp, jnp.zeros((B, 0, DA_HEADS, 2, DA_QK_DIM), dt), jnp.zeros((B, 0, DA_HEADS, DA_V_DIM), dt),
            jnp.zeros((B, RW_HEADS, RW_HEAD, RW_HEAD), dt), jnp.zeros((B, 1, RW_SHIFT_COLS), dt), l, *lw)
        ys, ksn, vsn, ssn, hsn = trunk_layer(
            ys, cache_attn_k[l], cache_attn_v[l], state_rwkv_wkv[l], state_rwkv_shift[l], l, *lw)
        kp_l.append(kp); vp_l.append(vp); sp_l.append(sp); hp_l.append(hp)
        ks_l.append(ksn); vs_l.append(vsn); ss_l.append(ssn); hs_l.append(hsn)
    return (yp, ys, jnp.stack(kp_l), jnp.stack(vp_l), jnp.stack(sp_l), jnp.stack(hp_l),
            jnp.stack(ks_l), jnp.stack(vs_l), jnp.stack(ss_l), jnp.stack(hs_l))
```

```python
import math
from contextlib import ExitStack
import numpy as np
import concourse.bass as bass
import concourse.mybir as mybir
from concourse.bass_utils import run_bass_kernel_spmd

F32 = mybir.dt.float32
BF16 = mybir.dt.bfloat16
I32 = mybir.dt.int32
ALU = mybir.AluOpType
AF = mybir.ActivationFunctionType
AX = mybir.AxisListType

CFG = {"NPT": 64, "debug": False, "phases": "A1,B,A2,C,D"}
NST = 4
PAST = 2048
DEC = 32
EPS = 1e-6
LNX_EPS = 64e-5
LAM_INIT = 0.8 - 0.6 * math.exp(-0.3 * 0)


class Res:
    def __init__(self, name, handle=None, excl=False):
        self.name = name
        self.h = handle
        self.excl = excl
        self.writers = {}
        self.readers = {}
        self.dsem = None
        self.dcount = 0

    def __getitem__(self, key):
        return self.h[key]


class Op:
    __slots__ = ("eng", "fn", "reads", "writes", "dma", "sres", "deps", "sig", "sigval", "idx")


COMPUTE = ("pe", "act", "dve", "pool")


class Prog:
    def __init__(self, nc, es):
        self.nc = nc
        self.es = es
        self.root_es = es
        self.ops = []
        self.engs = {"pe": nc.tensor, "act": nc.scalar, "dve": nc.vector, "pool": nc.gpsimd, "sp": nc.sync}
        self.nsem = 0
        self.all_res = []
        self.phase_res = []
        self.free_dsems = []
        self.esem = None
        self.ecount = {e: 0 for e in COMPUTE}
        self.waited = {}
        self.ninst = {e: 0 for e in self.engs}
        self.nwait = 0
        self.nops = 0

    def _reg(self, r):
        self.all_res.append(r)
        if self.es is not self.root_es:
            self.phase_res.append(r)
        return r

    def sb(self, name, shape, dtype):
        self.nalloc = getattr(self, "nalloc", 0) + 1
        name = f"{name}_{self.nalloc}"
        return self._reg(Res(name, self.es.enter_context(self.nc.sbuf_tensor(name, list(shape), dtype))))

    def ps(self, name, shape, dtype):
        self.nalloc = getattr(self, "nalloc", 0) + 1
        name = f"{name}_{self.nalloc}"
        return self._reg(Res(name, self.es.enter_context(self.nc.psum_tensor(name, list(shape), dtype)), excl=True))

    def dres(self, name):
        return self._reg(Res(name))

    def sub(self, res, n):
        return [self._reg(Res(f"{res.name}.{i}", res.h)) for i in range(n)]

    def sem(self, name):
        self.nsem += 1
        return self.root_es.enter_context(self.nc.semaphore(name))

    def op(self, eng, fn, r=(), w=()):
        o = Op()
        o.eng = eng; o.fn = fn; o.reads = tuple(r); o.writes = tuple(w)
        o.dma = False; o.sres = None; o.deps = None; o.sig = False; o.sigval = 0
        o.idx = self.nops
        self.nops += 1
        self.ops.append(o)
        return o

    def pe(self, fn, r=(), w=()): return self.op("pe", fn, r, w)
    def act(self, fn, r=(), w=()): return self.op("act", fn, r, w)
    def dve(self, fn, r=(), w=()): return self.op("dve", fn, r, w)
    def pool(self, fn, r=(), w=()): return self.op("pool", fn, r, w)

    def dma(self, q, out, in_, r=(), w=(), sres=None, **kw):
        if CFG.get("nostore") and q == "pool" and w and w[0].h is None:
            return None
        o = self.op(q, lambda e: e.dma_start(out=out, in_=in_, **kw), r, w)
        o.dma = True
        o.sres = sres
        assert sres is not None
        return o

    def _key(self, o):
        return ("d", id(o.sres)) if o.dma else o.eng

    def flush(self):
        if self.esem is None:
            self.esem = {e: self.sem("S_" + e) for e in COMPUTE}
        esem, ecount, waited = self.esem, self.ecount, self.waited
        last = {}
        for o in self.ops:
            deps = {}

            def add(d):
                if d is o:
                    return
                if (not d.dma) and (not o.dma) and d.eng == "pe" and o.eng == "pe":
                    return
                deps[d.idx] = d
            for r in o.reads:
                for d in r.writers.values():
                    add(d)
                if r.excl:
                    for d in r.readers.values():
                        add(d)
            for w in o.writes:
                if w.readers:
                    for d in w.readers.values():
                        add(d)
                    for d in w.writers.values():
                        add(d)
                else:
                    for d in w.writers.values():
                        if o.dma and d.dma:
                            continue
                        add(d)
            k = self._key(o)
            for r in o.reads:
                r.readers[k] = o
            for w in o.writes:
                if w.readers:
                    w.writers = {k: o}
                    w.readers = {}
                else:
                    w.writers[k] = o
            o.deps = list(deps.values())
            for d in o.deps:
                d.sig = True
            if not o.dma:
                last[o.eng] = o
        for o in last.values():
            o.sig = True
        for o in self.ops:
            eng = self.engs[o.eng]
            need = {}
            for d in o.deps:
                if d.dma:
                    s = d.sres
                    need[("d", id(s.dsem))] = (s.dsem, s.dcount)
                else:
                    v = d.sigval
                    assert v > 0
                    if d.eng not in need or need[d.eng][1] < v:
                        need[d.eng] = (esem[d.eng], v)
            for key, (s, v) in need.items():
                wk = (o.eng, key)
                if waited.get(wk, 0) >= v:
                    continue
                waited[wk] = v
                eng.wait_ge(s, v)
                self.nwait += 1
            ins = o.fn(eng)
            self.ninst[o.eng] += 1
            if o.dma:
                s = o.sres
                if s.dsem is None:
                    if self.free_dsems:
                        s.dsem, s.dcount = self.free_dsems.pop()
                    else:
                        s.dsem, s.dcount = self.sem("D_" + s.name), 0
                s.dcount += 16
                ins.then_inc(s.dsem, 16)
            elif o.sig:
                ecount[o.eng] += 1
                o.sigval = ecount[o.eng]
                ins.then_inc(esem[o.eng], 1)
        dsems = {}
        for r in self.all_res:
            if r.dsem is not None:
                dsems[id(r.dsem)] = (r.dsem, r.dcount)
        for s, c in self.free_dsems:
            dsems[id(s)] = (s, c)
        for en, eng in self.engs.items():
            for e2 in COMPUTE:
                if e2 == en or ecount[e2] == 0:
                    continue
                wk = (en, e2)
                if waited.get(wk, 0) < ecount[e2]:
                    waited[wk] = ecount[e2]
                    eng.wait_ge(esem[e2], ecount[e2])
            for key, (s, c) in dsems.items():
                wk = (en, ("d", key))
                if waited.get(wk, 0) < c:
                    waited[wk] = c
                    eng.wait_ge(s, c)
        for r in self.all_res:
            r.writers = {}
            r.readers = {}
        self.ops = []

    def end_phase(self):
        self.flush()
        for r in self.phase_res:
            if r.dsem is not None:
                self.free_dsems.append((r.dsem, r.dcount))
                r.dsem = None
        pr = set(id(r) for r in self.phase_res)
        self.all_res = [r for r in self.all_res if id(r) not in pr]
        self.phase_res = []

    @property
    def stats(self):
        return dict(ninst=self.ninst, nwait=self.nwait, nsem=self.nsem, sig=dict(self.ecount))


class Builder:
    def __init__(self):
        self.NPT = CFG["NPT"]
        self.NTT = self.NPT + NST
        self.T = self.NPT * 128
        self.nc = bass.Bass("TRN2", target_bir_lowering=False)
        self.es = ExitStack()
        self.P = Prog(self.nc, self.es)
        self.declare_dram()

    def dram(self, name, shape, dtype, kind):
        kw = {}
        if kind == "Internal":
            kw["addr_space"] = "Local"
        return self.nc.dram_tensor(name, list(shape), dtype, kind=kind, **kw).ap()

    def declare_dram(self):
        T, NTT = self.T, self.NTT
        I = lambda n, s: self.dram(n, s, F32, "ExternalInput")
        O = lambda n, s: self.dram(n, s, F32, "ExternalOutput")
        self.xp = I("xp", [T, 2048])
        self.xs = I("xs", [NST, DEC, 2048])
        self.w_att = I("w_att", [2048, 2048])
        self.w_rw = I("w_rw", [2048, 2176])
        self.w_out = I("w_out", [2048, 1024])
        self.xres_p = I("xres_p", [T, 1024])
        self.xres_s = I("xres_s", [NST, DEC, 1024])
        self.norm_g = I("norm_g", [2048])
        self.qg = I("qg", [64]); self.kg = I("kg", [64])
        self.lq1 = I("lq1", [64]); self.lk1 = I("lk1", [64]); self.lq2 = I("lq2", [64]); self.lk2 = I("lk2", [64])
        self.subln = I("subln", [128])
        self.mu = I("mu", [1664])
        self.w0 = I("w0", [512]); self.a0 = I("a0", [512])
        self.w_up = I("w_up", [64, 512]); self.a_up = I("a_up", [64, 512])
        self.k_k = I("k_k", [512]); self.k_a = I("k_a", [512]); self.r_k = I("r_k", [512])
        self.lnx_g = I("lnx_g", [512]); self.lnx_b = I("lnx_b", [512])
        self.cache_k = I("cache_k", [NST, PAST, 512])
        self.cache_v = I("cache_v", [NST, PAST, 512])
        self.wkv0 = I("wkv0", [NST, 8, 64, 64])
        self.shift0 = I("shift0", [NST, 1664])
        self.y_p = O("y_p", [T, 1024]); self.y_s = O("y_s", [NST, DEC, 1024])
        self.k_p = O("k_p", [T, 512]); self.v_p = O("v_p", [T, 512])
        self.wkv_p = O("wkv_p", [8, 64, 64]); self.shift_p = O("shift_p", [1664])
        self.k_s = O("k_s", [NST, DEC, 512]); self.v_s = O("v_s", [NST, DEC, 512])
        self.wkv_s = O("wkv_s", [NST, 8, 64, 64]); self.shift_s = O("shift_s", [NST, 1664])
        S = lambda n, s, d=BF16: self.dram(n, s, d, "Internal")
        self.QT = S("QT", [4, 128, NTT * 128])
        self.KT = S("KT", [4, 128, NTT * 128])
        self.Vs = S("Vs", [4, 128, NTT, 128])
        self.Gs = S("Gs", [NTT * 128, 512])
        dbg = CFG["debug"]
        self.Mx = self.dram("Mxs", [NTT * 128, 1024], BF16, "Internal")
        if dbg:
            self.Mx_dbg = self.dram("Mx", [NTT * 128, 1024], BF16, "ExternalOutput")
        self.CCH = 8
        self.Mall = [S(f"Mall{i}", [2 * self.CCH * 128, 1024]) for i in range((NTT + self.CCH - 1) // self.CCH)]
        D = self.P.dres
        self.dQT = D("dQT"); self.dKT = D("dKT"); self.dVs = D("dVs"); self.dGs = D("dGs")
        self.dMx = D("dMx"); self.dMall = D("dMall"); self.dOut = D("dOut")
        self.zero_pad()

    def zero_pad(self):
        pass

    def begin_phase(self):
        self.pes = ExitStack()
        self.P.es = self.pes

    def end_phase(self):
        self.P.end_phase()
        self.pes.close()
        self.P.es = self.P.root_es

    def bcast_load(self, name, src, n, q="sp"):
        t = self.P.sb(name, [128, n], F32)
        self.P.dma(q, t[:], src.partition_broadcast(128), w=[t], sres=t)
        return t

    def tile_rows(self, tt):
        return 128 if tt < self.NPT else DEC

    def setup(self):
        P, nc = self.P, self.nc
        NTT, NPT = self.NTT, self.NPT
        self.identf = P.sb("identf", [128, 128], F32)
        self.ident = P.sb("ident", [128, 128], BF16)
        self.maskf = P.sb("maskf", [128, 384], F32)
        self.maskb = P.sb("maskb", [128, 384], BF16)
        self.epst = P.sb("epst", [128, 1], F32)
        self.lnxeps = P.sb("lnxeps", [128, 1], F32)
        idf, mk = self.identf, self.maskf
        P.pool(lambda e: e.memset(self.epst[:], EPS), w=[self.epst])
        P.pool(lambda e: e.memset(self.lnxeps[:], LNX_EPS), w=[self.lnxeps])
        P.pool(lambda e: e.memset(idf[:], 1.0), w=[idf])
        P.pool(lambda e: e.affine_select(out=idf[:], in_=idf[:], compare_op=ALU.is_ge, fill=0.0, base=0,
                                         pattern=[[-1, 128]], channel_multiplier=1), r=[idf], w=[idf])
        P.pool(lambda e: e.affine_select(out=idf[:], in_=idf[:], compare_op=ALU.is_ge, fill=0.0, base=0,
                                         pattern=[[1, 128]], channel_multiplier=-1), r=[idf], w=[idf])
        P.dve(lambda e: e.tensor_copy(out=self.ident[:], in_=idf[:]), r=[idf], w=[self.ident])
        P.pool(lambda e: e.memset(mk[:], 1.0), w=[mk])
        P.pool(lambda e: e.affine_select(out=mk[:, 0:128], in_=mk[:, 0:128], compare_op=ALU.is_gt, fill=0.0, base=0,
                                         pattern=[[1, 128]], channel_multiplier=-1), r=[mk], w=[mk])
        P.pool(lambda e: e.affine_select(out=mk[:, 128:256], in_=mk[:, 128:256], compare_op=ALU.is_ge, fill=0.0, base=0,
                                         pattern=[[1, 128]], channel_multiplier=-1), r=[mk], w=[mk])
        P.pool(lambda e: e.affine_select(out=mk[:, 256:384], in_=mk[:, 256:384], compare_op=ALU.is_gt, fill=0.0, base=0,
                                         pattern=[[-1, 128]], channel_multiplier=1), r=[mk], w=[mk])
        P.dve(lambda e: e.tensor_copy(out=self.maskb[:], in_=mk[:]), r=[mk], w=[self.maskb])
        self.gb = self.bcast_load("gb", self.norm_g, 2048)
        self.gq = self.bcast_load("gq", self.qg, 64)
        self.gk = self.bcast_load("gk", self.kg, 64)
        self.ngq = P.sb("ngq", [128, 32], F32)
        self.ngk = P.sb("ngk", [128, 32], F32)
        for dst, src in ((self.ngq, self.gq), (self.ngk, self.gk)):
            P.dve(lambda e, dst=dst, src=src: e.tensor_scalar(out=dst[:], in0=src[:, 32:64], scalar1=-1.0, scalar2=None,
                                                             op0=ALU.mult), r=[src], w=[dst])
        self.sub_b = self.bcast_load("sub_b", self.subln, 128)
        P.dve(lambda e: e.tensor_scalar(out=self.sub_b[:], in0=self.sub_b[:], scalar1=1.0 - LAM_INIT, scalar2=None,
                                        op0=ALU.mult), r=[self.sub_b], w=[self.sub_b])
        lam_in = [self.bcast_load(n, s, 64) for n, s in (("lq1b", self.lq1), ("lk1b", self.lk1),
                                                         ("lq2b", self.lq2), ("lk2b", self.lk2))]
        lj = P.sb("lj", [128, 64], F32)
        ls = P.sb("ls", [128, 2], F32)
        self.neglam = P.sb("neglam", [128, 1], F32)
        for i in range(2):
            a, b = lam_in[2 * i], lam_in[2 * i + 1]
            P.dve(lambda e, a=a, b=b: e.tensor_tensor(out=lj[:], in0=a[:], in1=b[:], op=ALU.mult), r=[a, b], w=[lj])
            P.dve(lambda e, i=i: e.reduce_sum(out=ls[:, i:i + 1], in_=lj[:], axis=AX.X), r=[lj], w=[ls])
        P.act(lambda e: e.activation(out=ls[:], in_=ls[:], func=AF.Exp), r=[ls], w=[ls])
        P.dve(lambda e: e.tensor_tensor(out=self.neglam[:], in0=ls[:, 1:2], in1=ls[:, 0:1], op=ALU.subtract),
              r=[ls], w=[self.neglam])
        P.dve(lambda e: e.tensor_scalar(out=self.neglam[:], in0=self.neglam[:], scalar1=-LAM_INIT, scalar2=None,
                                        op0=ALU.add), r=[self.neglam], w=[self.neglam])
        self.mq = P.sb("mq", [128, 1], F32)
        self.mk8 = P.sb("mk8", [128, 1], F32)
        P.dve(lambda e: e.tensor_reduce(out=self.mq[:], in_=self.gq[:], axis=AX.X, op=ALU.max,
                                        apply_absolute_value=True), r=[self.gq], w=[self.mq])
        P.dve(lambda e: e.tensor_reduce(out=self.mk8[:], in_=self.gk[:], axis=AX.X, op=ALU.max,
                                        apply_absolute_value=True), r=[self.gk], w=[self.mk8])
        P.dve(lambda e: e.tensor_tensor(out=self.mk8[:], in0=self.mk8[:], in1=self.mk8[:], op=ALU.mult),
              r=[self.mk8], w=[self.mk8])
        P.dve(lambda e: e.tensor_scalar(out=self.mk8[:], in0=self.mk8[:], scalar1=64.0, scalar2=None, op0=ALU.mult),
              r=[self.mk8], w=[self.mk8])
        self.negC_p = P.sb("negC_p", [128, 1], F32)
        P.act(lambda e: e.activation(out=self.negC_p[:], in_=self.mk8[:], func=AF.Sqrt), r=[self.mk8], w=[self.negC_p])
        P.dve(lambda e: e.scalar_tensor_tensor(out=self.negC_p[:], in0=self.negC_p[:], scalar=-1.0, in1=self.mq[:],
                                               op0=ALU.mult, op1=ALU.mult), r=[self.negC_p, self.mq], w=[self.negC_p])

    def rope_tables(self):
        P = self.P
        NTT, NPT = self.NTT, self.NPT
        self.cosT = P.sb("cosT", [128, NTT, 32], F32)
        self.sinT = P.sb("sinT", [128, NTT, 32], F32)
        posi = P.sb("posi", [128, NTT], I32)
        posf = P.sb("posf", [128, NTT], F32)
        P.pool(lambda e: e.iota(posi[:, 0:NPT], pattern=[[128, NPT]], base=0, channel_multiplier=1), w=[posi])
        P.pool(lambda e: e.iota(posi[:, NPT:NTT], pattern=[[0, NST]], base=PAST, channel_multiplier=1), w=[posi])
        P.dve(lambda e: e.tensor_copy(out=posf[:], in_=posi[:]), r=[posi], w=[posf])
        invf = P.sb("invf", [128, 32], F32)
        for i in range(32):
            v = float(np.float32(10000.0) ** np.float32(-i / 32.0))
            P.pool(lambda e, i=i, v=v: e.memset(invf[:, i:i + 1], v), w=[invf])
        ang = P.sb("ang", [128, NTT, 32], F32)
        a2 = P.sb("ang2", [128, NTT, 32], F32)
        ki = P.sb("angk", [128, NTT, 32], I32)
        kf = P.sb("angkf", [128, NTT, 32], F32)
        P.dve(lambda e: e.tensor_tensor(out=ang[:], in0=posf[:, :].unsqueeze(2).broadcast_to([128, NTT, 32]),
                                        in1=invf[:, :].unsqueeze(1).broadcast_to([128, NTT, 32]), op=ALU.mult),
              r=[posf, invf], w=[ang])
        C1 = 6.28125
        C2 = 2.0 * math.pi - C1
        for which, dst, off in (("s", self.sinT, 0.0), ("c", self.cosT, math.pi / 2)):
            P.dve(lambda e, off=off: e.tensor_scalar(out=a2[:], in0=ang[:], scalar1=off, scalar2=1.0 / (2 * math.pi),
                                                     op0=ALU.add, op1=ALU.mult), r=[ang], w=[a2])
            P.dve(lambda e: e.tensor_copy(out=ki[:], in_=a2[:]), r=[a2], w=[ki])
            P.dve(lambda e: e.tensor_copy(out=kf[:], in_=ki[:]), r=[ki], w=[kf])
            P.dve(lambda e, off=off: e.tensor_scalar(out=a2[:], in0=ang[:], scalar1=off, scalar2=None, op0=ALU.add),
                  r=[ang], w=[a2])
            P.dve(lambda e: e.scalar_tensor_tensor(out=a2[:], in0=kf[:], scalar=-C1, in1=a2[:], op0=ALU.mult, op1=ALU.add),
                  r=[kf, a2], w=[a2])
            P.dve(lambda e: e.scalar_tensor_tensor(out=a2[:], in0=kf[:], scalar=-C2, in1=a2[:], op0=ALU.mult, op1=ALU.add),
                  r=[kf, a2], w=[a2])
            P.dve(lambda e: e.tensor_scalar(out=kf[:], in0=a2[:], scalar1=math.pi, scalar2=-2 * math.pi,
                                            op0=ALU.is_gt, op1=ALU.mult), r=[a2], w=[kf])
            P.dve(lambda e: e.tensor_tensor(out=a2[:], in0=a2[:], in1=kf[:], op=ALU.add), r=[a2, kf], w=[a2])
            P.dve(lambda e: e.tensor_scalar(out=kf[:], in0=a2[:], scalar1=-math.pi, scalar2=2 * math.pi,
                                            op0=ALU.is_lt, op1=ALU.mult), r=[a2], w=[kf])
            P.dve(lambda e: e.tensor_tensor(out=a2[:], in0=a2[:], in1=kf[:], op=ALU.add), r=[a2, kf], w=[a2])
            P.dve(lambda e: e.tensor_scalar(out=a2[:], in0=a2[:], scalar1=3.1415925, scalar2=-3.1415925,
                                            op0=ALU.min, op1=ALU.max), r=[a2], w=[a2])
            P.act(lambda e, dst=dst: e.activation(out=dst[:], in_=a2[:], func=AF.Sin), r=[a2], w=[dst])


    def alloc_xnorm(self, nx=2):
        P = self.P
        self.xt = [P.sb(f"xt{i}", [128, 2048], F32) for i in range(nx)]
        self.ss = P.sb("ss", [128, 1], F32)
        self.rstd = P.sb("rstd", [128, 1], F32)
        self.xn = P.sb("xn", [128, 2048], BF16)
        self.junk = self.xn
        self.xnT = [P.sb(f"xnT{i}", [128, 16, 128], BF16) for i in range(nx)]
        self.pT = [P.ps(f"pT{i}", [128, 8, 128], BF16) for i in range(2)]
        self.xcnt = 0

    def xnorm_tile(self, tt):
        P = self.P
        X = self.xt[self.xcnt % len(self.xt)]
        XT = self.xnT[self.xcnt % len(self.xnT)]
        self.xcnt += 1
        if tt < self.NPT:
            P.dma("sp", X[:], self.xp[tt * 128:(tt + 1) * 128, :], w=[X], sres=X)
        else:
            s = tt - self.NPT
            P.pool(lambda e: e.memset(X[:, :], 0.0), w=[X])
            P.dma("sp", X[0:DEC, :], self.xs[s], w=[X], sres=X)
        ss, rstd, xn, junk = self.ss, self.rstd, self.xn, self.junk
        P.act(lambda e: e.activation(out=junk[:], in_=X[:], func=AF.Square, accum_out=ss[:]), r=[X], w=[xn, ss])
        P.act(lambda e: e.activation(out=rstd[:], in_=ss[:], func=AF.Sqrt, scale=1.0 / 2048, bias=self.epst[:, 0:1]),
              r=[ss, self.epst], w=[rstd])
        P.dve(lambda e: e.reciprocal(out=rstd[:], in_=rstd[:]), r=[rstd], w=[rstd])
        P.dve(lambda e: e.scalar_tensor_tensor(out=xn[:], in0=X[:], scalar=rstd[:, 0:1], in1=self.gb[:],
                                               op0=ALU.mult, op1=ALU.mult), r=[X, rstd, self.gb], w=[xn])
        for hf in range(2):
            pt = self.pT[hf]
            for j in range(8):
                k = hf * 8 + j
                P.pe(lambda e, pt=pt, j=j, k=k: e.transpose(out=pt[:, j, :], in_=xn[:, k * 128:(k + 1) * 128],
                                                           identity=self.ident[:]), r=[xn, self.ident], w=[pt])
            if hf == 0:
                P.act(lambda e, pt=pt: e.copy(out=XT[:, 0:8, :], in_=pt[:]), r=[pt], w=[XT])
            else:
                P.dve(lambda e, pt=pt: e.tensor_copy(out=XT[:, 8:16, :], in_=pt[:]), r=[pt], w=[XT])
        return XT

    def phase_A1(self):
        P = self.P
        NTT, NPT = self.NTT, self.NPT
        self.begin_phase()
        self.rope_tables()
        self.alloc_xnorm()
        W = P.sb("Wb", [128, 16, 2048], BF16)
        self.pY = [P.ps(f"pY{i}", [128, 512], F32) for i in range(2)]
        self.pTq = P.ps("pTq", [128, 8, 128], BF16)
        for k in range(16):
            P.dma("pool", W[:, k, :], self.w_att[k * 128:(k + 1) * 128, :], w=[W], sres=W)
        pY = self.pY
        pTq = self.pTq
        sq = P.sb("a1_sq", [128, 512], F32)
        ssq = P.sb("a1_ssq", [128, 8], F32)
        rq = P.sb("a1_rq", [128, 8], F32)
        qs = P.sb("a1_qs", [128, 8, 64], F32)
        tmp = P.sb("a1_tmp", [128, 8, 64], F32)
        of = [P.sb(f"a1_of{i}", [128, 8, 64], F32) for i in range(2)]
        ob = P.sb("a1_ob", [128, 512], BF16)
        qT = [P.sb(f"a1_qT{i}", [128, 4, 128], BF16) for i in range(2)]
        vf = [P.sb(f"a1_vf{i}", [128, 512], F32) for i in range(2)]
        vb = [P.sb(f"a1_vb{i}", [128, 512], BF16) for i in range(2)]
        gbf = [P.sb(f"a1_gb{i}", [128, 512], BF16) for i in range(2)]
        tabs = {}
        for nm in ("q", "k"):
            tabs[nm] = (P.sb(f"a1_CC{nm}", [128, 64], F32), P.sb(f"a1_S1{nm}", [128, 32], F32),
                        P.sb(f"a1_nS2{nm}", [128, 32], F32))
        cnt = 0
        STOP = CFG.get("stop", 99)
        for tt in CFG.get("tiles", range(NTT)):
            XT = self.xnorm_tile(tt)
            if STOP <= 1:
                continue
            for nm, g, ng in (("q", self.gq, self.ngq), ("k", self.gk, self.ngk)):
                CC, S1, nS2 = tabs[nm]
                cos_b = self.cosT[:, tt, :].unsqueeze(1).broadcast_to([128, 2, 32])
                P.pool(lambda e, CC=CC, g=g, cos_b=cos_b: e.tensor_tensor(
                    out=CC[:, :].rearrange("p (a b) -> p a b", a=2), in0=g[:, :].rearrange("p (a b) -> p a b", a=2),
                    in1=cos_b, op=ALU.mult), r=[g, self.cosT], w=[CC])
                P.pool(lambda e, S1=S1, g=g, tt=tt: e.tensor_tensor(out=S1[:], in0=g[:, 0:32], in1=self.sinT[:, tt, :],
                                                                   op=ALU.mult), r=[g, self.sinT], w=[S1])
                P.pool(lambda e, nS2=nS2, ng=ng, tt=tt: e.tensor_tensor(out=nS2[:], in0=ng[:], in1=self.sinT[:, tt, :],
                                                                        op=ALU.mult), r=[ng, self.sinT], w=[nS2])
            for cb, nm in enumerate(("q", "k", "v", "g")):
                py = pY[cnt % 2]
                cnt += 1
                for k in range(16):
                    P.pe(lambda e, py=py, k=k, cb=cb, XT=XT: e.matmul(
                        py[:], lhsT=XT[:, k, :], rhs=W[:, k, cb * 512:(cb + 1) * 512], start=(k == 0), stop=(k == 15)),
                        r=[XT, W], w=[py])
                if STOP <= 2:
                    continue
                if nm in ("q", "k"):
                    CC, S1, nS2 = tabs[nm]
                    o_f = of[cnt % 2]
                    P.act(lambda e, py=py: e.activation(out=sq[:], in_=py[:], func=AF.Square), r=[py], w=[sq])
                    P.dve(lambda e: e.reduce_sum(out=ssq[:], in_=sq[:, :].rearrange("p (g d) -> p g d", g=8), axis=AX.X),
                          r=[sq], w=[ssq])
                    P.act(lambda e: e.activation(out=rq[:], in_=ssq[:], func=AF.Sqrt, scale=1.0 / 64,
                                                 bias=self.epst[:, 0:1]), r=[ssq, self.epst], w=[rq])
                    P.dve(lambda e: e.reciprocal(out=rq[:], in_=rq[:]), r=[rq], w=[rq])
                    if STOP <= 3:
                        continue
                    P.dve(lambda e, py=py: e.tensor_tensor(
                        out=qs[:], in0=py[:, :].rearrange("p (g d) -> p g d", g=8),
                        in1=rq[:, :].unsqueeze(2).broadcast_to([128, 8, 64]), op=ALU.mult), r=[py, rq], w=[qs])
                    if STOP <= 4:
                        continue
                    P.dve(lambda e, o_f=o_f, CC=CC: e.tensor_tensor(
                        out=o_f[:], in0=qs[:], in1=CC[:, :].unsqueeze(1).broadcast_to([128, 8, 64]), op=ALU.mult),
                        r=[qs, CC], w=[o_f])
                    P.pool(lambda e, nS2=nS2: e.tensor_tensor(
                        out=tmp[:, :, 0:32], in0=qs[:, :, 32:64],
                        in1=nS2[:, :].unsqueeze(1).broadcast_to([128, 8, 32]), op=ALU.mult), r=[qs, nS2], w=[tmp])
                    P.pool(lambda e, S1=S1: e.tensor_tensor(
                        out=tmp[:, :, 32:64], in0=qs[:, :, 0:32],
                        in1=S1[:, :].unsqueeze(1).broadcast_to([128, 8, 32]), op=ALU.mult), r=[qs, S1], w=[tmp])
                    P.dve(lambda e, o_f=o_f: e.tensor_tensor(out=o_f[:], in0=o_f[:], in1=tmp[:], op=ALU.add),
                          r=[o_f, tmp], w=[o_f])
                    if STOP <= 5:
                        continue
                    P.act(lambda e, o_f=o_f: e.copy(out=ob[:], in_=o_f[:, :, :].rearrange("p g d -> p (g d)")),
                          r=[o_f], w=[ob])
                    if nm == "k":
                        if tt < NPT:
                            P.dma("pool", self.k_p[tt * 128:(tt + 1) * 128, :], o_f[:, :, :].rearrange("p g d -> p (g d)"),
                                  r=[o_f], w=[self.dOut], sres=o_f)
                        else:
                            P.dma("pool", self.k_s[tt - NPT], o_f[0:DEC, :, :].rearrange("p g d -> p (g d)"),
                                  r=[o_f], w=[self.dOut], sres=o_f)
                    qt = qT[cnt % 2]
                    for h in range(4):
                        P.pe(lambda e, h=h: e.transpose(out=pTq[:, h, :], in_=ob[:, h * 128:(h + 1) * 128],
                                                        identity=self.ident[:]), r=[ob, self.ident], w=[pTq])
                    P.dve(lambda e, qt=qt: e.tensor_copy(out=qt[:], in_=pTq[:, 0:4, :]), r=[pTq], w=[qt])
                    dst, dres = (self.QT, self.dQT) if nm == "q" else (self.KT, self.dKT)
                    P.dma("pool", dst[:, :, tt * 128:(tt + 1) * 128].rearrange("h p c -> p h c"), qt[:],
                          r=[qt], w=[dres], sres=qt)
                elif STOP <= 6:
                    continue
                elif nm == "v" and CFG.get("nov"):
                    continue
                elif nm == "v":
                    v_f = vf[cnt % 2]
                    v_b = vb[cnt % 2]
                    P.act(lambda e, py=py, v_f=v_f: e.copy(out=v_f[:], in_=py[:]), r=[py], w=[v_f])
                    P.pool(lambda e, v_f=v_f, v_b=v_b: e.tensor_copy(out=v_b[:], in_=v_f[:]), r=[v_f], w=[v_b])
                    if tt < NPT:
                        P.dma("pool", self.v_p[tt * 128:(tt + 1) * 128, :], v_f[:], r=[v_f], w=[self.dOut], sres=v_f)
                    else:
                        P.dma("pool", self.v_s[tt - NPT], v_f[0:DEC, :], r=[v_f], w=[self.dOut], sres=v_f)
                    P.dma("pool", self.Vs[:, :, tt, :].rearrange("h p d -> p h d"),
                          v_b[:, :].rearrange("p (h d) -> p h d", h=4), r=[v_b], w=[self.dVs], sres=v_b)
                else:
                    g_b = gbf[cnt % 2]
                    P.act(lambda e, py=py, g_b=g_b: e.activation(out=g_b[:], in_=py[:], func=(AF.Copy if CFG.get("nosilu") else AF.Silu)), r=[py], w=[g_b])
                    P.dma("pool", self.Gs[tt * 128:(tt + 1) * 128, :], g_b[:], r=[g_b], w=[self.dGs], sres=g_b)
        self.end_phase()

    def phase_B(self):
        P = self.P
        NTT, NPT, T = self.NTT, self.NPT, self.T
        self.begin_phase()
        self.pTq = P.ps("pTq", [128, 8, 128], BF16)
        self.pM = P.ps("pM", [128, 512], F32)
        NKT = NPT
        Kb = [P.sb(f"b_K{i}", [128, max(T, PAST + 128)], BF16) for i in range(2)]
        Vb = [P.sb(f"b_V{i}", [128, max(NKT, 17), 130], BF16) for i in range(2)]
        for v in Vb:
            P.pool(lambda e, v=v: e.memset(v[:, :, 128:129], 1.0), w=[v])
            P.pool(lambda e, v=v: e.memset(v[:, :, 129:130], 0.0), w=[v])
        Kc = P.sb("b_Kc", [128, 16, 128], BF16)
        Qt = [P.sb(f"b_Q{i}", [128, 512], BF16) for i in range(2)]
        Pt = [P.sb(f"b_P{i}", [128, 512], BF16) for i in range(3)]
        Gt = [P.sb(f"b_G{i}", [128, 4, 128], BF16) for i in range(2)]
        pS = [P.ps(f"b_pS{i}", [128, 512], F32) for i in range(2)]
        pO = [[P.ps(f"b_pO{c}{i}", [128, 2, 256], F32) for i in range(2)] for c in range(2)]
        pTk = self.pTq
        rl = P.sb("b_rl", [128, 2], F32)
        o1 = P.sb("b_o1", [128, 128], F32)
        o2 = P.sb("b_o2", [128, 128], F32)
        jk = P.sb("b_jk", [128, 128], BF16)
        oss = P.sb("b_oss", [128, 1], F32)
        ors = P.sb("b_ors", [128, 1], F32)
        ob = [P.sb(f"b_ob{i}", [128, 4, 128], BF16) for i in range(2)]
        kss = P.sb("b_kss", [128, 8], F32)
        kmx = P.sb("b_kmx", [128, 1], F32)
        kmr = P.sb("b_kmr", [1, 128], F32)
        km1 = P.sb("b_km1", [1, 1], BF16)
        kmxb = P.sb("b_kmxb", [128, 1], BF16)
        onesr = P.sb("b_ones", [1, 128], BF16)
        P.pool(lambda e: e.memset(onesr[:], 1.0), w=[onesr])
        negC_s = P.sb("b_negCs", [128, 1], F32)
        Kcf = P.sb("b_Kcf", [128, 512], F32)
        pM = self.pM
        hc = 0
        qc = 0
        pc = 0
        sc = 0
        oc = 0
        seqs = [("p", 0)] + [("s", s) for s in range(NST)]
        for kind, s in seqs:
            if kind == "s":
                for kt in range(16):
                    P.dma("sp", Kcf[:], self.cache_k[s, kt * 128:(kt + 1) * 128, :], w=[Kcf], sres=Kcf)
                    P.dve(lambda e: e.tensor_tensor(out=Kcf[:], in0=Kcf[:], in1=Kcf[:], op=ALU.mult), r=[Kcf], w=[Kcf])
                    P.dve(lambda e: e.reduce_sum(out=kss[:], in_=Kcf[:, :].rearrange("p (g d) -> p g d", g=8), axis=AX.X),
                          r=[Kcf], w=[kss])
                    if kt == 0:
                        P.dve(lambda e: e.tensor_reduce(out=kmx[:], in_=kss[:], axis=AX.X, op=ALU.max), r=[kss], w=[kmx])
                    else:
                        P.dve(lambda e: e.tensor_reduce(out=kss[:, 0:1], in_=kss[:], axis=AX.X, op=ALU.max),
                              r=[kss], w=[kss])
                        P.dve(lambda e: e.tensor_tensor(out=kmx[:], in0=kmx[:], in1=kss[:, 0:1], op=ALU.max),
                              r=[kmx, kss], w=[kmx])
                P.dve(lambda e: e.tensor_tensor(out=kmx[:], in0=kmx[:], in1=self.mk8[:], op=ALU.max),
                      r=[kmx, self.mk8], w=[kmx])
                P.dve(lambda e: e.tensor_scalar(out=kmxb[:], in0=kmx[:], scalar1=1.01, scalar2=None, op0=ALU.mult),
                      r=[kmx], w=[kmxb])
                P.pe(lambda e: e.transpose(out=pM[0:1, 0:64].bitcast(BF16), in_=kmxb[:, 0:1], identity=self.ident[:]),
                     r=[kmxb, self.ident], w=[pM])
                P.dve(lambda e: e.tensor_copy(out=kmr[:], in_=pM[0:1, 0:64].bitcast(BF16)), r=[pM], w=[kmr])
                P.dve(lambda e: e.tensor_reduce(out=km1[:], in_=kmr[:], axis=AX.X, op=ALU.max), r=[kmr], w=[km1])
                P.pe(lambda e: e.matmul(pM[:, 0:1], lhsT=onesr[:], rhs=km1[:], start=True, stop=True),
                     r=[onesr, km1], w=[pM])
                P.act(lambda e: e.activation(out=negC_s[:], in_=pM[:, 0:1], func=AF.Sqrt), r=[pM], w=[negC_s])
                P.dve(lambda e: e.scalar_tensor_tensor(out=negC_s[:], in0=negC_s[:], scalar=-1.0, in1=self.mq[:],
                                                       op0=ALU.mult, op1=ALU.mult), r=[negC_s, self.mq], w=[negC_s])
                negC = negC_s
            else:
                negC = self.negC_p
            for h in range(4):
                K = Kb[hc % 2]
                V = Vb[hc % 2]
                hc += 1
                if kind == "p":
                    nq_tiles = NPT // 4 if NPT >= 4 else 1
                    QW = min(512, T)
                    P.dma("sp", K[:, 0:T], self.KT[h, :, 0:T], r=[self.dKT], w=[K], sres=K)
                    P.dma("sp", V[:, 0:NKT, 0:128], self.Vs[h, :, 0:NKT, :], r=[self.dVs], w=[V], sres=V)
                    tok0 = 0
                else:
                    nq_tiles = 1
                    QW = DEC
                    tt = NPT + s
                    tok0 = tt * 128
                    P.dma("pool", Kc[:], self.cache_k[s, :, h * 128:(h + 1) * 128].rearrange("(kt p) d -> p kt d", p=128),
                          w=[Kc], sres=Kc)
                    for g4 in range(4):
                        for j in range(4):
                            kt = g4 * 4 + j
                            P.pe(lambda e, kt=kt, j=j: e.transpose(out=pTk[:, j, :], in_=Kc[:, kt, :], identity=self.ident[:]),
                                 r=[Kc, self.ident], w=[pTk])
                        P.dve(lambda e, K=K, g4=g4: e.tensor_copy(
                            out=K[:, g4 * 512:(g4 + 1) * 512], in_=pTk[:, 0:4, :].rearrange("p a b -> p (a b)")),
                            r=[pTk], w=[K])
                    P.dma("sp", K[:, PAST:PAST + DEC], self.KT[h, :, tok0:tok0 + DEC], r=[self.dKT], w=[K], sres=K)
                    P.dma("pool", V[:, 0:16, 0:128],
                          self.cache_v[s, :, h * 128:(h + 1) * 128].rearrange("(kt p) d -> p kt d", p=128), w=[V], sres=V)
                    P.dma("sp", V[0:DEC, 16, 0:128], self.Vs[h, 0:DEC, tt, :], r=[self.dVs], w=[V], sres=V)
                for qi in range(nq_tiles):
                    Q = Qt[qc % 2]
                    G = Gt[qc % 2]
                    qc += 1
                    q0 = qi * QW
                    nsub = (QW + 127) // 128
                    P.dma("sp", Q[:, 0:QW], self.QT[h, :, tok0 + q0:tok0 + q0 + QW], r=[self.dQT], w=[Q], sres=Q)
                    if kind == "p":
                        P.dma("sp", G[:, 0:nsub, :],
                              self.Gs[q0:q0 + QW, h * 128:(h + 1) * 128].rearrange("(a p) d -> p a d", p=128),
                              r=[self.dGs], w=[G], sres=G)
                    else:
                        P.dma("sp", G[0:DEC, 0, :], self.Gs[tok0:tok0 + DEC, h * 128:(h + 1) * 128],
                              r=[self.dGs], w=[G], sres=G)
                    if kind == "p":
                        blocks = [(j * 128, 128, j, 0, None) for j in range(q0 // 128)]
                        for j in range(nsub):
                            blocks.append((q0 + j * 128, 128, q0 // 128 + j, j, j))
                    else:
                        blocks = [(j * 128, 128, j, 0, None) for j in range(16)] + [(PAST, DEC, 16, 0, None)]
                    steps = [(c, bi) + blk for c in range(2) for bi, blk in enumerate(blocks)]

                    def emit_qk(step):
                        nonlocal sc, pc
                        c, bi, kc0, nk, vt, s0, dg = step
                        ps = pS[sc % 2]
                        sc += 1
                        pt = Pt[pc % 3]
                        pc += 1
                        qa = s0 * 128
                        P.pe(lambda e, ps=ps, K=K, Q=Q, c=c, kc0=kc0, nk=nk, qa=qa, QW=QW: e.matmul(
                            ps[0:nk, qa:QW], lhsT=K[64 * c:64 * c + 64, kc0:kc0 + nk], rhs=Q[64 * c:64 * c + 64, qa:QW],
                            start=True, stop=True), r=[K, Q], w=[ps])
                        P.act(lambda e, ps=ps, pt=pt, nk=nk, qa=qa, QW=QW, negC=negC: e.activation(
                            out=pt[0:nk, qa:QW], in_=ps[0:nk, qa:QW], func=AF.Exp, scale=0.125, bias=negC[0:nk, 0:1]),
                            r=[ps, negC], w=[pt])
                        if dg is not None:
                            P.pool(lambda e, pt=pt, dg=dg: e.memset(pt[64:128, dg * 128:dg * 128 + 64], 0.0), w=[pt])
                        return pt

                    def emit_pv(step, pt):
                        c, bi, kc0, nk, vt, s0, dg = step
                        for sub in range(s0, nsub):
                            nqs = min(128, QW - sub * 128)
                            po = pO[c][sub // 2]
                            first = (bi == 0) and (sub % 2 == 0)
                            last = (dg == sub) if kind == "p" else (bi == len(blocks) - 1)
                            P.pe(lambda e, po=po, pt=pt, V=V, sub=sub, nqs=nqs, nk=nk, vt=vt, first=first, last=last:
                                 e.matmul(po[0:nqs, sub % 2, 0:130], lhsT=pt[0:nk, sub * 128:sub * 128 + nqs],
                                          rhs=V[0:nk, vt, :], start=first, stop=last, skip_group_check=True), r=[pt, V], w=[po])
                    pend = None
                    for step in steps:
                        pt_new = emit_qk(step)
                        if pend is not None:
                            emit_pv(*pend)
                        pend = (step, pt_new)
                    emit_pv(*pend)
                    obt = ob[oc % 2]
                    oc += 1
                    for sub in range(nsub):
                        nqs = min(128, QW - sub * 128)
                        p0 = pO[0][sub // 2]
                        p1 = pO[1][sub // 2]
                        si = sub % 2
                        P.dve(lambda e, p0=p0, si=si, nqs=nqs: e.reciprocal(out=rl[0:nqs, 0:1], in_=p0[0:nqs, si, 128:129]),
                              r=[p0], w=[rl])
                        P.dve(lambda e, p1=p1, si=si, nqs=nqs: e.reciprocal(out=rl[0:nqs, 1:2], in_=p1[0:nqs, si, 128:129]),
                              r=[p1], w=[rl])
                        P.dve(lambda e, nqs=nqs: e.tensor_tensor(out=rl[0:nqs, 1:2], in0=rl[0:nqs, 1:2],
                                                                 in1=self.neglam[0:nqs, :], op=ALU.mult),
                              r=[rl, self.neglam], w=[rl])
                        P.dve(lambda e, p0=p0, si=si, nqs=nqs: e.tensor_scalar(
                            out=o1[0:nqs, :], in0=p0[0:nqs, si, 0:128], scalar1=rl[0:nqs, 0:1], scalar2=None, op0=ALU.mult),
                            r=[p0, rl], w=[o1])
                        P.dve(lambda e, p1=p1, si=si, nqs=nqs: e.scalar_tensor_tensor(
                            out=o2[0:nqs, :], in0=p1[0:nqs, si, 0:128], scalar=rl[0:nqs, 1:2], in1=o1[0:nqs, :],
                            op0=ALU.mult, op1=ALU.add), r=[p1, rl, o1], w=[o2])
                        P.act(lambda e, nqs=nqs: e.activation(out=jk[0:nqs, :], in_=o2[0:nqs, :], func=AF.Square,
                                                              accum_out=oss[0:nqs, :]), r=[o2], w=[jk, oss])
                        P.act(lambda e, nqs=nqs: e.activation(out=ors[0:nqs, :], in_=oss[0:nqs, :], func=AF.Sqrt,
                                                              scale=1.0 / 128, bias=self.epst[0:nqs, 0:1]),
                              r=[oss, self.epst], w=[ors])
                        P.dve(lambda e, nqs=nqs: e.reciprocal(out=ors[0:nqs, :], in_=ors[0:nqs, :]), r=[ors], w=[ors])
                        P.dve(lambda e, nqs=nqs: e.scalar_tensor_tensor(
                            out=o1[0:nqs, :], in0=o2[0:nqs, :], scalar=ors[0:nqs, 0:1], in1=self.sub_b[0:nqs, :],
                            op0=ALU.mult, op1=ALU.mult), r=[o2, ors, self.sub_b], w=[o1])
                        P.dve(lambda e, nqs=nqs, sub=sub, G=G, obt=obt: e.tensor_tensor(
                            out=obt[0:nqs, sub, :], in0=o1[0:nqs, :], in1=G[0:nqs, sub, :], op=ALU.mult),
                            r=[o1, G], w=[obt])
                    if kind == "p":
                        P.dma("pool", self.Mx[q0:q0 + QW, h * 128:(h + 1) * 128].rearrange("(a p) d -> p a d", p=128),
                              obt[:, 0:nsub, :], r=[obt], w=[self.dMx], sres=obt)
                    else:
                        P.dma("pool", self.Mx[tok0:tok0 + DEC, h * 128:(h + 1) * 128], obt[0:DEC, 0, :],
                              r=[obt], w=[self.dMx], sres=obt)
        self.end_phase()

    def phase_A2(self):
        P = self.P
        NTT, NPT = self.NTT, self.NPT
        self.begin_phase()
        self.alloc_xnorm(nx=1)
        C0 = math.exp(-0.5)
        W = P.sb("W2", [128, 16, 2176], BF16)
        for k in range(16):
            P.dma("pool", W[:, k, :], self.w_rw[k * 128:(k + 1) * 128, :], w=[W], sres=W)
        banks = [P.ps(f"bk{i}", [128, 512], F32) for i in range(6)]
        self.bkc = 0

        def nb():
            b = banks[self.bkc % len(banks)]
            self.bkc += 1
            return b
        bl = self.bcast_load
        mu_b = bl("mu_b", self.mu, 1664)
        w0_b = bl("w0_b", self.w0, 512); a0_b = bl("a0_b", self.a0, 512)
        kk_b = bl("kk_b", self.k_k, 512); ka_b = bl("ka_b", self.k_a, 512)
        rk_b = bl("rk_b", self.r_k, 512)
        lg_b = bl("lg_b", self.lnx_g, 512); lb_b = bl("lb_b", self.lnx_b, 512)
        omka = P.sb("omka", [128, 512], F32)
        P.dve(lambda e: e.tensor_scalar(out=omka[:], in0=ka_b[:], scalar1=-1.0, scalar2=1.0, op0=ALU.mult, op1=ALU.add),
              r=[ka_b], w=[omka])
        wup = P.sb("wup", [128, 512], BF16)
        P.dma("pool", wup[0:64, :], self.w_up, w=[wup], sres=wup)
        P.dma("pool", wup[64:128, :], self.a_up, w=[wup], sres=wup)
        onesc = P.sb("onesc", [128, 1], BF16)
        shi = P.sb("shi", [128, 512], BF16)
        slo = P.sb("slo", [128, 512], BF16)
        P.pool(lambda e: e.memset(onesc[:], 1.0), w=[onesc])
        tiny = P.sb("tiny", [128, 1], F32)
        identb3 = self.ident
        Pp = P.sb("Pp", [128, 1664], F32)
        prev = P.sb("prev", [128, 1664], F32)
        lastrow = P.sb("lastrow", [1, 1664], F32)
        gsl = [P.sb(f"gs{i}", [128, 512], BF16) for i in range(2)]
        tl = P.sb("tl", [128, 128], BF16)
        tlT = P.sb("tlT", [128, 128], BF16)
        sigw = P.sb("sigw", [128, 512], F32)
        asig = P.sb("asig", [128, 512], F32)
        tA = P.sb("tA", [128, 512], F32)
        tB = P.sb("tB", [128, 512], F32)
        kkn = P.sb("kkn", [128, 512], F32)
        kmod = P.sb("kmod", [128, 512], F32)
        bvec = P.sb("bvec", [128, 512], F32)
        s8 = P.sb("s8", [128, 8], F32)
        bs8 = P.sb("bs8", [128, 8], F32)
        Ec = P.sb("Ec", [128, 512], F32); Ex = Ec
        En = P.sb("En", [128, 512], F32); Er = En
        rt = P.sb("rt", [128, 512], BF16); at = P.sb("at", [128, 512], BF16)
        kt = P.sb("kt", [128, 512], BF16); bt = P.sb("bt", [128, 512], BF16)
        kh = P.sb("kh", [128, 512], BF16); bh = P.sb("bh", [128, 512], BF16)
        vb = P.sb("vb", [128, 512], BF16)
        AR = P.sb("AR", [128, 4, 2, 128], BF16)
        BKz = P.sb("BKz", [128, 8, 2, 128], BF16)
        Hbz = P.sb("Hbz", [128, 8, 64], BF16)
        XL = P.sb("XL", [128, 8, 384], BF16)
        AK = P.sb("AK", [128, 8, 256], BF16)
        ML = [P.sb(f"ML{i}", [128, 8, 256], BF16) for i in range(2)]
        Q = [P.sb(f"Q{i}", [128, 8, 128], BF16) for i in range(2)]
        XLr = P.sub(XL, 8); AKr = P.sub(AK, 4)
        MLr = [P.sub(ML[0], 4), P.sub(ML[1], 4)]
        Qr = [P.sub(Q[0], 2), P.sub(Q[1], 2)]
        Wb = P.sb("Wb_", [128, 512], BF16)
        P.pool(lambda e: e.memset(BKz[:], 0.0), w=[BKz])
        P.pool(lambda e: e.memset(Hbz[:], 0.0), w=[Hbz])
        Uv = P.sb("Uv", [128, 512], F32)
        AhT = P.sb("AhT", [128, 4, 128], BF16)
        AhTo = P.sb("AhTo", [128, 4, 128], BF16)
        GC = P.sb("GC", [128, 4], F32)
        Hs = P.sb("Hs", [128, 4, 64], F32)
        Ub = P.sb("Ub", [128, 512], BF16)
        yf = tA
        ysq = tB
        m8 = P.sb("m8", [128, 8], F32); v8 = P.sb("v8", [128, 8], F32)
        yo = [Wb]
        S0v = Uv[0:64, :].rearrange("i (h j) -> i h j", h=8)
        Sov = Ec[0:64, :].rearrange("i (h j) -> i h j", h=8)
        mask3 = self.maskf
        mask2b = self.maskf[:, 0:256].unsqueeze(1).broadcast_to([128, 2, 256])
        P.pool(lambda e: e.memset(tiny[:], 1e-12), w=[tiny])

        def g8(t):
            return t[:, :].rearrange("p (g d) -> p g d", g=8)

        def b8(t):
            return t[:, :].unsqueeze(2).broadcast_to([128, 8, 64])

        def refresh_hbz(extra_r=()):
            P.act(lambda e: e.copy(out=Hbz[0:64, 0:8:2, :], in_=Hs[0:64, :, :]), r=[Hs] + list(extra_r), w=[Hbz])
            P.dve(lambda e: e.tensor_copy(out=Hbz[64:128, 1:8:2, :], in_=Hs[64:128, :, :]), r=[Hs] + list(extra_r), w=[Hbz])

        def store_state(dst, soi):
            hi_ = shi[:, 0:256].rearrange("p (a b) -> p a b", a=4)
            lo_ = slo[:, 0:256].rearrange("p (a b) -> p a b", a=4)
            P.act(lambda e: e.copy(out=hi_, in_=Hs[:]), r=[Hs], w=[shi])
            P.dve(lambda e: e.tensor_tensor(out=lo_, in0=Hs[:], in1=hi_, op=ALU.subtract), r=[Hs, shi], w=[slo])
            bk = nb()
            bkb = bk[:, :].bitcast(BF16)
            for j, (src, sv) in enumerate(((shi, hi_), (slo, lo_))):
                for p in range(4):
                    P.pe(lambda e, bkb=bkb, p=p, j=j, sv=sv: e.transpose(
                        out=bkb[0:64, j * 512 + p * 128:j * 512 + (p + 1) * 128], in_=sv[:, p, :], identity=self.ident[:]),
                        r=[src, self.ident], w=[bk])
            P.act(lambda e, bkb=bkb: e.copy(out=Ec[0:64, :], in_=bkb[0:64, 0:512]), r=[bk], w=[Ec])
            P.dve(lambda e, bkb=bkb: e.tensor_tensor(out=Ec[0:64, :], in0=Ec[0:64, :], in1=bkb[0:64, 512:1024], op=ALU.add),
                  r=[bk, Ec], w=[Ec])
            P.dma("pool", dst.rearrange("h i j -> i h j"), Sov, r=[Ec], w=[self.dOut], sres=Ec)

        def front_proj(tt_):
            gs_ = gsl[tt_ % 2]
            XT = self.xnorm_tile(tt_)
            for cb, (c0, cw) in enumerate(((0, 512), (512, 512), (1024, 512), (1536, 128), (1664, 512))):
                py = nb()
                for k in range(16):
                    P.pe(lambda e, py=py, k=k, c0=c0, cw=cw, XT=XT: e.matmul(
                        py[:, 0:cw], lhsT=XT[:, k, :], rhs=W[:, k, c0:c0 + cw], start=(k == 0), stop=(k == 15)),
                        r=[XT, W], w=[py])
                if cb < 4:
                    P.act(lambda e, py=py, c0=c0, cw=cw: e.copy(out=Pp[:, c0:c0 + cw], in_=py[:, 0:cw]), r=[py], w=[Pp])
                else:
                    P.act(lambda e, py=py, gs_=gs_: e.activation(out=gs_[:], in_=py[:], func=AF.Silu), r=[py], w=[gs_])

        yc = 0
        soi = 0
        A2STOP = CFG.get('a2stop', 99)
        tlist = list(CFG.get('tiles', range(NTT)))
        for ti, tt in enumerate(tlist):
            sample = tt >= NPT
            s = tt - NPT
            if tt == 0:
                P.pool(lambda e: e.memset(Hs[:], 0.0), w=[Hs])
                P.pool(lambda e: e.memset(lastrow[:], 0.0), w=[lastrow])
            if sample:
                P.dma("sp", S0v, self.wkv0[s].rearrange("h i j -> i h j"), w=[Uv], sres=Uv)
                P.act(lambda e: e.copy(out=shi[0:64, :], in_=Uv[0:64, :]), r=[Uv], w=[shi])
                P.dve(lambda e: e.tensor_tensor(out=slo[0:64, :], in0=Uv[0:64, :], in1=shi[0:64, :], op=ALU.subtract),
                      r=[Uv, shi], w=[slo])
                bk = nb()
                bkb = bk[:, :].bitcast(BF16)
                for j, src in enumerate((shi, slo)):
                    for p in range(4):
                        P.pe(lambda e, bkb=bkb, p=p, j=j, src=src: e.transpose(
                            out=bkb[:, j * 256 + p * 64:j * 256 + (p + 1) * 64], in_=src[0:64, p * 128:(p + 1) * 128],
                            identity=self.ident[0:64, 0:64]), r=[src, self.ident], w=[bk])
                P.act(lambda e, bkb=bkb: e.copy(out=Hs[:, :, :].rearrange("p a b -> p (a b)"), in_=bkb[:, 0:256]),
                      r=[bk], w=[Hs])
                P.dve(lambda e, bkb=bkb: e.tensor_tensor(out=Hs[:, :, :].rearrange("p a b -> p (a b)"),
                                                         in0=Hs[:, :, :].rearrange("p a b -> p (a b)"), in1=bkb[:, 256:512],
                                                         op=ALU.add), r=[bk, Hs], w=[Hs])
                refresh_hbz()
                P.dma("sp", lastrow[:], self.shift0[s:s + 1, :], w=[lastrow], sres=lastrow)
            if ti == 0:
                front_proj(tt)
            gs = gsl[tt % 2]
            if A2STOP <= 1:
                continue
            P.dma("sp", prev[1:128, :], Pp[0:127, :], r=[Pp], w=[prev], sres=prev)
            P.dma("sp", prev[0:1, :], lastrow[:], r=[lastrow], w=[prev], sres=prev)
            if not sample:
                if tt == NPT - 1:
                    P.dma("pool", self.shift_p.rearrange("(a c) -> a c", a=1), Pp[127:128, :], r=[Pp], w=[self.dOut], sres=Pp)
                else:
                    P.dma("sp", lastrow[:], Pp[127:128, :], r=[Pp], w=[lastrow], sres=lastrow)
            else:
                P.dma("pool", self.shift_s[s:s + 1, :], Pp[DEC - 1:DEC, :], r=[Pp], w=[self.dOut], sres=Pp)
            P.pool(lambda e: e.tensor_tensor(out=prev[:], in0=prev[:], in1=Pp[:], op=ALU.subtract), r=[prev, Pp], w=[prev])
            P.pool(lambda e: e.tensor_tensor(out=prev[:], in0=prev[:], in1=mu_b[:], op=ALU.mult), r=[prev, mu_b], w=[prev])
            P.pool(lambda e: e.tensor_tensor(out=prev[:], in0=prev[:], in1=Pp[:], op=ALU.add), r=[prev, Pp], w=[prev])
            xr = prev[:, 0:512]; xk = prev[:, 512:1024]; xv = prev[:, 1024:1536]
            if A2STOP <= 2:
                continue
            P.act(lambda e: e.activation(out=tl[:, 0:64], in_=prev[:, 1536:1600], func=AF.Tanh), r=[prev], w=[tl])
            P.act(lambda e: e.copy(out=tl[:, 64:128], in_=prev[:, 1600:1664]), r=[prev], w=[tl])
            bk = nb()
            P.pe(lambda e, bk=bk: e.transpose(out=bk[:, 0:64].bitcast(BF16), in_=tl[:], identity=self.ident[:]),
                 r=[tl, self.ident], w=[bk])
            P.act(lambda e, bk=bk: e.copy(out=tlT[:], in_=bk[:, 0:64].bitcast(BF16)), r=[bk], w=[tlT])
            bz = nb()
            P.pe(lambda e, bz=bz: e.matmul(bz[:], lhsT=tlT[0:64, :], rhs=wup[0:64, :], start=True, stop=True),
                 r=[tlT, wup], w=[bz])
            P.dve(lambda e, bz=bz: e.tensor_tensor(out=tA[:], in0=bz[:], in1=w0_b[:], op=ALU.add), r=[bz, w0_b], w=[tA])
            P.act(lambda e: e.activation(out=sigw[:], in_=tA[:], func=AF.Sigmoid), r=[tA], w=[sigw])
            bz2 = nb()
            P.pe(lambda e, bz2=bz2: e.matmul(bz2[:], lhsT=tlT[64:128, :], rhs=wup[64:128, :], start=True, stop=True),
                 r=[tlT, wup], w=[bz2])
            P.dve(lambda e, bz2=bz2: e.tensor_tensor(out=tB[:], in0=bz2[:], in1=a0_b[:], op=ALU.add), r=[bz2, a0_b], w=[tB])
            P.act(lambda e: e.activation(out=asig[:], in_=tB[:], func=AF.Sigmoid), r=[tB], w=[asig])
            if sample:
                P.pool(lambda e: e.memset(sigw[32:64, :], 0.0), w=[sigw])
                P.pool(lambda e: e.memset(sigw[64:128, :], 0.0), w=[sigw])
            if A2STOP <= 3:
                continue
            P.act(lambda e: e.copy(out=shi[:], in_=sigw[:]), r=[sigw], w=[shi])
            P.dve(lambda e: e.tensor_tensor(out=slo[:], in0=sigw[:], in1=shi[:], op=ALU.subtract), r=[sigw, shi], w=[slo])
            bc = nb(); bx = nb(); br = nb()
            for bnk, mc in ((bc, 128), (bx, 0), (br, 256)):
                P.pe(lambda e, bnk=bnk, mc=mc: e.matmul(bnk[:], lhsT=self.maskb[:, mc:mc + 128], rhs=shi[:], start=True, stop=False),
                     r=[self.maskb, shi], w=[bnk])
                P.pe(lambda e, bnk=bnk, mc=mc: e.matmul(bnk[:], lhsT=self.maskb[:, mc:mc + 128], rhs=slo[:], start=False, stop=True),
                     r=[self.maskb, slo], w=[bnk])
            P.act(lambda e, bc=bc: e.activation(out=Ec[:], in_=bc[:], func=AF.Exp, scale=-C0), r=[bc], w=[Ec])
            P.act(lambda e, bc=bc: e.activation(out=En[:], in_=bc[:], func=AF.Exp, scale=C0), r=[bc], w=[En])
            bg = nb()
            for p in range(4):
                P.pe(lambda e, bg=bg, p=p: e.matmul(bg[:, p:p + 1], lhsT=shi[:, p * 128:(p + 1) * 128], rhs=onesc[:],
                                                    start=(p == 0), stop=False, skip_group_check=True), r=[shi, onesc], w=[bg])
                P.pe(lambda e, bg=bg, p=p: e.matmul(bg[:, p:p + 1], lhsT=slo[:, p * 128:(p + 1) * 128], rhs=onesc[:],
                                                    start=False, stop=True, skip_group_check=True), r=[slo, onesc], w=[bg])
            P.act(lambda e, bg=bg: e.activation(out=GC[:], in_=bg[:, 0:4], func=AF.Exp, scale=-C0), r=[bg], w=[GC])
            if A2STOP <= 4:
                continue
            P.dve(lambda e: e.tensor_tensor(out=tA[:], in0=xk, in1=kk_b[:], op=ALU.mult), r=[prev, kk_b], w=[tA])
            P.act(lambda e: e.activation(out=tB[:], in_=tA[:], func=AF.Square), r=[tA], w=[tB])
            P.dve(lambda e: e.reduce_sum(out=s8[:], in_=g8(tB), axis=AX.X), r=[tB], w=[s8])
            P.act(lambda e: e.activation(out=s8[:], in_=s8[:], func=AF.Sqrt), r=[s8], w=[s8])
            P.dve(lambda e: e.tensor_scalar(out=s8[:], in0=s8[:], scalar1=tiny[:, 0:1], scalar2=None, op0=ALU.max),
                  r=[s8, tiny], w=[s8])
            P.dve(lambda e: e.reciprocal(out=s8[:], in_=s8[:]), r=[s8], w=[s8])
            P.dve(lambda e: e.tensor_tensor(out=g8(kkn), in0=g8(tA), in1=b8(s8), op=ALU.mult), r=[tA, s8], w=[kkn])
            P.pool(lambda e: e.tensor_tensor(out=tB[:], in0=asig[:], in1=ka_b[:], op=ALU.mult), r=[asig, ka_b], w=[tB])
            P.pool(lambda e: e.tensor_tensor(out=tB[:], in0=tB[:], in1=omka[:], op=ALU.add), r=[tB, omka], w=[tB])
            P.pool(lambda e: e.tensor_tensor(out=kmod[:], in0=xk, in1=tB[:], op=ALU.mult), r=[prev, tB], w=[kmod])
            P.dve(lambda e: e.tensor_tensor(out=bvec[:], in0=kkn[:], in1=asig[:], op=ALU.mult), r=[kkn, asig], w=[bvec])
            P.pool(lambda e: e.tensor_tensor(out=tA[:], in0=xr, in1=kmod[:], op=ALU.mult), r=[prev, kmod, kkn], w=[tA])
            P.pool(lambda e: e.tensor_tensor(out=tA[:], in0=tA[:], in1=rk_b[:], op=ALU.mult), r=[tA, rk_b], w=[tA])
            P.dve(lambda e: e.reduce_sum(out=bs8[:], in_=g8(tA), axis=AX.X), r=[tA], w=[bs8])
            P.dve(lambda e: e.tensor_tensor(out=rt[:], in0=xr, in1=Ec[:], op=ALU.mult), r=[prev, Ec], w=[rt])
            P.act(lambda e, bx=bx: e.activation(out=Ex[:], in_=bx[:], func=AF.Exp, scale=-C0), r=[bx], w=[Ex])
            P.dve(lambda e: e.scalar_tensor_tensor(out=at[:], in0=kkn[:], scalar=-1.0, in1=Ex[:], op0=ALU.mult, op1=ALU.mult),
                  r=[kkn, Ex], w=[at])
            P.dve(lambda e: e.tensor_tensor(out=kt[:], in0=kmod[:], in1=En[:], op=ALU.mult), r=[kmod, En], w=[kt])
            P.pool(lambda e: e.tensor_tensor(out=bt[:], in0=bvec[:], in1=En[:], op=ALU.mult), r=[bvec, En], w=[bt])
            P.act(lambda e, br=br: e.activation(out=Er[:], in_=br[:], func=AF.Exp, scale=-C0), r=[br], w=[Er])
            P.pool(lambda e: e.tensor_tensor(out=kh[:], in0=kmod[:], in1=Er[:], op=ALU.mult), r=[kmod, Er], w=[kh])
            P.pool(lambda e: e.tensor_tensor(out=bh[:], in0=bvec[:], in1=Er[:], op=ALU.mult), r=[bvec, Er], w=[bh])
            P.act(lambda e: e.copy(out=vb[:], in_=xv), r=[prev], w=[vb])
            if sample:
                for t_ in (kt, bt, kh, bh):
                    P.pool(lambda e, t_=t_: e.memset(t_[32:64, :], 0.0), w=[t_])
                    P.pool(lambda e, t_=t_: e.memset(t_[64:128, :], 0.0), w=[t_])
            if A2STOP <= 5:
                continue
            for dst, (o0, o1) in ((AR, (at, rt)), (BKz, (bt, kt))):
                bk = nb()
                bkb = bk[:, :].bitcast(BF16).rearrange("p (a b c) -> p a b c", a=4, b=2)
                for p in range(4):
                    for j, o in enumerate((o0, o1)):
                        P.pe(lambda e, bkb=bkb, p=p, j=j, o=o: e.transpose(
                            out=bkb[:, p, j, :], in_=o[:, p * 128:(p + 1) * 128], identity=self.ident[:]),
                            r=[o, self.ident], w=[bk])
                if dst is AR:
                    P.act(lambda e, dst=dst, bkb=bkb: e.copy(out=dst[:], in_=bkb), r=[bk], w=[dst])
                else:
                    P.act(lambda e, bkb=bkb: e.copy(out=BKz[0:64, 0:8:2, :, :], in_=bkb[0:64, :, :, :]), r=[bk], w=[BKz])
                    P.act(lambda e, bkb=bkb: e.copy(out=BKz[64:128, 1:8:2, :, :], in_=bkb[64:128, :, :, :]), r=[bk], w=[BKz])
            if A2STOP <= 6:
                continue
            for h in range(8):
                p, hp = h // 2, h % 2
                rows = slice(64 * hp, 64 * hp + 64)
                ba = nb()
                P.pe(lambda e, ba=ba, p=p, h=h: e.matmul(
                    ba[:, 0:256], lhsT=BKz[:, h, 0, :], rhs=AR[:, p, :, :].rearrange("k a t -> k (a t)"),
                    start=True, stop=True), r=[BKz, AR], w=[ba])
                P.pe(lambda e, ba=ba, p=p, h=h: e.matmul(
                    ba[:, 256:384], lhsT=AR[:, p, 0, :], rhs=BKz[:, h, 0, :], start=True, stop=True),
                    r=[BKz, AR], w=[ba])
                P.dve(lambda e, ba=ba, h=h: e.tensor_tensor(out=XL[:, h, :], in0=ba[:, 0:384], in1=mask3[:, :], op=ALU.mult),
                      r=[ba, mask3], w=[XLr[h]])
                if hp == 0:
                    bb = nb()
                P.pe(lambda e, bb=bb, p=p, h=h, hp=hp: e.matmul(
                    bb[:, hp * 256:(hp + 1) * 256], lhsT=BKz[:, h, 1, :],
                    rhs=AR[:, p, :, :].rearrange("k a t -> k (a t)"), start=True, stop=True), r=[BKz, AR], w=[bb])
                if hp == 1:
                    P.dve(lambda e, bb=bb, h=h: e.tensor_tensor(
                        out=AK[:, h - 1:h + 1, :], in0=bb[:, :].rearrange("p (a c) -> p a c", a=2), in1=mask2b, op=ALU.mult),
                        r=[bb, mask3], w=[AKr[p]])
            if A2STOP <= 7:
                continue
            P.dve(lambda e: e.tensor_tensor(out=Q[0][:], in0=XL[:, :, 0:128],
                                            in1=self.ident[:, :].unsqueeze(1).broadcast_to([128, 8, 128]), op=ALU.add),
                  r=XLr + [self.ident], w=Qr[0])
            qi = 0
            INV = CFG.get("invstop", 99)
            for lev in range(1, 7):
                if INV <= 0 or lev > CFG.get("invlev", 6):
                    break
                mlo = ML[lev % 2]
                mli = ML[(lev - 1) % 2]

                def Mprev(h):
                    return XL[:, h, 0:128] if lev == 1 else mli[:, h, 0:128]

                def Lprev(h):
                    return XL[:, h, 256:384] if lev == 1 else mli[:, h, 128:256]
                srcr = (lambda h: XLr[h]) if lev == 1 else (lambda h, lv=lev: MLr[(lv - 1) % 2][h // 2])
                mlor = MLr[lev % 2]
                if lev < 6:
                    for h2 in range(4):
                        bm = nb()
                        for j in range(2):
                            h = 2 * h2 + j
                            P.pe(lambda e, bm=bm, j=j, h=h, Mp=Mprev(h), Lp=Lprev(h): e.matmul(
                                bm[:, j * 256:j * 256 + 128], lhsT=Lp, rhs=Mp, start=True, stop=True), r=[srcr(h)], w=[bm])
                            P.pe(lambda e, bm=bm, j=j, h=h, Mp=Mprev(h), Lp=Lprev(h): e.matmul(
                                bm[:, j * 256 + 128:j * 256 + 256], lhsT=Mp, rhs=Lp, start=True, stop=True), r=[srcr(h)], w=[bm])
                        P.act(lambda e, bm=bm, h2=h2, mlo=mlo: e.copy(
                            out=mlo[:, 2 * h2:2 * h2 + 2, :].rearrange("p a c -> p (a c)"), in_=bm[:]), r=[bm], w=[mlor[h2]])
                    Lcur = lambda h: mlo[:, h, 128:256]
                else:
                    for h4 in range(2):
                        bm = nb()
                        for j in range(4):
                            h = 4 * h4 + j
                            P.pe(lambda e, bm=bm, j=j, Mp=Mprev(h), Lp=Lprev(h): e.matmul(
                                bm[:, j * 128:(j + 1) * 128], lhsT=Mp, rhs=Lp, start=True, stop=True), r=[srcr(h)], w=[bm])
                        P.act(lambda e, bm=bm, h4=h4, mlo=mlo: e.copy(
                            out=mlo[:, 4 * h4:4 * h4 + 4, 0:128], in_=bm[:, :].rearrange("p (a c) -> p a c", a=4)),
                            r=[bm], w=[mlor[2 * h4], mlor[2 * h4 + 1]])
                    Lcur = lambda h: mlo[:, h, 0:128]
                if INV <= 1:
                    continue
                qo, qn = Q[qi % 2], Q[(qi + 1) % 2]
                qor, qnr = Qr[qi % 2], Qr[(qi + 1) % 2]
                qi += 1
                for h4 in range(2):
                    bq = nb()
                    for j in range(4):
                        h = 4 * h4 + j
                        P.pe(lambda e, bq=bq, j=j, h=h, Lc=Lcur(h), qo=qo: e.matmul(
                            bq[:, j * 128:(j + 1) * 128], lhsT=Lc, rhs=qo[:, h, :], start=True, stop=True),
                            r=[mlor[h // 2], qor[h4]], w=[bq])
                    P.dve(lambda e, bq=bq, h4=h4, qo=qo, qn=qn: e.tensor_tensor(
                        out=qn[:, 4 * h4:4 * h4 + 4, :].rearrange("p a c -> p (a c)"), in0=bq[:],
                        in1=qo[:, 4 * h4:4 * h4 + 4, :].rearrange("p a c -> p (a c)"), op=ALU.add), r=[bq, qor[h4]], w=[qnr[h4]])
            PT = Q[qi % 2]
            PTr = Qr[qi % 2]
            if A2STOP <= 8:
                continue
            bw = nb()
            for h in range(8):
                P.pe(lambda e, bw=bw, h=h: e.matmul(bw[:, h * 64:(h + 1) * 64], lhsT=AK[:, h, 0:128],
                                                    rhs=vb[:, h * 64:(h + 1) * 64], start=True, stop=True),
                     r=[AKr[h // 2], vb], w=[bw])
            P.act(lambda e, bw=bw: e.copy(out=Wb[:], in_=bw[:]), r=[bw], w=[Wb])
            if CFG.get("s9", 99) <= 1:
                continue
            bu = nb()
            for h in range(8):
                P.pe(lambda e, bu=bu, h=h, PT=PT: e.matmul(bu[:, h * 64:(h + 1) * 64], lhsT=PT[:, h, :],
                                                           rhs=Wb[:, h * 64:(h + 1) * 64], start=True, stop=True),
                     r=[PTr[h // 4], Wb], w=[bu])
            P.act(lambda e, bu=bu: e.copy(out=Uv[:], in_=bu[:]), r=[bu], w=[Uv])
            if CFG.get("s9", 99) <= 2:
                continue
            bhe = nb(); bho = nb()
            for h in range(8):
                p, hp = h // 2, h % 2
                bb_ = bhe if hp == 0 else bho
                P.pe(lambda e, bb_=bb_, h=h, p=p, PT=PT: e.matmul(
                    bb_[:, p * 128:(p + 1) * 128], lhsT=(kt if CFG.get("va") else at)[:, p * 128:(p + 1) * 128], rhs=PT[:, h, :],
                    start=True, stop=True), r=[at, PTr[h // 4]], w=[bb_])
            P.act(lambda e, bhe=bhe: e.copy(out=AhT[:, :, :].rearrange("p a t -> p (a t)"), in_=bhe[:, :]),
                  r=[bhe], w=[AhT])
            P.act(lambda e, bho=bho: e.copy(out=AhTo[:, :, :].rearrange("p a t -> p (a t)"), in_=bho[:, :]),
                  r=[bho], w=[AhTo])
            if A2STOP <= 9:
                continue
            if ti + 1 < len(tlist):
                front_proj(tlist[ti + 1])
            bU = nb()
            for h in range(8):
                p, hp = h // 2, h % 2
                rows = slice(64 * hp, 64 * hp + 64)
                Ah_ = AhT if hp == 0 else AhTo
                P.pe(lambda e, bU=bU, h=h, p=p, Ah_=Ah_: e.matmul(
                    bU[:, h * 64:(h + 1) * 64], lhsT=Ah_[:, p, :], rhs=Hbz[:, h, :], start=True, stop=True),
                    r=[Ah_, Hbz], w=[bU])
            P.dve(lambda e, bU=bU: e.tensor_tensor(out=Ub[:], in0=bU[:], in1=Uv[:], op=ALU.add), r=[bU, Uv], w=[Ub])
            bY = nb()
            for h in range(8):
                p, hp = h // 2, h % 2
                rows = slice(64 * hp, 64 * hp + 64)
                cs = slice(h * 64, (h + 1) * 64)
                P.pe(lambda e, bY=bY, h=h, p=p, rows=rows, cs=cs: e.matmul(
                    bY[:, cs], lhsT=AR[:, p, 1, :], rhs=Hbz[:, h, :], start=(h == 0), stop=False, skip_group_check=True),
                    r=[AR, Hbz], w=[bY])
                P.pe(lambda e, bY=bY, h=h, cs=cs: e.matmul(
                    bY[:, cs], lhsT=XL[:, h, 128:256], rhs=Ub[:, cs], start=False, stop=False, skip_group_check=True),
                    r=[XLr[h], Ub], w=[bY])
                P.pe(lambda e, bY=bY, h=h, cs=cs: e.matmul(
                    bY[:, cs], lhsT=AK[:, h, 128:256], rhs=vb[:, cs], start=False, stop=True, skip_group_check=True),
                    r=[AKr[h // 2], vb], w=[bY])
            if A2STOP <= 10:
                continue
            bHe = nb(); bHo = nb()
            for h in range(8):
                p, hp = h // 2, h % 2
                cs = slice(h * 64, (h + 1) * 64)
                pc = slice(p * 128, (p + 1) * 128)
                bb_ = bHe if hp == 0 else bHo
                o_ = bb_[:, p * 64:(p + 1) * 64]
                P.pe(lambda e, o_=o_, cs=cs, pc=pc, h=h: e.matmul(o_, lhsT=kh[:, pc], rhs=vb[:, cs], start=(h < 2), stop=False,
                                                                  skip_group_check=True), r=[kh, vb], w=[bb_])
                P.pe(lambda e, o_=o_, cs=cs, pc=pc: e.matmul(o_, lhsT=bh[:, pc], rhs=Ub[:, cs], start=False, stop=True,
                                                             skip_group_check=True), r=[bh, Ub], w=[bb_])
            P.dve(lambda e: e.tensor_tensor(out=Hs[:], in0=Hs[:], in1=GC[:, :].unsqueeze(2).broadcast_to([128, 4, 64]),
                                            op=ALU.mult), r=[Hs, GC], w=[Hs])
            P.dve(lambda e, bHe=bHe: e.tensor_tensor(out=Hs[0:64, :, :], in0=Hs[0:64, :, :],
                                                     in1=bHe[0:64, 0:256].rearrange("p (a b) -> p a b", a=4), op=ALU.add),
                  r=[Hs, bHe, bY], w=[Hs])
            P.dve(lambda e, bHo=bHo: e.tensor_tensor(out=Hs[64:128, :, :], in0=Hs[64:128, :, :],
                                                     in1=bHo[64:128, 0:256].rearrange("p (a b) -> p a b", a=4), op=ALU.add),
                  r=[Hs, bHo], w=[Hs])
            refresh_hbz()
            if A2STOP <= 11:
                continue
            P.act(lambda e, bY=bY: e.copy(out=yf[:], in_=bY[:]), r=[bY], w=[yf])
            P.dve(lambda e: e.reduce_sum(out=m8[:], in_=g8(yf), axis=AX.X), r=[yf], w=[m8])
            P.act(lambda e: e.activation(out=ysq[:], in_=yf[:], func=AF.Square), r=[yf], w=[ysq])
            P.dve(lambda e: e.reduce_sum(out=v8[:], in_=g8(ysq), axis=AX.X), r=[ysq], w=[v8])
            P.dve(lambda e: e.tensor_scalar(out=m8[:], in0=m8[:], scalar1=1.0 / 64, scalar2=None, op0=ALU.mult), r=[m8], w=[m8])
            P.dve(lambda e: e.tensor_tensor(out=s8[:], in0=m8[:], in1=m8[:], op=ALU.mult), r=[m8], w=[s8])
            P.dve(lambda e: e.scalar_tensor_tensor(out=v8[:], in0=v8[:], scalar=1.0 / 64, in1=s8[:], op0=ALU.mult,
                                                   op1=ALU.subtract), r=[v8, s8], w=[v8])
            P.act(lambda e: e.activation(out=v8[:], in_=v8[:], func=AF.Sqrt, bias=self.lnxeps[:, 0:1]),
                  r=[v8, self.lnxeps], w=[v8])
            P.dve(lambda e: e.reciprocal(out=v8[:], in_=v8[:]), r=[v8], w=[v8])
            P.dve(lambda e: e.tensor_tensor(out=g8(yf), in0=g8(yf), in1=b8(m8), op=ALU.subtract), r=[yf, m8], w=[yf])
            P.dve(lambda e: e.tensor_tensor(out=g8(yf), in0=g8(yf), in1=b8(v8), op=ALU.mult), r=[yf, v8], w=[yf])
            P.pool(lambda e: e.tensor_tensor(out=yf[:], in0=yf[:], in1=lg_b[:], op=ALU.mult), r=[yf, lg_b], w=[yf])
            P.pool(lambda e: e.tensor_tensor(out=yf[:], in0=yf[:], in1=lb_b[:], op=ALU.add), r=[yf, lb_b], w=[yf])
            P.pool(lambda e: e.tensor_tensor(out=g8(ysq), in0=g8(prev[:, 1024:1536]), in1=b8(bs8), op=ALU.mult),
                   r=[prev, bs8, ysq], w=[ysq])
            P.pool(lambda e: e.tensor_tensor(out=yf[:], in0=yf[:], in1=ysq[:], op=ALU.add), r=[yf, ysq], w=[yf])
            yot = yo[0]
            yc += 1
            P.dve(lambda e, yot=yot, gs=gs: e.tensor_tensor(out=yot[:], in0=yf[:], in1=gs[:], op=ALU.mult), r=[yf, gs], w=[yot])
            nr = DEC if sample else 128
            P.dma("pool", self.Mx[tt * 128:tt * 128 + nr, 512:1024], yot[0:nr, :], r=[yot], w=[self.dMx], sres=yot)
            if A2STOP <= 12:
                continue
            if tt == NPT - 1:
                store_state(self.wkv_p, soi); soi += 1
            if sample:
                store_state(self.wkv_s[s], soi); soi += 1
        self.end_phase()

    def phase_C(self):
        P = self.P
        P.flush()
        sem = P.sem("cc")
        CH = self.CCH
        n = 0
        for ci, t0 in enumerate(range(0, self.NTT, CH)):
            nt = min(CH, self.NTT - t0)
            ins = self.nc.gpsimd.collective_compute(
                "AllGather", ALU.bypass, replica_groups=[[0, 1], [2, 3], [4, 5], [6, 7]],
                ins=[self.Mx[t0 * 128:(t0 + nt) * 128, :]], outs=[self.Mall[ci][0:2 * nt * 128, :]])
            ins.then_inc(sem, 1)
            n += 1
        for eng in P.engs.values():
            eng.wait_ge(sem, n)

    def phase_D(self):
        P = self.P
        NTT, NPT = self.NTT, self.NPT
        R = NTT * 128
        self.begin_phase()
        Wo = P.sb("Wo", [128, 16, 1024], BF16)
        for k in range(16):
            P.dma("pool", Wo[:, k, :], self.w_out[k * 128:(k + 1) * 128, :], w=[Wo], sres=Wo)
        Mt = [P.sb(f"Mt{i}", [128, 2, 1024], BF16) for i in range(2)]
        MT = [P.sb(f"MT{i}", [128, 16, 128], BF16) for i in range(2)]
        xr = [P.sb(f"xr{i}", [128, 1024], F32) for i in range(2)]
        yo = [P.sb(f"yo{i}", [128, 1024], F32) for i in range(2)]
        pT = [P.ps(f"dpT{i}", [128, 8, 128], BF16) for i in range(2)]
        pY = [P.ps(f"dpY{i}", [128, 512], F32) for i in range(2)]
        yc = 0
        for tt in range(NTT):
            sample = tt >= NPT
            nr = DEC if sample else 128
            M = Mt[tt % 2]; T_ = MT[tt % 2]; X = xr[tt % 2]; Y = yo[tt % 2]
            if sample:
                P.pool(lambda e, M=M: e.memset(M[:], 0.0), w=[M])
            ci, tl_ = tt // self.CCH, tt % self.CCH
            ntc = min(self.CCH, NTT - ci * self.CCH)
            for rk in range(2):
                r0 = rk * ntc * 128 + tl_ * 128
                P.dma("sp", M[0:nr, rk, :], self.Mall[ci][r0:r0 + nr, :], r=[self.dMall], w=[M], sres=M)
            if sample:
                P.dma("sp", X[0:nr, :], self.xres_s[tt - NPT], w=[X], sres=X)
            else:
                P.dma("sp", X[:], self.xres_p[tt * 128:(tt + 1) * 128, :], w=[X], sres=X)
            for hf in range(2):
                pt = pT[hf]
                for j in range(8):
                    k = hf * 8 + j
                    P.pe(lambda e, pt=pt, j=j, k=k, M=M: e.transpose(
                        out=pt[:, j, :], in_=M[:, k // 8, (k % 8) * 128:(k % 8 + 1) * 128], identity=self.ident[:]),
                        r=[M, self.ident], w=[pt])
                if hf == 0:
                    P.act(lambda e, pt=pt, T_=T_: e.copy(out=T_[:, 0:8, :], in_=pt[:]), r=[pt], w=[T_])
                else:
                    P.dve(lambda e, pt=pt, T_=T_: e.tensor_copy(out=T_[:, 8:16, :], in_=pt[:]), r=[pt], w=[T_])
            for cb in range(2):
                py = pY[yc % 2]
                yc += 1
                for k in range(16):
                    P.pe(lambda e, py=py, k=k, cb=cb, T_=T_: e.matmul(
                        py[:], lhsT=T_[:, k, :], rhs=Wo[:, k, cb * 512:(cb + 1) * 512], start=(k == 0), stop=(k == 15)),
                        r=[T_, Wo], w=[py])
                P.dve(lambda e, py=py, cb=cb, X=X, Y=Y, nr=nr: e.tensor_tensor(
                    out=Y[0:nr, cb * 512:(cb + 1) * 512], in0=py[0:nr, :], in1=X[0:nr, cb * 512:(cb + 1) * 512], op=ALU.add),
                    r=[py, X], w=[Y])
            if sample:
                P.dma("pool", self.y_s[tt - NPT], Y[0:nr, :], r=[Y], w=[self.dOut], sres=Y)
            else:
                P.dma("pool", self.y_p[tt * 128:(tt + 1) * 128, :], Y[:], r=[Y], w=[self.dOut], sres=Y)
        self.end_phase()

    def phase_DBG(self):
        P = self.P
        t = P.sb("dbgt", [128, 1024], BF16)
        for tt in range(self.NTT):
            P.dma("sp", t[:], self.Mx[tt * 128:(tt + 1) * 128, :], r=[self.dMx], w=[t], sres=t)
            P.dma("sp", self.Mx_dbg[tt * 128:(tt + 1) * 128, :], t[:], r=[t], w=[self.dOut], sres=t)
        P.flush()

    def build(self):
        P = self.P
        self.setup()
        P.flush()
        phases = CFG["phases"].split(",")
        for ph in phases:
            if hasattr(self, "phase_" + ph):
                getattr(self, "phase_" + ph)()
        P.flush()
        self.stats = P.stats
        return self.nc


def _core_inputs(c, inp, NPT):
    b, hh = c // 2, c % 2
    T = NPT * 128
    f = lambda a: np.ascontiguousarray(a, dtype=np.float32)
    w_in = inp["w_in"][0]
    a = slice(hh * 512, hh * 512 + 512)
    att_cols = np.concatenate([np.arange(i * 1024 + hh * 512, i * 1024 + hh * 512 + 512) for i in range(4)])
    pb = 4096
    rw_cols = np.concatenate([np.arange(pb + i * 1024 + hh * 512, pb + i * 1024 + hh * 512 + 512) for i in range(3)]
                             + [np.arange(pb + 3072, pb + 3200)]
                             + [np.arange(pb + 3200 + hh * 512, pb + 3200 + hh * 512 + 512)])
    sh_cols = np.concatenate([np.arange(i * 1024 + hh * 512, i * 1024 + hh * 512 + 512) for i in range(3)]
                             + [np.arange(3072, 3200)])
    w_out = inp["w_out"][0]
    rows = np.concatenate([np.arange(0, 512), np.arange(1024, 1536), np.arange(512, 1024), np.arange(1536, 2048)])
    oc = slice(hh * 1024, hh * 1024 + 1024)
    sb = slice(4 * b, 4 * b + 4)
    d = {
        "xp": f(inp["x_prompt"][b, :T]),
        "xs": f(inp["x_sample"][sb]),
        "w_att": f(w_in[:, att_cols]),
        "w_rw": f(w_in[:, rw_cols]),
        "w_out": f(w_out[rows][:, oc]),
        "xres_p": f(inp["x_prompt"][b, :T, oc]),
        "xres_s": f(inp["x_sample"][sb, :, oc]),
        "norm_g": f(inp["norm_g"][0]),
        "qg": f(inp["q_norm_g"][0]), "kg": f(inp["k_norm_g"][0]),
        "lq1": f(inp["lambda_q1"][0]), "lk1": f(inp["lambda_k1"][0]),
        "lq2": f(inp["lambda_q2"][0]), "lk2": f(inp["lambda_k2"][0]),
        "subln": f(inp["subln_g"][0]),
        "mu": f(inp["shift_mu"][0][sh_cols]),
        "w0": f(inp["w0"][0][a]), "a0": f(inp["a0"][0][a]),
        "w_up": f(inp["w_up"][0][:, a]), "a_up": f(inp["a_up"][0][:, a]),
        "k_k": f(inp["k_k"][0][a]), "k_a": f(inp["k_a"][0][a]),
        "r_k": f(inp["r_k"][0][8 * hh:8 * hh + 8].reshape(512)),
        "lnx_g": f(inp["lnx_g"][0][a]), "lnx_b": f(inp["lnx_b"][0][a]),
        "cache_k": f(inp["cache_attn_k"][0, sb, :, 4 * hh:4 * hh + 4].reshape(4, PAST, 512)),
        "cache_v": f(inp["cache_attn_v"][0, sb, :, 4 * hh:4 * hh + 4].reshape(4, PAST, 512)),
        "wkv0": f(inp["state_rwkv_wkv"][0, sb, 8 * hh:8 * hh + 8]),
        "shift0": f(inp["state_rwkv_shift"][0, sb, 0][:, sh_cols]),
    }
    return d, sh_cols


_CACHE = {}


def run_device(inputs, NPT=None):
    NPT = CFG["NPT"] if NPT is None else NPT
    CFG["NPT"] = NPT
    key = (NPT, CFG["debug"], CFG["phases"])
    if key not in _CACHE:
        bld = Builder()
        nc = bld.build()
        _CACHE[key] = (nc, bld.stats)
    nc, stats = _CACHE[key]
    if CFG.get("debug"):
        print("CFG", {k: (v if not isinstance(v, (list, range)) else list(v)) for k, v in CFG.items()}, stats, flush=True)
    inp = {k: np.asarray(v) for k, v in inputs.items()}
    in_maps = []
    sh_cols = None
    for c in range(8):
        d, sh_cols = _core_inputs(c, inp, NPT)
        in_maps.append(d)
    ncores = CFG.get("ncores", 8)
    res = run_bass_kernel_spmd(nc, in_maps[:ncores], core_ids=list(range(ncores)))
    return res.results, sh_cols


def kernel(**inputs):
    NPT = CFG["NPT"]
    T = NPT * 128
    R, sh_cols = run_device(inputs, NPT)
    B = 4
    y_p = np.zeros((B, T, 2048), np.float32)
    y_s = np.zeros((16, DEC, 2048), np.float32)
    k_p = np.zeros((1, B, T, 8, 2, 64), np.float32)
    v_p = np.zeros((1, B, T, 8, 128), np.float32)
    wkv_p = np.zeros((1, B, 16, 64, 64), np.float32)
    sh_p = np.zeros((1, B, 1, 3200), np.float32)
    k_s = np.zeros((1, 16, DEC, 8, 2, 64), np.float32)
    v_s = np.zeros((1, 16, DEC, 8, 128), np.float32)
    wkv_s = np.zeros((1, 16, 16, 64, 64), np.float32)
    sh_s = np.zeros((1, 16, 1, 3200), np.float32)
    for c in range(8):
        b, hh = c // 2, c % 2
        r = R[c]
        sh_cols = np.concatenate([np.arange(i * 1024 + hh * 512, i * 1024 + hh * 512 + 512) for i in range(3)]
                                 + [np.arange(3072, 3200)])
        oc = slice(hh * 1024, hh * 1024 + 1024)
        sb = slice(4 * b, 4 * b + 4)
        y_p[b, :, oc] = r["y_p"]
        y_s[sb, :, oc] = r["y_s"]
        k_p[0, b, :, 4 * hh:4 * hh + 4] = r["k_p"].reshape(T, 4, 2, 64)
        v_p[0, b, :, 4 * hh:4 * hh + 4] = r["v_p"].reshape(T, 4, 128)
        wkv_p[0, b, 8 * hh:8 * hh + 8] = r["wkv_p"]
        sh_p[0, b, 0, sh_cols] = r["shift_p"]
        k_s[0, sb, :, 4 * hh:4 * hh + 4] = r["k_s"].reshape(4, DEC, 4, 2, 64)
        v_s[0, sb, :, 4 * hh:4 * hh + 4] = r["v_s"].reshape(4, DEC, 4, 128)
        wkv_s[0, sb, 8 * hh:8 * hh + 8] = r["wkv_s"]
        sh_s[0, sb, 0][:, sh_cols] = r["shift_s"]
    return (y_p, y_s, k_p, v_p, wkv_p, sh_p, k_s, v_s, wkv_s, sh_s)
```

```python
import math
from contextlib import ExitStack
import numpy as np
import concourse.bass as bass
import concourse.mybir as mybir
from concourse.bass_utils import run_bass_kernel_spmd

F32 = mybir.dt.float32
BF16 = mybir.dt.bfloat16
I32 = mybir.dt.int32
ALU = mybir.AluOpType
AF = mybir.ActivationFunctionType
AX = mybir.AxisListType

CFG = {"NPT": 64, "debug": False, "phases": "A1,B,A2,C,D"}
NST = 4
PAST = 2048
DEC = 32
EPS = 1e-6
LNX_EPS = 64e-5
LAM_INIT = 0.8 - 0.6 * math.exp(-0.3 * 0)


class Res:
    def __init__(self, name, handle=None, excl=False):
        self.name = name
        self.h = handle
        self.excl = excl
        self.writers = {}
        self.readers = {}
        self.dsem = None
        self.dcount = 0

    def __getitem__(self, key):
        return self.h[key]


class Op:
    __slots__ = ("eng", "fn", "reads", "writes", "dma", "sres", "deps", "sig", "sigval", "idx")


COMPUTE = ("pe", "act", "dve", "pool")


class Prog:
    def __init__(self, nc, es):
        self.nc = nc
        self.es = es
        self.root_es = es
        self.ops = []
        self.engs = {"pe": nc.tensor, "act": nc.scalar, "dve": nc.vector, "pool": nc.gpsimd, "sp": nc.sync}
        self.nsem = 0
        self.all_res = []
        self.phase_res = []
        self.free_dsems = []
        self.esem = None
        self.ecount = {e: 0 for e in COMPUTE}
        self.waited = {}
        self.ninst = {e: 0 for e in self.engs}
        self.nwait = 0
        self.nops = 0

    def _reg(self, r):
        self.all_res.append(r)
        if self.es is not self.root_es:
            self.phase_res.append(r)
        return r

    def sb(self, name, shape, dtype):
        self.nalloc = getattr(self, "nalloc", 0) + 1
        name = f"{name}_{self.nalloc}"
        return self._reg(Res(name, self.es.enter_context(self.nc.sbuf_tensor(name, list(shape), dtype))))

    def ps(self, name, shape, dtype):
        self.nalloc = getattr(self, "nalloc", 0) + 1
        name = f"{name}_{self.nalloc}"
        return self._reg(Res(name, self.es.enter_context(self.nc.psum_tensor(name, list(shape), dtype)), excl=True))

    def dres(self, name):
        return self._reg(Res(name))

    def sub(self, res, n):
        return [self._reg(Res(f"{res.name}.{i}", res.h)) for i in range(n)]

    def sem(self, name):
        self.nsem += 1
        return self.root_es.enter_context(self.nc.semaphore(name))

    def op(self, eng, fn, r=(), w=()):
        o = Op()
        o.eng = eng; o.fn = fn; o.reads = tuple(r); o.writes = tuple(w)
        o.dma = False; o.sres = None; o.deps = None; o.sig = False; o.sigval = 0
        o.idx = self.nops
        self.nops += 1
        self.ops.append(o)
        return o

    def pe(self, fn, r=(), w=()): return self.op("pe", fn, r, w)
    def act(self, fn, r=(), w=()): return self.op("act", fn, r, w)
    def dve(self, fn, r=(), w=()): return self.op("dve", fn, r, w)
    def pool(self, fn, r=(), w=()): return self.op("pool", fn, r, w)

    def dma(self, q, out, in_, r=(), w=(), sres=None, **kw):
        if CFG.get("nostore") and q == "pool" and w and w[0].h is None:
            return None
        o = self.op(q, lambda e: e.dma_start(out=out, in_=in_, **kw), r, w)
        o.dma = True
        o.sres = sres
        assert sres is not None
        return o

    def _key(self, o):
        return ("d", id(o.sres)) if o.dma else o.eng

    def flush(self):
        if self.esem is None:
            self.esem = {e: self.sem("S_" + e) for e in COMPUTE}
        esem, ecount, waited = self.esem, self.ecount, self.waited
        last = {}
        for o in self.ops:
            deps = {}

            def add(d):
                if d is o:
                    return
                if (not d.dma) and (not o.dma) and d.eng == "pe" and o.eng == "pe":
                    return
                deps[d.idx] = d
            for r in o.reads:
                for d in r.writers.values():
                    add(d)
                if r.excl:
                    for d in r.readers.values():
                        add(d)
            for w in o.writes:
                if w.readers:
                    for d in w.readers.values():
                        add(d)
                    for d in w.writers.values():
                        add(d)
                else:
                    for d in w.writers.values():
                        if o.dma and d.dma:
                            continue
                        add(d)
            k = self._key(o)
            for r in o.reads:
                r.readers[k] = o
            for w in o.writes:
                if w.readers:
                    w.writers = {k: o}
                    w.readers = {}
                else:
                    w.writers[k] = o
            o.deps = list(deps.values())
            for d in o.deps:
                d.sig = True
            if not o.dma:
                last[o.eng] = o
        for o in last.values():
            o.sig = True
        for o in self.ops:
            eng = self.engs[o.eng]
            need = {}
            for d in o.deps:
                if d.dma:
                    s = d.sres
                    need[("d", id(s.dsem))] = (s.dsem, s.dcount)
                else:
                    v = d.sigval
                    assert v > 0
                    if d.eng not in need or need[d.eng][1] < v:
                        need[d.eng] = (esem[d.eng], v)
            for key, (s, v) in need.items():
                wk = (o.eng, key)
                if waited.get(wk, 0) >= v:
                    continue
                waited[wk] = v
                eng.wait_ge(s, v)
                self.nwait += 1
            ins = o.fn(eng)
            self.ninst[o.eng] += 1
            if o.dma:
                s = o.sres
                if s.dsem is None:
                    if self.free_dsems:
                        s.dsem, s.dcount = self.free_dsems.pop()
                    else:
                        s.dsem, s.dcount = self.sem("D_" + s.name), 0
                s.dcount += 16
                ins.then_inc(s.dsem, 16)
            elif o.sig:
                ecount[o.eng] += 1
                o.sigval = ecount[o.eng]
                ins.then_inc(esem[o.eng], 1)
        dsems = {}
        for r in self.all_res:
            if r.dsem is not None:
                dsems[id(r.dsem)] = (r.dsem, r.dcount)
        for s, c in self.free_dsems:
            dsems[id(s)] = (s, c)
        for en, eng in self.engs.items():
            for e2 in COMPUTE:
                if e2 == en or ecount[e2] == 0:
                    continue
                wk = (en, e2)
                if waited.get(wk, 0) < ecount[e2]:
                    waited[wk] = ecount[e2]
                    eng.wait_ge(esem[e2], ecount[e2])
            for key, (s, c) in dsems.items():
                wk = (en, ("d", key))
                if waited.get(wk, 0) < c:
                    waited[wk] = c
                    eng.wait_ge(s, c)
        for r in self.all_res:
            r.writers = {}
            r.readers = {}
        self.ops = []

    def end_phase(self):
        self.flush()
        for r in self.phase_res:
            if r.dsem is not None:
                self.free_dsems.append((r.dsem, r.dcount))
                r.dsem = None
        pr = set(id(r) for r in self.phase_res)
        self.all_res = [r for r in self.all_res if id(r) not in pr]
        self.phase_res = []

    @property
    def stats(self):
        return dict(ninst=self.ninst, nwait=self.nwait, nsem=self.nsem, sig=dict(self.ecount))


class Builder:
    def __init__(self):
        self.NPT = CFG["NPT"]
        self.NTT = self.NPT + NST
        self.T = self.NPT * 128
        self.nc = bass.Bass("TRN2", target_bir_lowering=False)
        self.es = ExitStack()
        self.P = Prog(self.nc, self.es)
        self.declare_dram()

    def dram(self, name, shape, dtype, kind):
        kw = {}
        if kind == "Internal":
            kw["addr_space"] = "Local"
        return self.nc.dram_tensor(name, list(shape), dtype, kind=kind, **kw).ap()

    def declare_dram(self):
        T, NTT = self.T, self.NTT
        I = lambda n, s: self.dram(n, s, F32, "ExternalInput")
        O = lambda n, s: self.dram(n, s, F32, "ExternalOutput")
        self.xp = I("xp", [T, 2048])
        self.xs = I("xs", [NST, DEC, 2048])
        self.w_att = I("w_att", [2048, 2048])
        self.w_rw = I("w_rw", [2048, 2176])
        self.w_out = I("w_out", [2048, 1024])
        self.xres_p = I("xres_p", [T, 1024])
        self.xres_s = I("xres_s", [NST, DEC, 1024])
        self.norm_g = I("norm_g", [2048])
        self.qg = I("qg", [64]); self.kg = I("kg", [64])
        self.lq1 = I("lq1", [64]); self.lk1 = I("lk1", [64]); self.lq2 = I("lq2", [64]); self.lk2 = I("lk2", [64])
        self.subln = I("subln", [128])
        self.mu = I("mu", [1664])
        self.w0 = I("w0", [512]); self.a0 = I("a0", [512])
        self.w_up = I("w_up", [64, 512]); self.a_up = I("a_up", [64, 512])
        self.k_k = I("k_k", [512]); self.k_a = I("k_a", [512]); self.r_k = I("r_k", [512])
        self.lnx_g = I("lnx_g", [512]); self.lnx_b = I("lnx_b", [512])
        self.cache_k = I("cache_k", [NST, PAST, 512])
        self.cache_v = I("cache_v", [NST, PAST, 512])
        self.wkv0 = I("wkv0", [NST, 8, 64, 64])
        self.shift0 = I("shift0", [NST, 1664])
        self.y_p = O("y_p", [T, 1024]); self.y_s = O("y_s", [NST, DEC, 1024])
        self.k_p = O("k_p", [T, 512]); self.v_p = O("v_p", [T, 512])
        self.wkv_p = O("wkv_p", [8, 64, 64]); self.shift_p = O("shift_p", [1664])
        self.k_s = O("k_s", [NST, DEC, 512]); self.v_s = O("v_s", [NST, DEC, 512])
        self.wkv_s = O("wkv_s", [NST, 8, 64, 64]); self.shift_s = O("shift_s", [NST, 1664])
        S = lambda n, s, d=BF16: self.dram(n, s, d, "Internal")
        self.QT = S("QT", [4, 128, NTT * 128])
        self.KT = S("KT", [4, 128, NTT * 128])
        self.Vs = S("Vs", [4, 128, NTT, 128])
        self.Gs = S("Gs", [NTT * 128, 512])
        dbg = CFG["debug"]
        self.Mx = self.dram("Mxs", [NTT * 128, 1024], BF16, "Internal")
        if dbg:
            self.Mx_dbg = self.dram("Mx", [NTT * 128, 1024], BF16, "ExternalOutput")
        self.CCH = 8
        self.Mall = [S(f"Mall{i}", [2 * self.CCH * 128, 1024]) for i in range((NTT + self.CCH - 1) // self.CCH)]
        D = self.P.dres
        self.dQT = D("dQT"); self.dKT = D("dKT"); self.dVs = D("dVs"); self.dGs = D("dGs")
        self.dMx = D("dMx"); self.dMall = D("dMall"); self.dOut = D("dOut")
        self.zero_pad()

    def zero_pad(self):
        pass

    def begin_phase(self):
        self.pes = ExitStack()
        self.P.es = self.pes

    def end_phase(self):
        self.P.end_phase()
        self.pes.close()
        self.P.es = self.P.root_es

    def bcast_load(self, name, src, n, q="sp"):
        t = self.P.sb(name, [128, n], F32)
        self.P.dma(q, t[:], src.partition_broadcast(128), w=[t], sres=t)
        return t

    def tile_rows(self, tt):
        return 128 if tt < self.NPT else DEC

    def setup(self):
        P, nc = self.P, self.nc
        NTT, NPT = self.NTT, self.NPT
        self.identf = P.sb("identf", [128, 128], F32)
        self.ident = P.sb("ident", [128, 128], BF16)
        self.maskf = P.sb("maskf", [128, 384], F32)
        self.maskb = P.sb("maskb", [128, 384], BF16)
        self.epst = P.sb("epst", [128, 1], F32)
        self.lnxeps = P.sb("lnxeps", [128, 1], F32)
        idf, mk = self.identf, self.maskf
        P.pool(lambda e: e.memset(self.epst[:], EPS), w=[self.epst])
        P.pool(lambda e: e.memset(self.lnxeps[:], LNX_EPS), w=[self.lnxeps])
        P.pool(lambda e: e.memset(idf[:], 1.0), w=[idf])
        P.pool(lambda e: e.affine_select(out=idf[:], in_=idf[:], compare_op=ALU.is_ge, fill=0.0, base=0,
                                         pattern=[[-1, 128]], channel_multiplier=1), r=[idf], w=[idf])
        P.pool(lambda e: e.affine_select(out=idf[:], in_=idf[:], compare_op=ALU.is_ge, fill=0.0, base=0,
                                         pattern=[[1, 128]], channel_multiplier=-1), r=[idf], w=[idf])
        P.dve(lambda e: e.tensor_copy(out=self.ident[:], in_=idf[:]), r=[idf], w=[self.ident])
        P.pool(lambda e: e.memset(mk[:], 1.0), w=[mk])
        P.pool(lambda e: e.affine_select(out=mk[:, 0:128], in_=mk[:, 0:128], compare_op=ALU.is_gt, fill=0.0, base=0,
                                         pattern=[[1, 128]], channel_multiplier=-1), r=[mk], w=[mk])
        P.pool(lambda e: e.affine_select(out=mk[:, 128:256], in_=mk[:, 128:256], compare_op=ALU.is_ge, fill=0.0, base=0,
                                         pattern=[[1, 128]], channel_multiplier=-1), r=[mk], w=[mk])
        P.pool(lambda e: e.affine_select(out=mk[:, 256:384], in_=mk[:, 256:384], compare_op=ALU.is_gt, fill=0.0, base=0,
                                         pattern=[[-1, 128]], channel_multiplier=1), r=[mk], w=[mk])
        P.dve(lambda e: e.tensor_copy(out=self.maskb[:], in_=mk[:]), r=[mk], w=[self.maskb])
        self.gb = self.bcast_load("gb", self.norm_g, 2048)
        self.gq = self.bcast_load("gq", self.qg, 64)
        self.gk = self.bcast_load("gk", self.kg, 64)
        self.ngq = P.sb("ngq", [128, 32], F32)
        self.ngk = P.sb("ngk", [128, 32], F32)
        for dst, src in ((self.ngq, self.gq), (self.ngk, self.gk)):
            P.dve(lambda e, dst=dst, src=src: e.tensor_scalar(out=dst[:], in0=src[:, 32:64], scalar1=-1.0, scalar2=None,
                                                             op0=ALU.mult), r=[src], w=[dst])
        self.sub_b = self.bcast_load("sub_b", self.subln, 128)
        P.dve(lambda e: e.tensor_scalar(out=self.sub_b[:], in0=self.sub_b[:], scalar1=1.0 - LAM_INIT, scalar2=None,
                                        op0=ALU.mult), r=[self.sub_b], w=[self.sub_b])
        lam_in = [self.bcast_load(n, s, 64) for n, s in (("lq1b", self.lq1), ("lk1b", self.lk1),
                                                         ("lq2b", self.lq2), ("lk2b", self.lk2))]
        lj = P.sb("lj", [128, 64], F32)
        ls = P.sb("ls", [128, 2], F32)
        self.neglam = P.sb("neglam", [128, 1], F32)
        for i in range(2):
            a, b = lam_in[2 * i], lam_in[2 * i + 1]
            P.dve(lambda e, a=a, b=b: e.tensor_tensor(out=lj[:], in0=a[:], in1=b[:], op=ALU.mult), r=[a, b], w=[lj])
            P.dve(lambda e, i=i: e.reduce_sum(out=ls[:, i:i + 1], in_=lj[:], axis=AX.X), r=[lj], w=[ls])
        P.act(lambda e: e.activation(out=ls[:], in_=ls[:], func=AF.Exp), r=[ls], w=[ls])
        P.dve(lambda e: e.tensor_tensor(out=self.neglam[:], in0=ls[:, 1:2], in1=ls[:, 0:1], op=ALU.subtract),
              r=[ls], w=[self.neglam])
        P.dve(lambda e: e.tensor_scalar(out=self.neglam[:], in0=self.neglam[:], scalar1=-LAM_INIT, scalar2=None,
                                        op0=ALU.add), r=[self.neglam], w=[self.neglam])
        self.mq = P.sb("mq", [128, 1], F32)
        self.mk8 = P.sb("mk8", [128, 1], F32)
        P.dve(lambda e: e.tensor_reduce(out=self.mq[:], in_=self.gq[:], axis=AX.X, op=ALU.max,
                                        apply_absolute_value=True), r=[self.gq], w=[self.mq])
        P.dve(lambda e: e.tensor_reduce(out=self.mk8[:], in_=self.gk[:], axis=AX.X, op=ALU.max,
                                        apply_absolute_value=True), r=[self.gk], w=[self.mk8])
        P.dve(lambda e: e.tensor_tensor(out=self.mk8[:], in0=self.mk8[:], in1=self.mk8[:], op=ALU.mult),
              r=[self.mk8], w=[self.mk8])
        P.dve(lambda e: e.tensor_scalar(out=self.mk8[:], in0=self.mk8[:], scalar1=64.0, scalar2=None, op0=ALU.mult),
              r=[self.mk8], w=[self.mk8])
        self.negC_p = P.sb("negC_p", [128, 1], F32)
        P.act(lambda e: e.activation(out=self.negC_p[:], in_=self.mk8[:], func=AF.Sqrt), r=[self.mk8], w=[self.negC_p])
        P.dve(lambda e: e.scalar_tensor_tensor(out=self.negC_p[:], in0=self.negC_p[:], scalar=-1.0, in1=self.mq[:],
                                               op0=ALU.mult, op1=ALU.mult), r=[self.negC_p, self.mq], w=[self.negC_p])

    def rope_tables(self):
        P = self.P
        NTT, NPT = self.NTT, self.NPT
        self.cosT = P.sb("cosT", [128, NTT, 32], F32)
        self.sinT = P.sb("sinT", [128, NTT, 32], F32)
        posi = P.sb("posi", [128, NTT], I32)
        posf = P.sb("posf", [128, NTT], F32)
        P.pool(lambda e: e.iota(posi[:, 0:NPT], pattern=[[128, NPT]], base=0, channel_multiplier=1), w=[posi])
        P.pool(lambda e: e.iota(posi[:, NPT:NTT], pattern=[[0, NST]], base=PAST, channel_multiplier=1), w=[posi])
        P.dve(lambda e: e.tensor_copy(out=posf[:], in_=posi[:]), r=[posi], w=[posf])
        invf = P.sb("invf", [128, 32], F32)
        for i in range(32):
            v = float(np.float32(10000.0) ** np.float32(-i / 32.0))
            P.pool(lambda e, i=i, v=v: e.memset(invf[:, i:i + 1], v), w=[invf])
        ang = P.sb("ang", [128, NTT, 32], F32)
        a2 = P.sb("ang2", [128, NTT, 32], F32)
        ki = P.sb("angk", [128, NTT, 32], I32)
        kf = P.sb("angkf", [128, NTT, 32], F32)
        P.dve(lambda e: e.tensor_tensor(out=ang[:], in0=posf[:, :].unsqueeze(2).broadcast_to([128, NTT, 32]),
                                        in1=invf[:, :].unsqueeze(1).broadcast_to([128, NTT, 32]), op=ALU.mult),
              r=[posf, invf], w=[ang])
        C1 = 6.28125
        C2 = 2.0 * math.pi - C1
        for which, dst, off in (("s", self.sinT, 0.0), ("c", self.cosT, math.pi / 2)):
            P.dve(lambda e, off=off: e.tensor_scalar(out=a2[:], in0=ang[:], scalar1=off, scalar2=1.0 / (2 * math.pi),
                                                     op0=ALU.add, op1=ALU.mult), r=[ang], w=[a2])
            P.dve(lambda e: e.tensor_copy(out=ki[:], in_=a2[:]), r=[a2], w=[ki])
            P.dve(lambda e: e.tensor_copy(out=kf[:], in_=ki[:]), r=[ki], w=[kf])
            P.dve(lambda e, off=off: e.tensor_scalar(out=a2[:], in0=ang[:], scalar1=off, scalar2=None, op0=ALU.add),
                  r=[ang], w=[a2])
            P.dve(lambda e: e.scalar_tensor_tensor(out=a2[:], in0=kf[:], scalar=-C1, in1=a2[:], op0=ALU.mult, op1=ALU.add),
                  r=[kf, a2], w=[a2])
            P.dve(lambda e: e.scalar_tensor_tensor(out=a2[:], in0=kf[:], scalar=-C2, in1=a2[:], op0=ALU.mult, op1=ALU.add),
                  r=[kf, a2], w=[a2])
            P.dve(lambda e: e.tensor_scalar(out=kf[:], in0=a2[:], scalar1=math.pi, scalar2=-2 * math.pi,
                                            op0=ALU.is_gt, op1=ALU.mult), r=[a2], w=[kf])
            P.dve(lambda e: e.tensor_tensor(out=a2[:], in0=a2[:], in1=kf[:], op=ALU.add), r=[a2, kf], w=[a2])
            P.dve(lambda e: e.tensor_scalar(out=kf[:], in0=a2[:], scalar1=-math.pi, scalar2=2 * math.pi,
                                            op0=ALU.is_lt, op1=ALU.mult), r=[a2], w=[kf])
            P.dve(lambda e: e.tensor_tensor(out=a2[:], in0=a2[:], in1=kf[:], op=ALU.add), r=[a2, kf], w=[a2])
            P.dve(lambda e: e.tensor_scalar(out=a2[:], in0=a2[:], scalar1=3.1415925, scalar2=-3.1415925,
                                            op0=ALU.min, op1=ALU.max), r=[a2], w=[a2])
            P.act(lambda e, dst=dst: e.activation(out=dst[:], in_=a2[:], func=AF.Sin), r=[a2], w=[dst])


    def alloc_xnorm(self, nx=2):
        P = self.P
        self.xt = [P.sb(f"xt{i}", [128, 2048], F32) for i in range(nx)]
        self.ss = P.sb("ss", [128, 1], F32)
        self.rstd = P.sb("rstd", [128, 1], F32)
        self.xn = P.sb("xn", [128, 2048], BF16)
        self.junk = self.xn
        self.xnT = [P.sb(f"xnT{i}", [128, 16, 128], BF16) for i in range(nx)]
        self.pT = [P.ps(f"pT{i}", [128, 8, 128], BF16) for i in range(2)]
        self.xcnt = 0

    def xnorm_tile(self, tt):
        P = self.P
        X = self.xt[self.xcnt % len(self.xt)]
        XT = self.xnT[self.xcnt % len(self.xnT)]
        self.xcnt += 1
        if tt < self.NPT:
            P.dma("sp", X[:], self.xp[tt * 128:(tt + 1) * 128, :], w=[X], sres=X)
        else:
            s = tt - self.NPT
            P.pool(lambda e: e.memset(X[:, :], 0.0), w=[X])
            P.dma("sp", X[0:DEC, :], self.xs[s], w=[X], sres=X)
        ss, rstd, xn, junk = self.ss, self.rstd, self.xn, self.junk
        P.act(lambda e: e.activation(out=junk[:], in_=X[:], func=AF.Square, accum_out=ss[:]), r=[X], w=[xn, ss])
        P.act(lambda e: e.activation(out=rstd[:], in_=ss[:], func=AF.Sqrt, scale=1.0 / 2048, bias=self.epst[:, 0:1]),
              r=[ss, self.epst], w=[rstd])
        P.dve(lambda e: e.reciprocal(out=rstd[:], in_=rstd[:]), r=[rstd], w=[rstd])
        P.dve(lambda e: e.scalar_tensor_tensor(out=xn[:], in0=X[:], scalar=rstd[:, 0:1], in1=self.gb[:],
                                               op0=ALU.mult, op1=ALU.mult), r=[X, rstd, self.gb], w=[xn])
        for hf in range(2):
            pt = self.pT[hf]
            for j in range(8):
                k = hf * 8 + j
                P.pe(lambda e, pt=pt, j=j, k=k: e.transpose(out=pt[:, j, :], in_=xn[:, k * 128:(k + 1) * 128],
                                                           identity=self.ident[:]), r=[xn, self.ident], w=[pt])
            if hf == 0:
                P.act(lambda e, pt=pt: e.copy(out=XT[:, 0:8, :], in_=pt[:]), r=[pt], w=[XT])
            else:
                P.dve(lambda e, pt=pt: e.tensor_copy(out=XT[:, 8:16, :], in_=pt[:]), r=[pt], w=[XT])
        return XT

    def phase_A1(self):
        P = self.P
        NTT, NPT = self.NTT, self.NPT
        self.begin_phase()
        self.rope_tables()
        self.alloc_xnorm()
        W = P.sb("Wb", [128, 16, 2048], BF16)
        self.pY = [P.ps(f"pY{i}", [128, 512], F32) for i in range(2)]
        self.pTq = P.ps("pTq", [128, 8, 128], BF16)
        for k in range(16):
            P.dma("pool", W[:, k, :], self.w_att[k * 128:(k + 1) * 128, :], w=[W], sres=W)
        pY = self.pY
        pTq = self.pTq
        sq = P.sb("a1_sq", [128, 512], F32)
        ssq = P.sb("a1_ssq", [128, 8], F32)
        rq = P.sb("a1_rq", [128, 8], F32)
        qs = P.sb("a1_qs", [128, 8, 64], F32)
        tmp = P.sb("a1_tmp", [128, 8, 64], F32)
        of = [P.sb(f"a1_of{i}", [128, 8, 64], F32) for i in range(2)]
        ob = P.sb("a1_ob", [128, 512], BF16)
        qT = [P.sb(f"a1_qT{i}", [128, 4, 128], BF16) for i in range(2)]
        vf = [P.sb(f"a1_vf{i}", [128, 512], F32) for i in range(2)]
        vb = [P.sb(f"a1_vb{i}", [128, 512], BF16) for i in range(2)]
        gbf = [P.sb(f"a1_gb{i}", [128, 512], BF16) for i in range(2)]
        tabs = {}
        for nm in ("q", "k"):
            tabs[nm] = (P.sb(f"a1_CC{nm}", [128, 64], F32), P.sb(f"a1_S1{nm}", [128, 32], F32),
                        P.sb(f"a1_nS2{nm}", [128, 32], F32))
        cnt = 0
        STOP = CFG.get("stop", 99)
        tlist = list(CFG.get("tiles", range(NTT)))
        XT_next = self.xnorm_tile(tlist[0])
        for ti, tt in enumerate(tlist):
            XT = XT_next
            if STOP <= 1:
                continue
            for nm, g, ng in (("q", self.gq, self.ngq), ("k", self.gk, self.ngk)):
                CC, S1, nS2 = tabs[nm]
                cos_b = self.cosT[:, tt, :].unsqueeze(1).broadcast_to([128, 2, 32])
                P.pool(lambda e, CC=CC, g=g, cos_b=cos_b: e.tensor_tensor(
                    out=CC[:, :].rearrange("p (a b) -> p a b", a=2), in0=g[:, :].rearrange("p (a b) -> p a b", a=2),
                    in1=cos_b, op=ALU.mult), r=[g, self.cosT], w=[CC])
                P.pool(lambda e, S1=S1, g=g, tt=tt: e.tensor_tensor(out=S1[:], in0=g[:, 0:32], in1=self.sinT[:, tt, :],
                                                                   op=ALU.mult), r=[g, self.sinT], w=[S1])
                P.pool(lambda e, nS2=nS2, ng=ng, tt=tt: e.tensor_tensor(out=nS2[:], in0=ng[:], in1=self.sinT[:, tt, :],
                                                                        op=ALU.mult), r=[ng, self.sinT], w=[nS2])
            for cb, nm in enumerate(("q", "k", "v", "g")):
                py = pY[cnt % 2]
                cnt += 1
                for k in range(16):
                    P.pe(lambda e, py=py, k=k, cb=cb, XT=XT: e.matmul(
                        py[:], lhsT=XT[:, k, :], rhs=W[:, k, cb * 512:(cb + 1) * 512], start=(k == 0), stop=(k == 15)),
                        r=[XT, W], w=[py])
                if cb == 0 and ti + 1 < len(tlist):
                    XT_next = self.xnorm_tile(tlist[ti + 1])
                if STOP <= 2:
                    continue
                if nm in ("q", "k"):
                    CC, S1, nS2 = tabs[nm]
                    o_f = of[cnt % 2]
                    P.act(lambda e, py=py: e.activation(out=sq[:], in_=py[:], func=AF.Square), r=[py], w=[sq])
                    P.dve(lambda e: e.reduce_sum(out=ssq[:], in_=sq[:, :].rearrange("p (g d) -> p g d", g=8), axis=AX.X),
                          r=[sq], w=[ssq])
                    P.act(lambda e: e.activation(out=rq[:], in_=ssq[:], func=AF.Sqrt, scale=1.0 / 64,
                                                 bias=self.epst[:, 0:1]), r=[ssq, self.epst], w=[rq])
                    P.dve(lambda e: e.reciprocal(out=rq[:], in_=rq[:]), r=[rq], w=[rq])
                    if STOP <= 3:
                        continue
                    P.dve(lambda e, py=py: e.tensor_tensor(
                        out=qs[:], in0=py[:, :].rearrange("p (g d) -> p g d", g=8),
                        in1=rq[:, :].unsqueeze(2).broadcast_to([128, 8, 64]), op=ALU.mult), r=[py, rq], w=[qs])
                    if STOP <= 4:
                        continue
                    P.dve(lambda e, o_f=o_f, CC=CC: e.tensor_tensor(
                        out=o_f[:], in0=qs[:], in1=CC[:, :].unsqueeze(1).broadcast_to([128, 8, 64]), op=ALU.mult),
                        r=[qs, CC], w=[o_f])
                    P.pool(lambda e, nS2=nS2: e.tensor_tensor(
                        out=tmp[:, :, 0:32], in0=qs[:, :, 32:64],
                        in1=nS2[:, :].unsqueeze(1).broadcast_to([128, 8, 32]), op=ALU.mult), r=[qs, nS2], w=[tmp])
                    P.pool(lambda e, S1=S1: e.tensor_tensor(
                        out=tmp[:, :, 32:64], in0=qs[:, :, 0:32],
                        in1=S1[:, :].unsqueeze(1).broadcast_to([128, 8, 32]), op=ALU.mult), r=[qs, S1], w=[tmp])
                    P.dve(lambda e, o_f=o_f: e.tensor_tensor(out=o_f[:], in0=o_f[:], in1=tmp[:], op=ALU.add),
                          r=[o_f, tmp], w=[o_f])
                    if STOP <= 5:
                        continue
                    P.act(lambda e, o_f=o_f: e.copy(out=ob[:], in_=o_f[:, :, :].rearrange("p g d -> p (g d)")),
                          r=[o_f], w=[ob])
                    if nm == "k":
                        if tt < NPT:
                            P.dma("pool", self.k_p[tt * 128:(tt + 1) * 128, :], o_f[:, :, :].rearrange("p g d -> p (g d)"),
                                  r=[o_f], w=[self.dOut], sres=o_f)
                        else:
                            P.dma("pool", self.k_s[tt - NPT], o_f[0:DEC, :, :].rearrange("p g d -> p (g d)"),
                                  r=[o_f], w=[self.dOut], sres=o_f)
                    qt = qT[cnt % 2]
                    for h in range(4):
                        P.pe(lambda e, h=h: e.transpose(out=pTq[:, h, :], in_=ob[:, h * 128:(h + 1) * 128],
                                                        identity=self.ident[:]), r=[ob, self.ident], w=[pTq])
                    P.dve(lambda e, qt=qt: e.tensor_copy(out=qt[:], in_=pTq[:, 0:4, :]), r=[pTq], w=[qt])
                    dst, dres = (self.QT, self.dQT) if nm == "q" else (self.KT, self.dKT)
                    P.dma("pool", dst[:, :, tt * 128:(tt + 1) * 128].rearrange("h p c -> p h c"), qt[:],
                          r=[qt], w=[dres], sres=qt)
                elif STOP <= 6:
                    continue
                elif nm == "v" and CFG.get("nov"):
                    continue
                elif nm == "v":
                    v_f = vf[cnt % 2]
                    v_b = vb[cnt % 2]
                    P.act(lambda e, py=py, v_f=v_f: e.copy(out=v_f[:], in_=py[:]), r=[py], w=[v_f])
                    P.pool(lambda e, v_f=v_f, v_b=v_b: e.tensor_copy(out=v_b[:], in_=v_f[:]), r=[v_f], w=[v_b])
                    if tt < NPT:
                        P.dma("pool", self.v_p[tt * 128:(tt + 1) * 128, :], v_f[:], r=[v_f], w=[self.dOut], sres=v_f)
                    else:
                        P.dma("pool", self.v_s[tt - NPT], v_f[0:DEC, :], r=[v_f], w=[self.dOut], sres=v_f)
                    P.dma("pool", self.Vs[:, :, tt, :].rearrange("h p d -> p h d"),
                          v_b[:, :].rearrange("p (h d) -> p h d", h=4), r=[v_b], w=[self.dVs], sres=v_b)
                else:
                    g_b = gbf[cnt % 2]
                    P.act(lambda e, py=py, g_b=g_b: e.activation(out=g_b[:], in_=py[:], func=(AF.Copy if CFG.get("nosilu") else AF.Silu)), r=[py], w=[g_b])
                    P.dma("pool", self.Gs[tt * 128:(tt + 1) * 128, :], g_b[:], r=[g_b], w=[self.dGs], sres=g_b)
        self.end_phase()

    def phase_B(self):
        P = self.P
        NTT, NPT, T = self.NTT, self.NPT, self.T
        self.begin_phase()
        self.pTq = P.ps("pTq", [128, 8, 128], BF16)
        self.pM = P.ps("pM", [128, 512], F32)
        NKT = NPT
        Kb = [P.sb(f"b_K{i}", [128, max(T, PAST + 128)], BF16) for i in range(2)]
        Vb = [P.sb(f"b_V{i}", [128, max(NKT, 17), 130], BF16) for i in range(2)]
        for v in Vb:
            P.pool(lambda e, v=v: e.memset(v[:, :, 128:129], 1.0), w=[v])
            P.pool(lambda e, v=v: e.memset(v[:, :, 129:130], 0.0), w=[v])
        Kc = P.sb("b_Kc", [128, 16, 128], BF16)
        Qt = [P.sb(f"b_Q{i}", [128, 512], BF16) for i in range(2)]
        Pt = [P.sb(f"b_P{i}", [128, 512], BF16) for i in range(3)]
        Gt = [P.sb(f"b_G{i}", [128, 4, 128], BF16) for i in range(2)]
        pS = [P.ps(f"b_pS{i}", [128, 512], F32) for i in range(2)]
        pO = [[P.ps(f"b_pO{c}{i}", [128, 2, 256], F32) for i in range(2)] for c in range(2)]
        pTk = self.pTq
        rl = P.sb("b_rl", [128, 2], F32)
        o1 = P.sb("b_o1", [128, 128], F32)
        o2 = P.sb("b_o2", [128, 128], F32)
        jk = P.sb("b_jk", [128, 128], BF16)
        oss = P.sb("b_oss", [128, 1], F32)
        ors = P.sb("b_ors", [128, 1], F32)
        ob = [P.sb(f"b_ob{i}", [128, 4, 128], BF16) for i in range(2)]
        kss = P.sb("b_kss", [128, 8], F32)
        kmx = P.sb("b_kmx", [128, 1], F32)
        kmr = P.sb("b_kmr", [1, 128], F32)
        km1 = P.sb("b_km1", [1, 1], BF16)
        kmxb = P.sb("b_kmxb", [128, 1], BF16)
        onesr = P.sb("b_ones", [1, 128], BF16)
        P.pool(lambda e: e.memset(onesr[:], 1.0), w=[onesr])
        negC_s = P.sb("b_negCs", [128, 1], F32)
        Kcf = P.sb("b_Kcf", [128, 512], F32)
        pM = self.pM
        hc = 0
        qc = 0
        pc = 0
        sc = 0
        oc = 0
        seqs = [("p", 0)] + [("s", s) for s in range(NST)]
        for kind, s in seqs:
            if kind == "s":
                for kt in range(16):
                    P.dma("sp", Kcf[:], self.cache_k[s, kt * 128:(kt + 1) * 128, :], w=[Kcf], sres=Kcf)
                    P.dve(lambda e: e.tensor_tensor(out=Kcf[:], in0=Kcf[:], in1=Kcf[:], op=ALU.mult), r=[Kcf], w=[Kcf])
                    P.dve(lambda e: e.reduce_sum(out=kss[:], in_=Kcf[:, :].rearrange("p (g d) -> p g d", g=8), axis=AX.X),
                          r=[Kcf], w=[kss])
                    if kt == 0:
                        P.dve(lambda e: e.tensor_reduce(out=kmx[:], in_=kss[:], axis=AX.X, op=ALU.max), r=[kss], w=[kmx])
                    else:
                        P.dve(lambda e: e.tensor_reduce(out=kss[:, 0:1], in_=kss[:], axis=AX.X, op=ALU.max),
                              r=[kss], w=[kss])
                        P.dve(lambda e: e.tensor_tensor(out=kmx[:], in0=kmx[:], in1=kss[:, 0:1], op=ALU.max),
                              r=[kmx, kss], w=[kmx])
                P.dve(lambda e: e.tensor_tensor(out=kmx[:], in0=kmx[:], in1=self.mk8[:], op=ALU.max),
                      r=[kmx, self.mk8], w=[kmx])
                P.dve(lambda e: e.tensor_scalar(out=kmxb[:], in0=kmx[:], scalar1=1.01, scalar2=None, op0=ALU.mult),
                      r=[kmx], w=[kmxb])
                P.pe(lambda e: e.transpose(out=pM[0:1, 0:64].bitcast(BF16), in_=kmxb[:, 0:1], identity=self.ident[:]),
                     r=[kmxb, self.ident], w=[pM])
                P.dve(lambda e: e.tensor_copy(out=kmr[:], in_=pM[0:1, 0:64].bitcast(BF16)), r=[pM], w=[kmr])
                P.dve(lambda e: e.tensor_reduce(out=km1[:], in_=kmr[:], axis=AX.X, op=ALU.max), r=[kmr], w=[km1])
                P.pe(lambda e: e.matmul(pM[:, 0:1], lhsT=onesr[:], rhs=km1[:], start=True, stop=True),
                     r=[onesr, km1], w=[pM])
                P.act(lambda e: e.activation(out=negC_s[:], in_=pM[:, 0:1], func=AF.Sqrt), r=[pM], w=[negC_s])
                P.dve(lambda e: e.scalar_tensor_tensor(out=negC_s[:], in0=negC_s[:], scalar=-1.0, in1=self.mq[:],
                                                       op0=ALU.mult, op1=ALU.mult), r=[negC_s, self.mq], w=[negC_s])
                negC = negC_s
            else:
                negC = self.negC_p
            for h in range(4):
                K = Kb[hc % 2]
                V = Vb[hc % 2]
                hc += 1
                if kind == "p":
                    nq_tiles = NPT // 4 if NPT >= 4 else 1
                    QW = min(512, T)
                    P.dma("sp", K[:, 0:T], self.KT[h, :, 0:T], r=[self.dKT], w=[K], sres=K)
                    P.dma("sp", V[:, 0:NKT, 0:128], self.Vs[h, :, 0:NKT, :], r=[self.dVs], w=[V], sres=V)
                    tok0 = 0
                else:
                    nq_tiles = 1
                    QW = DEC
                    tt = NPT + s
                    tok0 = tt * 128
                    P.dma("pool", Kc[:], self.cache_k[s, :, h * 128:(h + 1) * 128].rearrange("(kt p) d -> p kt d", p=128),
                          w=[Kc], sres=Kc)
                    for g4 in range(4):
                        for j in range(4):
                            kt = g4 * 4 + j
                            P.pe(lambda e, kt=kt, j=j: e.transpose(out=pTk[:, j, :], in_=Kc[:, kt, :], identity=self.ident[:]),
                                 r=[Kc, self.ident], w=[pTk])
                        P.dve(lambda e, K=K, g4=g4: e.tensor_copy(
                            out=K[:, g4 * 512:(g4 + 1) * 512], in_=pTk[:, 0:4, :].rearrange("p a b -> p (a b)")),
                            r=[pTk], w=[K])
                    P.dma("sp", K[:, PAST:PAST + DEC], self.KT[h, :, tok0:tok0 + DEC], r=[self.dKT], w=[K], sres=K)
                    P.dma("pool", V[:, 0:16, 0:128],
                          self.cache_v[s, :, h * 128:(h + 1) * 128].rearrange("(kt p) d -> p kt d", p=128), w=[V], sres=V)
                    P.dma("sp", V[0:DEC, 16, 0:128], self.Vs[h, 0:DEC, tt, :], r=[self.dVs], w=[V], sres=V)
                for qi in range(nq_tiles):
                    Q = Qt[qc % 2]
                    G = Gt[qc % 2]
                    qc += 1
                    q0 = qi * QW
                    nsub = (QW + 127) // 128
                    P.dma("sp", Q[:, 0:QW], self.QT[h, :, tok0 + q0:tok0 + q0 + QW], r=[self.dQT], w=[Q], sres=Q)
                    if kind == "p":
                        P.dma("sp", G[:, 0:nsub, :],
                              self.Gs[q0:q0 + QW, h * 128:(h + 1) * 128].rearrange("(a p) d -> p a d", p=128),
                              r=[self.dGs], w=[G], sres=G)
                    else:
                        P.dma("sp", G[0:DEC, 0, :], self.Gs[tok0:tok0 + DEC, h * 128:(h + 1) * 128],
                              r=[self.dGs], w=[G], sres=G)
                    if kind == "p":
                        blocks = [(j * 128, 128, j, 0, None) for j in range(q0 // 128)]
                        for j in range(nsub):
                            blocks.append((q0 + j * 128, 128, q0 // 128 + j, j, j))
                    else:
                        blocks = [(j * 128, 128, j, 0, None) for j in range(16)] + [(PAST, DEC, 16, 0, None)]
                    steps = [(c, bi) + blk for c in range(2) for bi, blk in enumerate(blocks)]

                    def emit_qk(step):
                        nonlocal sc, pc
                        c, bi, kc0, nk, vt, s0, dg = step
                        ps = pS[sc % 2]
                        sc += 1
                        pt = Pt[pc % 3]
                        pc += 1
                        qa = s0 * 128
                        P.pe(lambda e, ps=ps, K=K, Q=Q, c=c, kc0=kc0, nk=nk, qa=qa, QW=QW: e.matmul(
                            ps[0:nk, qa:QW], lhsT=K[64 * c:64 * c + 64, kc0:kc0 + nk], rhs=Q[64 * c:64 * c + 64, qa:QW],
                            start=True, stop=True), r=[K, Q], w=[ps])
                        P.act(lambda e, ps=ps, pt=pt, nk=nk, qa=qa, QW=QW, negC=negC: e.activation(
                            out=pt[0:nk, qa:QW], in_=ps[0:nk, qa:QW], func=AF.Exp, scale=0.125, bias=negC[0:nk, 0:1]),
                            r=[ps, negC], w=[pt])
                        if dg is not None:
                            P.pool(lambda e, pt=pt, dg=dg: e.memset(pt[64:128, dg * 128:dg * 128 + 64], 0.0), w=[pt])
                        return pt

                    def emit_pv(step, pt):
                        c, bi, kc0, nk, vt, s0, dg = step
                        for sub in range(s0, nsub):
                            nqs = min(128, QW - sub * 128)
                            po = pO[c][sub // 2]
                            first = (bi == 0) and (sub % 2 == 0)
                            last = (dg == sub) if kind == "p" else (bi == len(blocks) - 1)
                            P.pe(lambda e, po=po, pt=pt, V=V, sub=sub, nqs=nqs, nk=nk, vt=vt, first=first, last=last:
                                 e.matmul(po[0:nqs, sub % 2, 0:130], lhsT=pt[0:nk, sub * 128:sub * 128 + nqs],
                                          rhs=V[0:nk, vt, :], start=first, stop=last, skip_group_check=True), r=[pt, V], w=[po])
                    pend = None
                    for step in steps:
                        pt_new = emit_qk(step)
                        if pend is not None:
                            emit_pv(*pend)
                        pend = (step, pt_new)
                    emit_pv(*pend)
                    obt = ob[oc % 2]
                    oc += 1
                    for sub in range(nsub):
                        nqs = min(128, QW - sub * 128)
                        p0 = pO[0][sub // 2]
                        p1 = pO[1][sub // 2]
                        si = sub % 2
                        P.dve(lambda e, p0=p0, si=si, nqs=nqs: e.reciprocal(out=rl[0:nqs, 0:1], in_=p0[0:nqs, si, 128:129]),
                              r=[p0], w=[rl])
                        P.dve(lambda e, p1=p1, si=si, nqs=nqs: e.reciprocal(out=rl[0:nqs, 1:2], in_=p1[0:nqs, si, 128:129]),
                              r=[p1], w=[rl])
                        P.dve(lambda e, nqs=nqs: e.tensor_tensor(out=rl[0:nqs, 1:2], in0=rl[0:nqs, 1:2],
                                                                 in1=self.neglam[0:nqs, :], op=ALU.mult),
                              r=[rl, self.neglam], w=[rl])
                        P.dve(lambda e, p0=p0, si=si, nqs=nqs: e.tensor_scalar(
                            out=o1[0:nqs, :], in0=p0[0:nqs, si, 0:128], scalar1=rl[0:nqs, 0:1], scalar2=None, op0=ALU.mult),
                            r=[p0, rl], w=[o1])
                        P.dve(lambda e, p1=p1, si=si, nqs=nqs: e.scalar_tensor_tensor(
                            out=o2[0:nqs, :], in0=p1[0:nqs, si, 0:128], scalar=rl[0:nqs, 1:2], in1=o1[0:nqs, :],
                            op0=ALU.mult, op1=ALU.add), r=[p1, rl, o1], w=[o2])
                        P.act(lambda e, nqs=nqs: e.activation(out=jk[0:nqs, :], in_=o2[0:nqs, :], func=AF.Square,
                                                              accum_out=oss[0:nqs, :]), r=[o2], w=[jk, oss])
                        P.act(lambda e, nqs=nqs: e.activation(out=ors[0:nqs, :], in_=oss[0:nqs, :], func=AF.Sqrt,
                                                              scale=1.0 / 128, bias=self.epst[0:nqs, 0:1]),
                              r=[oss, self.epst], w=[ors])
                        P.dve(lambda e, nqs=nqs: e.reciprocal(out=ors[0:nqs, :], in_=ors[0:nqs, :]), r=[ors], w=[ors])
                        P.dve(lambda e, nqs=nqs: e.scalar_tensor_tensor(
                            out=o1[0:nqs, :], in0=o2[0:nqs, :], scalar=ors[0:nqs, 0:1], in1=self.sub_b[0:nqs, :],
                            op0=ALU.mult, op1=ALU.mult), r=[o2, ors, self.sub_b], w=[o1])
                        P.dve(lambda e, nqs=nqs, sub=sub, G=G, obt=obt: e.tensor_tensor(
                            out=obt[0:nqs, sub, :], in0=o1[0:nqs, :], in1=G[0:nqs, sub, :], op=ALU.mult),
                            r=[o1, G], w=[obt])
                    if kind == "p":
                        P.dma("pool", self.Mx[q0:q0 + QW, h * 128:(h + 1) * 128].rearrange("(a p) d -> p a d", p=128),
                              obt[:, 0:nsub, :], r=[obt], w=[self.dMx], sres=obt)
                    else:
                        P.dma("pool", self.Mx[tok0:tok0 + DEC, h * 128:(h + 1) * 128], obt[0:DEC, 0, :],
                              r=[obt], w=[self.dMx], sres=obt)
        self.end_phase()

    def phase_A2(self):
        P = self.P
        NTT, NPT = self.NTT, self.NPT
        self.begin_phase()
        self.alloc_xnorm(nx=1)
        C0 = math.exp(-0.5)
        W = P.sb("W2", [128, 16, 2176], BF16)
        for k in range(16):
            P.dma("pool", W[:, k, :], self.w_rw[k * 128:(k + 1) * 128, :], w=[W], sres=W)
        banks = [P.ps(f"bk{i}", [128, 512], F32) for i in range(6)]
        self.bkc = 0

        def nb():
            b = banks[self.bkc % len(banks)]
            self.bkc += 1
            return b
        bl = self.bcast_load
        mu_b = bl("mu_b", self.mu, 1664)
        w0_b = bl("w0_b", self.w0, 512); a0_b = bl("a0_b", self.a0, 512)
        kk_b = bl("kk_b", self.k_k, 512); ka_b = bl("ka_b", self.k_a, 512)
        rk_b = bl("rk_b", self.r_k, 512)
        lg_b = bl("lg_b", self.lnx_g, 512); lb_b = bl("lb_b", self.lnx_b, 512)
        omka = P.sb("omka", [128, 512], F32)
        P.dve(lambda e: e.tensor_scalar(out=omka[:], in0=ka_b[:], scalar1=-1.0, scalar2=1.0, op0=ALU.mult, op1=ALU.add),
              r=[ka_b], w=[omka])
        wup = P.sb("wup", [128, 512], BF16)
        P.dma("pool", wup[0:64, :], self.w_up, w=[wup], sres=wup)
        P.dma("pool", wup[64:128, :], self.a_up, w=[wup], sres=wup)
        onesc = P.sb("onesc", [128, 1], BF16)
        shi = P.sb("shi", [128, 512], BF16)
        slo = P.sb("slo", [128, 512], BF16)
        P.pool(lambda e: e.memset(onesc[:], 1.0), w=[onesc])
        tiny = P.sb("tiny", [128, 1], F32)
        identb3 = self.ident
        Pp = P.sb("Pp", [128, 1664], F32)
        prev = P.sb("prev", [128, 1664], F32)
        lastrow = P.sb("lastrow", [1, 1664], F32)
        gs = P.sb("gs", [128, 512], BF16)
        tl = P.sb("tl", [128, 128], BF16)
        tlT = P.sb("tlT", [128, 128], BF16)
        sigw = P.sb("sigw", [128, 512], F32)
        asig = P.sb("asig", [128, 512], F32)
        tA = P.sb("tA", [128, 512], F32)
        tB = P.sb("tB", [128, 512], F32)
        kkn = P.sb("kkn", [128, 512], F32)
        kmod = P.sb("kmod", [128, 512], F32)
        bvec = P.sb("bvec", [128, 512], F32)
        s8 = P.sb("s8", [128, 8], F32)
        bs8 = P.sb("bs8", [128, 8], F32)
        Ec = P.sb("Ec", [128, 512], F32); Ex = Ec
        En = P.sb("En", [128, 512], F32); Er = En
        rt = P.sb("rt", [128, 512], BF16); at = P.sb("at", [128, 512], BF16)
        kt = P.sb("kt", [128, 512], BF16); bt = P.sb("bt", [128, 512], BF16)
        kh = P.sb("kh", [128, 512], BF16); bh = P.sb("bh", [128, 512], BF16)
        vb = P.sb("vb", [128, 512], BF16)
        AR = P.sb("AR", [128, 4, 2, 128], BF16)
        BKz = P.sb("BKz", [128, 8, 2, 128], BF16)
        Hbz = P.sb("Hbz", [128, 8, 64], BF16)
        XL = P.sb("XL", [128, 8, 384], BF16)
        AK = P.sb("AK", [128, 8, 256], BF16)
        ML = [P.sb(f"ML{i}", [128, 8, 256], BF16) for i in range(2)]
        Q = [P.sb(f"Q{i}", [128, 8, 128], BF16) for i in range(2)]
        XLr = P.sub(XL, 8); AKr = P.sub(AK, 4)
        MLr = [P.sub(ML[0], 4), P.sub(ML[1], 4)]
        Qr = [P.sub(Q[0], 2), P.sub(Q[1], 2)]
        Wb = P.sb("Wb_", [128, 512], BF16)
        P.pool(lambda e: e.memset(BKz[:], 0.0), w=[BKz])
        P.pool(lambda e: e.memset(Hbz[:], 0.0), w=[Hbz])
        Uv = P.sb("Uv", [128, 512], F32)
        AhT = P.sb("AhT", [128, 4, 128], BF16)
        AhTo = P.sb("AhTo", [128, 4, 128], BF16)
        GC = P.sb("GC", [128, 4], F32)
        Hs = P.sb("Hs", [128, 4, 64], F32)
        Hb = P.sb("Hb", [128, 4, 64], BF16)
        Hlo = P.sb("Hlo", [128, 4, 64], BF16)
        Ub = P.sb("Ub", [128, 512], BF16)
        yf = tA
        ysq = tB
        m8 = P.sb("m8", [128, 8], F32); v8 = P.sb("v8", [128, 8], F32)
        yo = [Wb]
        S0v = Uv[0:64, :].rearrange("i (h j) -> i h j", h=8)
        Sov = Ec[0:64, :].rearrange("i (h j) -> i h j", h=8)
        mask3 = self.maskf
        mask2b = self.maskf[:, 0:256].unsqueeze(1).broadcast_to([128, 2, 256])
        P.pool(lambda e: e.memset(tiny[:], 1e-12), w=[tiny])

        def g8(t):
            return t[:, :].rearrange("p (g d) -> p g d", g=8)

        def b8(t):
            return t[:, :].unsqueeze(2).broadcast_to([128, 8, 64])

        def refresh_hbz(extra_r=()):
            P.act(lambda e: e.copy(out=Hbz[0:64, 0:8:2, :], in_=Hs[0:64, :, :]), r=[Hs] + list(extra_r), w=[Hbz])
            P.dve(lambda e: e.tensor_copy(out=Hbz[64:128, 1:8:2, :], in_=Hs[64:128, :, :]), r=[Hs] + list(extra_r), w=[Hbz])

        def store_state(dst, soi):
            P.act(lambda e: e.copy(out=Hb[:], in_=Hs[:]), r=[Hs], w=[Hb])
            P.dve(lambda e: e.tensor_tensor(out=Hlo[:], in0=Hs[:], in1=Hb[:], op=ALU.subtract), r=[Hs, Hb], w=[Hlo])
            bk = nb()
            bkb = bk[:, :].bitcast(BF16)
            for j, src in enumerate((Hb, Hlo)):
                for p in range(4):
                    P.pe(lambda e, bkb=bkb, p=p, j=j, src=src: e.transpose(
                        out=bkb[0:64, j * 512 + p * 128:j * 512 + (p + 1) * 128], in_=src[:, p, :], identity=self.ident[:]),
                        r=[src, self.ident], w=[bk])
            P.act(lambda e, bkb=bkb: e.copy(out=Ec[0:64, :], in_=bkb[0:64, 0:512]), r=[bk], w=[Ec])
            P.dve(lambda e, bkb=bkb: e.tensor_tensor(out=Ec[0:64, :], in0=Ec[0:64, :], in1=bkb[0:64, 512:1024], op=ALU.add),
                  r=[bk, Ec], w=[Ec])
            P.dma("pool", dst.rearrange("h i j -> i h j"), Sov, r=[Ec], w=[self.dOut], sres=Ec)

        yc = 0
        soi = 0
        A2STOP = CFG.get('a2stop', 99)
        for tt in CFG.get('tiles', range(NTT)):
            sample = tt >= NPT
            s = tt - NPT
            if tt == 0:
                P.pool(lambda e: e.memset(Hs[:], 0.0), w=[Hs])
                P.pool(lambda e: e.memset(lastrow[:], 0.0), w=[lastrow])
            if sample:
                P.dma("sp", S0v, self.wkv0[s].rearrange("h i j -> i h j"), w=[Uv], sres=Uv)
                P.act(lambda e: e.copy(out=shi[0:64, :], in_=Uv[0:64, :]), r=[Uv], w=[shi])
                P.dve(lambda e: e.tensor_tensor(out=slo[0:64, :], in0=Uv[0:64, :], in1=shi[0:64, :], op=ALU.subtract),
                      r=[Uv, shi], w=[slo])
                bk = nb()
                bkb = bk[:, :].bitcast(BF16)
                for j, src in enumerate((shi, slo)):
                    for p in range(4):
                        P.pe(lambda e, bkb=bkb, p=p, j=j, src=src: e.transpose(
                            out=bkb[:, j * 256 + p * 64:j * 256 + (p + 1) * 64], in_=src[0:64, p * 128:(p + 1) * 128],
                            identity=self.ident[0:64, 0:64]), r=[src, self.ident], w=[bk])
                P.act(lambda e, bkb=bkb: e.copy(out=Hs[:, :, :].rearrange("p a b -> p (a b)"), in_=bkb[:, 0:256]),
                      r=[bk], w=[Hs])
                P.dve(lambda e, bkb=bkb: e.tensor_tensor(out=Hs[:, :, :].rearrange("p a b -> p (a b)"),
                                                         in0=Hs[:, :, :].rearrange("p a b -> p (a b)"), in1=bkb[:, 256:512],
                                                         op=ALU.add), r=[bk, Hs], w=[Hs])
                refresh_hbz()
                P.dma("sp", lastrow[:], self.shift0[s:s + 1, :], w=[lastrow], sres=lastrow)
            XT = self.xnorm_tile(tt)
            for cb, (c0, cw) in enumerate(((0, 512), (512, 512), (1024, 512), (1536, 128), (1664, 512))):
                py = nb()
                for k in range(16):
                    P.pe(lambda e, py=py, k=k, c0=c0, cw=cw, XT=XT: e.matmul(
                        py[:, 0:cw], lhsT=XT[:, k, :], rhs=W[:, k, c0:c0 + cw], start=(k == 0), stop=(k == 15)),
                        r=[XT, W], w=[py])
                if cb < 4:
                    P.act(lambda e, py=py, c0=c0, cw=cw: e.copy(out=Pp[:, c0:c0 + cw], in_=py[:, 0:cw]), r=[py], w=[Pp])
                else:
                    P.act(lambda e, py=py: e.activation(out=gs[:], in_=py[:], func=AF.Silu), r=[py], w=[gs])
            if A2STOP <= 1:
                continue
            P.dma("sp", prev[1:128, :], Pp[0:127, :], r=[Pp], w=[prev], sres=prev)
            P.dma("sp", prev[0:1, :], lastrow[:], r=[lastrow], w=[prev], sres=prev)
            if not sample:
                if tt == NPT - 1:
                    P.dma("pool", self.shift_p.rearrange("(a c) -> a c", a=1), Pp[127:128, :], r=[Pp], w=[self.dOut], sres=Pp)
                else:
                    P.dma("sp", lastrow[:], Pp[127:128, :], r=[Pp], w=[lastrow], sres=lastrow)
            else:
                P.dma("pool", self.shift_s[s:s + 1, :], Pp[DEC - 1:DEC, :], r=[Pp], w=[self.dOut], sres=Pp)
            P.pool(lambda e: e.tensor_tensor(out=prev[:], in0=prev[:], in1=Pp[:], op=ALU.subtract), r=[prev, Pp], w=[prev])
            P.pool(lambda e: e.tensor_tensor(out=prev[:], in0=prev[:], in1=mu_b[:], op=ALU.mult), r=[prev, mu_b], w=[prev])
            P.pool(lambda e: e.tensor_tensor(out=prev[:], in0=prev[:], in1=Pp[:], op=ALU.add), r=[prev, Pp], w=[prev])
            xr = prev[:, 0:512]; xk = prev[:, 512:1024]; xv = prev[:, 1024:1536]
            if A2STOP <= 2:
                continue
            P.act(lambda e: e.activation(out=tl[:, 0:64], in_=prev[:, 1536:1600], func=AF.Tanh), r=[prev], w=[tl])
            P.act(lambda e: e.copy(out=tl[:, 64:128], in_=prev[:, 1600:1664]), r=[prev], w=[tl])
            bk = nb()
            P.pe(lambda e, bk=bk: e.transpose(out=bk[:, 0:64].bitcast(BF16), in_=tl[:], identity=self.ident[:]),
                 r=[tl, self.ident], w=[bk])
            P.act(lambda e, bk=bk: e.copy(out=tlT[:], in_=bk[:, 0:64].bitcast(BF16)), r=[bk], w=[tlT])
            bz = nb()
            P.pe(lambda e, bz=bz: e.matmul(bz[:], lhsT=tlT[0:64, :], rhs=wup[0:64, :], start=True, stop=True),
                 r=[tlT, wup], w=[bz])
            P.dve(lambda e, bz=bz: e.tensor_tensor(out=tA[:], in0=bz[:], in1=w0_b[:], op=ALU.add), r=[bz, w0_b], w=[tA])
            P.act(lambda e: e.activation(out=sigw[:], in_=tA[:], func=AF.Sigmoid), r=[tA], w=[sigw])
            bz2 = nb()
            P.pe(lambda e, bz2=bz2: e.matmul(bz2[:], lhsT=tlT[64:128, :], rhs=wup[64:128, :], start=True, stop=True),
                 r=[tlT, wup], w=[bz2])
            P.dve(lambda e, bz2=bz2: e.tensor_tensor(out=tB[:], in0=bz2[:], in1=a0_b[:], op=ALU.add), r=[bz2, a0_b], w=[tB])
            P.act(lambda e: e.activation(out=asig[:], in_=tB[:], func=AF.Sigmoid), r=[tB], w=[asig])
            if sample:
                P.pool(lambda e: e.memset(sigw[32:64, :], 0.0), w=[sigw])
                P.pool(lambda e: e.memset(sigw[64:128, :], 0.0), w=[sigw])
            if A2STOP <= 3:
                continue
            P.act(lambda e: e.copy(out=shi[:], in_=sigw[:]), r=[sigw], w=[shi])
            P.dve(lambda e: e.tensor_tensor(out=slo[:], in0=sigw[:], in1=shi[:], op=ALU.subtract), r=[sigw, shi], w=[slo])
            bc = nb(); bx = nb(); br = nb()
            for bnk, mc in ((bc, 128), (bx, 0), (br, 256)):
                P.pe(lambda e, bnk=bnk, mc=mc: e.matmul(bnk[:], lhsT=self.maskb[:, mc:mc + 128], rhs=shi[:], start=True, stop=False),
                     r=[self.maskb, shi], w=[bnk])
                P.pe(lambda e, bnk=bnk, mc=mc: e.matmul(bnk[:], lhsT=self.maskb[:, mc:mc + 128], rhs=slo[:], start=False, stop=True),
                     r=[self.maskb, slo], w=[bnk])
            P.act(lambda e, bc=bc: e.activation(out=Ec[:], in_=bc[:], func=AF.Exp, scale=-C0), r=[bc], w=[Ec])
            P.act(lambda e, bc=bc: e.activation(out=En[:], in_=bc[:], func=AF.Exp, scale=C0), r=[bc], w=[En])
            bg = nb()
            for p in range(4):
                P.pe(lambda e, bg=bg, p=p: e.matmul(bg[:, p:p + 1], lhsT=shi[:, p * 128:(p + 1) * 128], rhs=onesc[:],
                                                    start=(p == 0), stop=False, skip_group_check=True), r=[shi, onesc], w=[bg])
                P.pe(lambda e, bg=bg, p=p: e.matmul(bg[:, p:p + 1], lhsT=slo[:, p * 128:(p + 1) * 128], rhs=onesc[:],
                                                    start=False, stop=True, skip_group_check=True), r=[slo, onesc], w=[bg])
            P.act(lambda e, bg=bg: e.activation(out=GC[:], in_=bg[:, 0:4], func=AF.Exp, scale=-C0), r=[bg], w=[GC])
            if A2STOP <= 4:
                continue
            P.dve(lambda e: e.tensor_tensor(out=tA[:], in0=xk, in1=kk_b[:], op=ALU.mult), r=[prev, kk_b], w=[tA])
            P.act(lambda e: e.activation(out=tB[:], in_=tA[:], func=AF.Square), r=[tA], w=[tB])
            P.dve(lambda e: e.reduce_sum(out=s8[:], in_=g8(tB), axis=AX.X), r=[tB], w=[s8])
            P.act(lambda e: e.activation(out=s8[:], in_=s8[:], func=AF.Sqrt), r=[s8], w=[s8])
            P.dve(lambda e: e.tensor_scalar(out=s8[:], in0=s8[:], scalar1=tiny[:, 0:1], scalar2=None, op0=ALU.max),
                  r=[s8, tiny], w=[s8])
            P.dve(lambda e: e.reciprocal(out=s8[:], in_=s8[:]), r=[s8], w=[s8])
            P.dve(lambda e: e.tensor_tensor(out=g8(kkn), in0=g8(tA), in1=b8(s8), op=ALU.mult), r=[tA, s8], w=[kkn])
            P.pool(lambda e: e.tensor_tensor(out=tB[:], in0=asig[:], in1=ka_b[:], op=ALU.mult), r=[asig, ka_b], w=[tB])
            P.pool(lambda e: e.tensor_tensor(out=tB[:], in0=tB[:], in1=omka[:], op=ALU.add), r=[tB, omka], w=[tB])
            P.pool(lambda e: e.tensor_tensor(out=kmod[:], in0=xk, in1=tB[:], op=ALU.mult), r=[prev, tB], w=[kmod])
            P.dve(lambda e: e.tensor_tensor(out=bvec[:], in0=kkn[:], in1=asig[:], op=ALU.mult), r=[kkn, asig], w=[bvec])
            P.pool(lambda e: e.tensor_tensor(out=tA[:], in0=xr, in1=kmod[:], op=ALU.mult), r=[prev, kmod, kkn], w=[tA])
            P.pool(lambda e: e.tensor_tensor(out=tA[:], in0=tA[:], in1=rk_b[:], op=ALU.mult), r=[tA, rk_b], w=[tA])
            P.dve(lambda e: e.reduce_sum(out=bs8[:], in_=g8(tA), axis=AX.X), r=[tA], w=[bs8])
            P.dve(lambda e: e.tensor_tensor(out=rt[:], in0=xr, in1=Ec[:], op=ALU.mult), r=[prev, Ec], w=[rt])
            P.act(lambda e, bx=bx: e.activation(out=Ex[:], in_=bx[:], func=AF.Exp, scale=-C0), r=[bx], w=[Ex])
            P.dve(lambda e: e.scalar_tensor_tensor(out=at[:], in0=kkn[:], scalar=-1.0, in1=Ex[:], op0=ALU.mult, op1=ALU.mult),
                  r=[kkn, Ex], w=[at])
            P.dve(lambda e: e.tensor_tensor(out=kt[:], in0=kmod[:], in1=En[:], op=ALU.mult), r=[kmod, En], w=[kt])
            P.pool(lambda e: e.tensor_tensor(out=bt[:], in0=bvec[:], in1=En[:], op=ALU.mult), r=[bvec, En], w=[bt])
            P.act(lambda e, br=br: e.activation(out=Er[:], in_=br[:], func=AF.Exp, scale=-C0), r=[br], w=[Er])
            P.pool(lambda e: e.tensor_tensor(out=kh[:], in0=kmod[:], in1=Er[:], op=ALU.mult), r=[kmod, Er], w=[kh])
            P.pool(lambda e: e.tensor_tensor(out=bh[:], in0=bvec[:], in1=Er[:], op=ALU.mult), r=[bvec, Er], w=[bh])
            P.act(lambda e: e.copy(out=vb[:], in_=xv), r=[prev], w=[vb])
            if sample:
                for t_ in (kt, bt, kh, bh):
                    P.pool(lambda e, t_=t_: e.memset(t_[32:64, :], 0.0), w=[t_])
                    P.pool(lambda e, t_=t_: e.memset(t_[64:128, :], 0.0), w=[t_])
            if A2STOP <= 5:
                continue
            for dst, (o0, o1) in ((AR, (at, rt)), (BKz, (bt, kt))):
                bk = nb()
                bkb = bk[:, :].bitcast(BF16).rearrange("p (a b c) -> p a b c", a=4, b=2)
                for p in range(4):
                    for j, o in enumerate((o0, o1)):
                        P.pe(lambda e, bkb=bkb, p=p, j=j, o=o: e.transpose(
                            out=bkb[:, p, j, :], in_=o[:, p * 128:(p + 1) * 128], identity=self.ident[:]),
                            r=[o, self.ident], w=[bk])
                if dst is AR:
                    P.act(lambda e, dst=dst, bkb=bkb: e.copy(out=dst[:], in_=bkb), r=[bk], w=[dst])
                else:
                    P.act(lambda e, bkb=bkb: e.copy(out=BKz[0:64, 0:8:2, :, :], in_=bkb[0:64, :, :, :]), r=[bk], w=[BKz])
                    P.act(lambda e, bkb=bkb: e.copy(out=BKz[64:128, 1:8:2, :, :], in_=bkb[64:128, :, :, :]), r=[bk], w=[BKz])
            if A2STOP <= 6:
                continue
            for h in range(8):
                p, hp = h // 2, h % 2
                rows = slice(64 * hp, 64 * hp + 64)
                ba = nb()
                P.pe(lambda e, ba=ba, p=p, h=h: e.matmul(
                    ba[:, 0:256], lhsT=BKz[:, h, 0, :], rhs=AR[:, p, :, :].rearrange("k a t -> k (a t)"),
                    start=True, stop=True), r=[BKz, AR], w=[ba])
                P.pe(lambda e, ba=ba, p=p, h=h: e.matmul(
                    ba[:, 256:384], lhsT=AR[:, p, 0, :], rhs=BKz[:, h, 0, :], start=True, stop=True),
                    r=[BKz, AR], w=[ba])
                P.dve(lambda e, ba=ba, h=h: e.tensor_tensor(out=XL[:, h, :], in0=ba[:, 0:384], in1=mask3[:, :], op=ALU.mult),
                      r=[ba, mask3], w=[XLr[h]])
                if hp == 0:
                    bb = nb()
                P.pe(lambda e, bb=bb, p=p, h=h, hp=hp: e.matmul(
                    bb[:, hp * 256:(hp + 1) * 256], lhsT=BKz[:, h, 1, :],
                    rhs=AR[:, p, :, :].rearrange("k a t -> k (a t)"), start=True, stop=True), r=[BKz, AR], w=[bb])
                if hp == 1:
                    P.dve(lambda e, bb=bb, h=h: e.tensor_tensor(
                        out=AK[:, h - 1:h + 1, :], in0=bb[:, :].rearrange("p (a c) -> p a c", a=2), in1=mask2b, op=ALU.mult),
                        r=[bb, mask3], w=[AKr[p]])
            if A2STOP <= 7:
                continue
            P.dve(lambda e: e.tensor_tensor(out=Q[0][:], in0=XL[:, :, 0:128],
                                            in1=self.ident[:, :].unsqueeze(1).broadcast_to([128, 8, 128]), op=ALU.add),
                  r=XLr + [self.ident], w=Qr[0])
            qi = 0
            INV = CFG.get("invstop", 99)
            for lev in range(1, 7):
                if INV <= 0 or lev > CFG.get("invlev", 6):
                    break
                mlo = ML[lev % 2]
                mli = ML[(lev - 1) % 2]

                def Mprev(h):
                    return XL[:, h, 0:128] if lev == 1 else mli[:, h, 0:128]

                def Lprev(h):
                    return XL[:, h, 256:384] if lev == 1 else mli[:, h, 128:256]
                srcr = (lambda h: XLr[h]) if lev == 1 else (lambda h, lv=lev: MLr[(lv - 1) % 2][h // 2])
                mlor = MLr[lev % 2]
                if lev < 6:
                    for h2 in range(4):
                        bm = nb()
                        for j in range(2):
                            h = 2 * h2 + j
                            P.pe(lambda e, bm=bm, j=j, h=h, Mp=Mprev(h), Lp=Lprev(h): e.matmul(
                                bm[:, j * 256:j * 256 + 128], lhsT=Lp, rhs=Mp, start=True, stop=True), r=[srcr(h)], w=[bm])
                            P.pe(lambda e, bm=bm, j=j, h=h, Mp=Mprev(h), Lp=Lprev(h): e.matmul(
                                bm[:, j * 256 + 128:j * 256 + 256], lhsT=Mp, rhs=Lp, start=True, stop=True), r=[srcr(h)], w=[bm])
                        P.act(lambda e, bm=bm, h2=h2, mlo=mlo: e.copy(
                            out=mlo[:, 2 * h2:2 * h2 + 2, :].rearrange("p a c -> p (a c)"), in_=bm[:]), r=[bm], w=[mlor[h2]])
                    Lcur = lambda h: mlo[:, h, 128:256]
                else:
                    for h4 in range(2):
                        bm = nb()
                        for j in range(4):
                            h = 4 * h4 + j
                            P.pe(lambda e, bm=bm, j=j, Mp=Mprev(h), Lp=Lprev(h): e.matmul(
                                bm[:, j * 128:(j + 1) * 128], lhsT=Mp, rhs=Lp, start=True, stop=True), r=[srcr(h)], w=[bm])
                        P.act(lambda e, bm=bm, h4=h4, mlo=mlo: e.copy(
                            out=mlo[:, 4 * h4:4 * h4 + 4, 0:128], in_=bm[:, :].rearrange("p (a c) -> p a c", a=4)),
                            r=[bm], w=[mlor[2 * h4], mlor[2 * h4 + 1]])
                    Lcur = lambda h: mlo[:, h, 0:128]
                if INV <= 1:
                    continue
                qo, qn = Q[qi % 2], Q[(qi + 1) % 2]
                qor, qnr = Qr[qi % 2], Qr[(qi + 1) % 2]
                qi += 1
                for h4 in range(2):
                    bq = nb()
                    for j in range(4):
                        h = 4 * h4 + j
                        P.pe(lambda e, bq=bq, j=j, h=h, Lc=Lcur(h), qo=qo: e.matmul(
                            bq[:, j * 128:(j + 1) * 128], lhsT=Lc, rhs=qo[:, h, :], start=True, stop=True),
                            r=[mlor[h // 2], qor[h4]], w=[bq])
                    P.dve(lambda e, bq=bq, h4=h4, qo=qo, qn=qn: e.tensor_tensor(
                        out=qn[:, 4 * h4:4 * h4 + 4, :].rearrange("p a c -> p (a c)"), in0=bq[:],
                        in1=qo[:, 4 * h4:4 * h4 + 4, :].rearrange("p a c -> p (a c)"), op=ALU.add), r=[bq, qor[h4]], w=[qnr[h4]])
            PT = Q[qi % 2]
            PTr = Qr[qi % 2]
            if A2STOP <= 8:
                continue
            bw = nb()
            for h in range(8):
                P.pe(lambda e, bw=bw, h=h: e.matmul(bw[:, h * 64:(h + 1) * 64], lhsT=AK[:, h, 0:128],
                                                    rhs=vb[:, h * 64:(h + 1) * 64], start=True, stop=True),
                     r=[AKr[h // 2], vb], w=[bw])
            P.act(lambda e, bw=bw: e.copy(out=Wb[:], in_=bw[:]), r=[bw], w=[Wb])
            if CFG.get("s9", 99) <= 1:
                continue
            bu = nb()
            for h in range(8):
                P.pe(lambda e, bu=bu, h=h, PT=PT: e.matmul(bu[:, h * 64:(h + 1) * 64], lhsT=PT[:, h, :],
                                                           rhs=Wb[:, h * 64:(h + 1) * 64], start=True, stop=True),
                     r=[PTr[h // 4], Wb], w=[bu])
            P.act(lambda e, bu=bu: e.copy(out=Uv[:], in_=bu[:]), r=[bu], w=[Uv])
            if CFG.get("s9", 99) <= 2:
                continue
            bhe = nb(); bho = nb()
            for h in range(8):
                p, hp = h // 2, h % 2
                bb_ = bhe if hp == 0 else bho
                P.pe(lambda e, bb_=bb_, h=h, p=p, PT=PT: e.matmul(
                    bb_[:, p * 128:(p + 1) * 128], lhsT=(kt if CFG.get("va") else at)[:, p * 128:(p + 1) * 128], rhs=PT[:, h, :],
                    start=True, stop=True), r=[at, PTr[h // 4]], w=[bb_])
            P.act(lambda e, bhe=bhe: e.copy(out=AhT[:, :, :].rearrange("p a t -> p (a t)"), in_=bhe[:, :]),
                  r=[bhe], w=[AhT])
            P.act(lambda e, bho=bho: e.copy(out=AhTo[:, :, :].rearrange("p a t -> p (a t)"), in_=bho[:, :]),
                  r=[bho], w=[AhTo])
            if A2STOP <= 9:
                continue
            bU = nb()
            for h in range(8):
                p, hp = h // 2, h % 2
                rows = slice(64 * hp, 64 * hp + 64)
                Ah_ = AhT if hp == 0 else AhTo
                P.pe(lambda e, bU=bU, h=h, p=p, Ah_=Ah_: e.matmul(
                    bU[:, h * 64:(h + 1) * 64], lhsT=Ah_[:, p, :], rhs=Hbz[:, h, :], start=True, stop=True),
                    r=[Ah_, Hbz], w=[bU])
            P.dve(lambda e, bU=bU: e.tensor_tensor(out=Ub[:], in0=bU[:], in1=Uv[:], op=ALU.add), r=[bU, Uv], w=[Ub])
            bY = nb()
            for h in range(8):
                p, hp = h // 2, h % 2
                rows = slice(64 * hp, 64 * hp + 64)
                cs = slice(h * 64, (h + 1) * 64)
                P.pe(lambda e, bY=bY, h=h, p=p, rows=rows, cs=cs: e.matmul(
                    bY[:, cs], lhsT=AR[:, p, 1, :], rhs=Hbz[:, h, :], start=(h == 0), stop=False, skip_group_check=True),
                    r=[AR, Hbz], w=[bY])
                P.pe(lambda e, bY=bY, h=h, cs=cs: e.matmul(
                    bY[:, cs], lhsT=XL[:, h, 128:256], rhs=Ub[:, cs], start=False, stop=False, skip_group_check=True),
                    r=[XLr[h], Ub], w=[bY])
                P.pe(lambda e, bY=bY, h=h, cs=cs: e.matmul(
                    bY[:, cs], lhsT=AK[:, h, 128:256], rhs=vb[:, cs], start=False, stop=True, skip_group_check=True),
                    r=[AKr[h // 2], vb], w=[bY])
            if A2STOP <= 10:
                continue
            bHe = nb(); bHo = nb()
            for h in range(8):
                p, hp = h // 2, h % 2
                cs = slice(h * 64, (h + 1) * 64)
                pc = slice(p * 128, (p + 1) * 128)
                bb_ = bHe if hp == 0 else bHo
                o_ = bb_[:, p * 64:(p + 1) * 64]
                P.pe(lambda e, o_=o_, cs=cs, pc=pc, h=h: e.matmul(o_, lhsT=kh[:, pc], rhs=vb[:, cs], start=(h < 2), stop=False,
                                                                  skip_group_check=True), r=[kh, vb], w=[bb_])
                P.pe(lambda e, o_=o_, cs=cs, pc=pc: e.matmul(o_, lhsT=bh[:, pc], rhs=Ub[:, cs], start=False, stop=True,
                                                             skip_group_check=True), r=[bh, Ub], w=[bb_])
            P.dve(lambda e: e.tensor_tensor(out=Hs[:], in0=Hs[:], in1=GC[:, :].unsqueeze(2).broadcast_to([128, 4, 64]),
                                            op=ALU.mult), r=[Hs, GC], w=[Hs])
            P.dve(lambda e, bHe=bHe: e.tensor_tensor(out=Hs[0:64, :, :], in0=Hs[0:64, :, :],
                                                     in1=bHe[0:64, 0:256].rearrange("p (a b) -> p a b", a=4), op=ALU.add),
                  r=[Hs, bHe, bY], w=[Hs])
            P.dve(lambda e, bHo=bHo: e.tensor_tensor(out=Hs[64:128, :, :], in0=Hs[64:128, :, :],
                                                     in1=bHo[64:128, 0:256].rearrange("p (a b) -> p a b", a=4), op=ALU.add),
                  r=[Hs, bHo], w=[Hs])
            refresh_hbz()
            if A2STOP <= 11:
                continue
            P.act(lambda e, bY=bY: e.copy(out=yf[:], in_=bY[:]), r=[bY], w=[yf])
            P.dve(lambda e: e.reduce_sum(out=m8[:], in_=g8(yf), axis=AX.X), r=[yf], w=[m8])
            P.act(lambda e: e.activation(out=ysq[:], in_=yf[:], func=AF.Square), r=[yf], w=[ysq])
            P.dve(lambda e: e.reduce_sum(out=v8[:], in_=g8(ysq), axis=AX.X), r=[ysq], w=[v8])
            P.dve(lambda e: e.tensor_scalar(out=m8[:], in0=m8[:], scalar1=1.0 / 64, scalar2=None, op0=ALU.mult), r=[m8], w=[m8])
            P.dve(lambda e: e.tensor_tensor(out=s8[:], in0=m8[:], in1=m8[:], op=ALU.mult), r=[m8], w=[s8])
            P.dve(lambda e: e.scalar_tensor_tensor(out=v8[:], in0=v8[:], scalar=1.0 / 64, in1=s8[:], op0=ALU.mult,
                                                   op1=ALU.subtract), r=[v8, s8], w=[v8])
            P.act(lambda e: e.activation(out=v8[:], in_=v8[:], func=AF.Sqrt, bias=self.lnxeps[:, 0:1]),
                  r=[v8, self.lnxeps], w=[v8])
            P.dve(lambda e: e.reciprocal(out=v8[:], in_=v8[:]), r=[v8], w=[v8])
            P.dve(lambda e: e.tensor_tensor(out=g8(yf), in0=g8(yf), in1=b8(m8), op=ALU.subtract), r=[yf, m8], w=[yf])
            P.dve(lambda e: e.tensor_tensor(out=g8(yf), in0=g8(yf), in1=b8(v8), op=ALU.mult), r=[yf, v8], w=[yf])
            P.pool(lambda e: e.tensor_tensor(out=yf[:], in0=yf[:], in1=lg_b[:], op=ALU.mult), r=[yf, lg_b], w=[yf])
            P.pool(lambda e: e.tensor_tensor(out=yf[:], in0=yf[:], in1=lb_b[:], op=ALU.add), r=[yf, lb_b], w=[yf])
            P.pool(lambda e: e.tensor_tensor(out=g8(ysq), in0=g8(prev[:, 1024:1536]), in1=b8(bs8), op=ALU.mult),
                   r=[prev, bs8, ysq], w=[ysq])
            P.pool(lambda e: e.tensor_tensor(out=yf[:], in0=yf[:], in1=ysq[:], op=ALU.add), r=[yf, ysq], w=[yf])
            yot = yo[0]
            yc += 1
            P.dve(lambda e, yot=yot: e.tensor_tensor(out=yot[:], in0=yf[:], in1=gs[:], op=ALU.mult), r=[yf, gs], w=[yot])
            nr = DEC if sample else 128
            P.dma("pool", self.Mx[tt * 128:tt * 128 + nr, 512:1024], yot[0:nr, :], r=[yot], w=[self.dMx], sres=yot)
            if A2STOP <= 12:
                continue
            if tt == NPT - 1:
                store_state(self.wkv_p, soi); soi += 1
            if sample:
                store_state(self.wkv_s[s], soi); soi += 1
        self.end_phase()

    def phase_C(self):
        P = self.P
        P.flush()
        sem = P.sem("cc")
        CH = self.CCH
        n = 0
        for ci, t0 in enumerate(range(0, self.NTT, CH)):
            nt = min(CH, self.NTT - t0)
            ins = self.nc.gpsimd.collective_compute(
                "AllGather", ALU.bypass, replica_groups=[[0, 1], [2, 3], [4, 5], [6, 7]],
                ins=[self.Mx[t0 * 128:(t0 + nt) * 128, :]], outs=[self.Mall[ci][0:2 * nt * 128, :]])
            ins.then_inc(sem, 1)
            n += 1
        for eng in P.engs.values():
            eng.wait_ge(sem, n)

    def phase_D(self):
        P = self.P
        NTT, NPT = self.NTT, self.NPT
        R = NTT * 128
        self.begin_phase()
        Wo = P.sb("Wo", [128, 16, 1024], BF16)
        for k in range(16):
            P.dma("pool", Wo[:, k, :], self.w_out[k * 128:(k + 1) * 128, :], w=[Wo], sres=Wo)
        Mt = [P.sb(f"Mt{i}", [128, 2, 1024], BF16) for i in range(2)]
        MT = [P.sb(f"MT{i}", [128, 16, 128], BF16) for i in range(2)]
        xr = [P.sb(f"xr{i}", [128, 1024], F32) for i in range(2)]
        yo = [P.sb(f"yo{i}", [128, 1024], F32) for i in range(2)]
        pT = [P.ps(f"dpT{i}", [128, 8, 128], BF16) for i in range(2)]
        pY = [P.ps(f"dpY{i}", [128, 512], F32) for i in range(2)]
        yc = 0
        for tt in range(NTT):
            sample = tt >= NPT
            nr = DEC if sample else 128
            M = Mt[tt % 2]; T_ = MT[tt % 2]; X = xr[tt % 2]; Y = yo[tt % 2]
            if sample:
                P.pool(lambda e, M=M: e.memset(M[:], 0.0), w=[M])
            ci, tl_ = tt // self.CCH, tt % self.CCH
            ntc = min(self.CCH, NTT - ci * self.CCH)
            for rk in range(2):
                r0 = rk * ntc * 128 + tl_ * 128
                P.dma("sp", M[0:nr, rk, :], self.Mall[ci][r0:r0 + nr, :], r=[self.dMall], w=[M], sres=M)
            if sample:
                P.dma("sp", X[0:nr, :], self.xres_s[tt - NPT], w=[X], sres=X)
            else:
                P.dma("sp", X[:], self.xres_p[tt * 128:(tt + 1) * 128, :], w=[X], sres=X)
            for hf in range(2):
                pt = pT[hf]
                for j in range(8):
                    k = hf * 8 + j
                    P.pe(lambda e, pt=pt, j=j, k=k, M=M: e.transpose(
                        out=pt[:, j, :], in_=M[:, k // 8, (k % 8) * 128:(k % 8 + 1) * 128], identity=self.ident[:]),
                        r=[M, self.ident], w=[pt])
                if hf == 0:
                    P.act(lambda e, pt=pt, T_=T_: e.copy(out=T_[:, 0:8, :], in_=pt[:]), r=[pt], w=[T_])
                else:
                    P.dve(lambda e, pt=pt, T_=T_: e.tensor_copy(out=T_[:, 8:16, :], in_=pt[:]), r=[pt], w=[T_])
            for cb in range(2):
                py = pY[yc % 2]
                yc += 1
                for k in range(16):
                    P.pe(lambda e, py=py, k=k, cb=cb, T_=T_: e.matmul(
                        py[:], lhsT=T_[:, k, :], rhs=Wo[:, k, cb * 512:(cb + 1) * 512], start=(k == 0), stop=(k == 15)),
                        r=[T_, Wo], w=[py])
                P.dve(lambda e, py=py, cb=cb, X=X, Y=Y, nr=nr: e.tensor_tensor(
                    out=Y[0:nr, cb * 512:(cb + 1) * 512], in0=py[0:nr, :], in1=X[0:nr, cb * 512:(cb + 1) * 512], op=ALU.add),
                    r=[py, X], w=[Y])
            if sample:
                P.dma("pool", self.y_s[tt - NPT], Y[0:nr, :], r=[Y], w=[self.dOut], sres=Y)
            else:
                P.dma("pool", self.y_p[tt * 128:(tt + 1) * 128, :], Y[:], r=[Y], w=[self.dOut], sres=Y)
        self.end_phase()

    def phase_DBG(self):
        P = self.P
        t = P.sb("dbgt", [128, 1024], BF16)
        for tt in range(self.NTT):
            P.dma("sp", t[:], self.Mx[tt * 128:(tt + 1) * 128, :], r=[self.dMx], w=[t], sres=t)
            P.dma("sp", self.Mx_dbg[tt * 128:(tt + 1) * 128, :], t[:], r=[t], w=[self.dOut], sres=t)
        P.flush()

    def build(self):
        P = self.P
        self.setup()
        P.flush()
        phases = CFG["phases"].split(",")
        for ph in phases:
            if hasattr(self, "phase_" + ph):
                getattr(self, "phase_" + ph)()
        P.flush()
        self.stats = P.stats
        return self.nc


def _core_inputs(c, inp, NPT):
    b, hh = c // 2, c % 2
    T = NPT * 128
    f = lambda a: np.ascontiguousarray(a, dtype=np.float32)
    w_in = inp["w_in"][0]
    a = slice(hh * 512, hh * 512 + 512)
    att_cols = np.concatenate([np.arange(i * 1024 + hh * 512, i * 1024 + hh * 512 + 512) for i in range(4)])
    pb = 4096
    rw_cols = np.concatenate([np.arange(pb + i * 1024 + hh * 512, pb + i * 1024 + hh * 512 + 512) for i in range(3)]
                             + [np.arange(pb + 3072, pb + 3200)]
                             + [np.arange(pb + 3200 + hh * 512, pb + 3200 + hh * 512 + 512)])
    sh_cols = np.concatenate([np.arange(i * 1024 + hh * 512, i * 1024 + hh * 512 + 512) for i in range(3)]
                             + [np.arange(3072, 3200)])
    w_out = inp["w_out"][0]
    rows = np.concatenate([np.arange(0, 512), np.arange(1024, 1536), np.arange(512, 1024), np.arange(1536, 2048)])
    oc = slice(hh * 1024, hh * 1024 + 1024)
    sb = slice(4 * b, 4 * b + 4)
    d = {
        "xp": f(inp["x_prompt"][b, :T]),
        "xs": f(inp["x_sample"][sb]),
        "w_att": f(w_in[:, att_cols]),
        "w_rw": f(w_in[:, rw_cols]),
        "w_out": f(w_out[rows][:, oc]),
        "xres_p": f(inp["x_prompt"][b, :T, oc]),
        "xres_s": f(inp["x_sample"][sb, :, oc]),
        "norm_g": f(inp["norm_g"][0]),
        "qg": f(inp["q_norm_g"][0]), "kg": f(inp["k_norm_g"][0]),
        "lq1": f(inp["lambda_q1"][0]), "lk1": f(inp["lambda_k1"][0]),
        "lq2": f(inp["lambda_q2"][0]), "lk2": f(inp["lambda_k2"][0]),
        "subln": f(inp["subln_g"][0]),
        "mu": f(inp["shift_mu"][0][sh_cols]),
        "w0": f(inp["w0"][0][a]), "a0": f(inp["a0"][0][a]),
        "w_up": f(inp["w_up"][0][:, a]), "a_up": f(inp["a_up"][0][:, a]),
        "k_k": f(inp["k_k"][0][a]), "k_a": f(inp["k_a"][0][a]),
        "r_k": f(inp["r_k"][0][8 * hh:8 * hh + 8].reshape(512)),
        "lnx_g": f(inp["lnx_g"][0][a]), "lnx_b": f(inp["lnx_b"][0][a]),
        "cache_k": f(inp["cache_attn_k"][0, sb, :, 4 * hh:4 * hh + 4].reshape(4, PAST, 512)),
        "cache_v": f(inp["cache_attn_v"][0, sb, :, 4 * hh:4 * hh + 4].reshape(4, PAST, 512)),
        "wkv0": f(inp["state_rwkv_wkv"][0, sb, 8 * hh:8 * hh + 8]),
        "shift0": f(inp["state_rwkv_shift"][0, sb, 0][:, sh_cols]),
    }
    return d, sh_cols


_CACHE = {}


def run_device(inputs, NPT=None):
    NPT = CFG["NPT"] if NPT is None else NPT
    CFG["NPT"] = NPT
    key = (NPT, CFG["debug"], CFG["phases"])
    if key not in _CACHE:
        bld = Builder()
        nc = bld.build()
        _CACHE[key] = (nc, bld.stats)
    nc, stats = _CACHE[key]
    if CFG.get("debug"):
        print("CFG", {k: (v if not isinstance(v, (list, range)) else list(v)) for k, v in CFG.items()}, stats, flush=True)
    inp = {k: np.asarray(v) for k, v in inputs.items()}
    in_maps = []
    sh_cols = None
    for c in range(8):
        d, sh_cols = _core_inputs(c, inp, NPT)
        in_maps.append(d)
    ncores = CFG.get("ncores", 8)
    res = run_bass_kernel_spmd(nc, in_maps[:ncores], core_ids=list(range(ncores)))
    return res.results, sh_cols


def kernel(**inputs):
    NPT = CFG["NPT"]
    T = NPT * 128
    R, sh_cols = run_device(inputs, NPT)
    B = 4
    y_p = np.zeros((B, T, 2048), np.float32)
    y_s = np.zeros((16, DEC, 2048), np.float32)
    k_p = np.zeros((1, B, T, 8, 2, 64), np.float32)
    v_p = np.zeros((1, B, T, 8, 128), np.float32)
    wkv_p = np.zeros((1, B, 16, 64, 64), np.float32)
    sh_p = np.zeros((1, B, 1, 3200), np.float32)
    k_s = np.zeros((1, 16, DEC, 8, 2, 64), np.float32)
    v_s = np.zeros((1, 16, DEC, 8, 128), np.float32)
    wkv_s = np.zeros((1, 16, 16, 64, 64), np.float32)
    sh_s = np.zeros((1, 16, 1, 3200), np.float32)
    for c in range(8):
        b, hh = c // 2, c % 2
        r = R[c]
        sh_cols = np.concatenate([np.arange(i * 1024 + hh * 512, i * 1024 + hh * 512 + 512) for i in range(3)]
                                 + [np.arange(3072, 3200)])
        oc = slice(hh * 1024, hh * 1024 + 1024)
        sb = slice(4 * b, 4 * b + 4)
        y_p[b, :, oc] = r["y_p"]
        y_s[sb, :, oc] = r["y_s"]
        k_p[0, b, :, 4 * hh:4 * hh + 4] = r["k_p"].reshape(T, 4, 2, 64)
        v_p[0, b, :, 4 * hh:4 * hh + 4] = r["v_p"].reshape(T, 4, 128)
        wkv_p[0, b, 8 * hh:8 * hh + 8] = r["wkv_p"]
        sh_p[0, b, 0, sh_cols] = r["shift_p"]
        k_s[0, sb, :, 4 * hh:4 * hh + 4] = r["k_s"].reshape(4, DEC, 4, 2, 64)
        v_s[0, sb, :, 4 * hh:4 * hh + 4] = r["v_s"].reshape(4, DEC, 4, 128)
        wkv_s[0, sb, 8 * hh:8 * hh + 8] = r["wkv_s"]
        sh_s[0, sb, 0][:, sh_cols] = r["shift_s"]
    return (y_p, y_s, k_p, v_p, wkv_p, sh_p, k_s, v_s, wkv_s, sh_s)
```

```python
import math
from contextlib import ExitStack
import numpy as np
import concourse.bass as bass
import concourse.mybir as mybir
from concourse.bass_utils import run_bass_kernel_spmd

F32 = mybir.dt.float32
BF16 = mybir.dt.bfloat16
I32 = mybir.dt.int32
ALU = mybir.AluOpType
AF = mybir.ActivationFunctionType
AX = mybir.AxisListType

CFG = {"NPT": 64, "debug": False, "phases": "A1,B,A2,C,D"}
NST = 4
PAST = 2048
DEC = 32
EPS = 1e-6
LNX_EPS = 64e-5
LAM_INIT = 0.8 - 0.6 * math.exp(-0.3 * 0)


class Res:
    def __init__(self, name, handle=None, excl=False):
        self.name = name
        self.h = handle
        self.excl = excl
        self.writers = {}
        self.readers = {}
        self.dsems = {}

    def __getitem__(self, key):
        return self.h[key]


class Op:
    __slots__ = ("eng", "fn", "reads", "writes", "dma", "sres", "deps", "sig", "sigval", "idx")


COMPUTE = ("pe", "act", "dve", "pool")


class Prog:
    def __init__(self, nc, es):
        self.nc = nc
        self.es = es
        self.root_es = es
        self.ops = []
        self.engs = {"pe": nc.tensor, "act": nc.scalar, "dve": nc.vector, "pool": nc.gpsimd, "sp": nc.sync}
        self.nsem = 0
        self.all_res = []
        self.phase_res = []
        self.free_dsems = {"sw": [], "hw": []}
        self.esem = None
        self.ecount = {e: 0 for e in COMPUTE}
        self.waited = {}
        self.ninst = {e: 0 for e in self.engs}
        self.nwait = 0
        self.nops = 0

    def _reg(self, r):
        self.all_res.append(r)
        if self.es is not self.root_es:
            self.phase_res.append(r)
        return r

    def sb(self, name, shape, dtype):
        self.nalloc = getattr(self, "nalloc", 0) + 1
        name = f"{name}_{self.nalloc}"
        return self._reg(Res(name, self.es.enter_context(self.nc.sbuf_tensor(name, list(shape), dtype))))

    def ps(self, name, shape, dtype):
        self.nalloc = getattr(self, "nalloc", 0) + 1
        name = f"{name}_{self.nalloc}"
        return self._reg(Res(name, self.es.enter_context(self.nc.psum_tensor(name, list(shape), dtype)), excl=True))

    def dres(self, name):
        return self._reg(Res(name))

    def sub(self, res, n):
        return [self._reg(Res(f"{res.name}.{i}", res.h)) for i in range(n)]

    def sem(self, name):
        self.nsem += 1
        return self.root_es.enter_context(self.nc.semaphore(name))

    def op(self, eng, fn, r=(), w=()):
        o = Op()
        o.eng = eng; o.fn = fn; o.reads = tuple(r); o.writes = tuple(w)
        o.dma = False; o.sres = None; o.deps = None; o.sig = False; o.sigval = 0
        o.idx = self.nops
        self.nops += 1
        self.ops.append(o)
        return o

    def pe(self, fn, r=(), w=()): return self.op("pe", fn, r, w)
    def act(self, fn, r=(), w=()): return self.op("act", fn, r, w)
    def dve(self, fn, r=(), w=()): return self.op("dve", fn, r, w)
    def pool(self, fn, r=(), w=()): return self.op("pool", fn, r, w)

    def dma(self, q, out, in_, r=(), w=(), sres=None, **kw):
        if CFG.get("nostore") and q == "pool" and w and w[0].h is None:
            return None
        o = self.op(q, lambda e: e.dma_start(out=out, in_=in_, **kw), r, w)
        o.dma = True
        o.sres = sres
        assert sres is not None
        return o

    @staticmethod
    def _kind(o):
        return "sw" if o.eng == "pool" else "hw"

    def _key(self, o):
        return ("d", id(o.sres), self._kind(o)) if o.dma else o.eng

    def flush(self):
        if self.esem is None:
            self.esem = {e: self.sem("S_" + e) for e in COMPUTE}
        esem, ecount, waited = self.esem, self.ecount, self.waited
        last = {}
        for o in self.ops:
            deps = {}

            def add(d):
                if d is o:
                    return
                if (not d.dma) and (not o.dma) and d.eng == "pe" and o.eng == "pe":
                    return
                deps[d.idx] = d
            for r in o.reads:
                for d in r.writers.values():
                    add(d)
                if r.excl:
                    for d in r.readers.values():
                        add(d)
            for w in o.writes:
                if w.readers:
                    for d in w.readers.values():
                        add(d)
                    for d in w.writers.values():
                        add(d)
                else:
                    for d in w.writers.values():
                        if o.dma and d.dma:
                            continue
                        add(d)
            k = self._key(o)
            for r in o.reads:
                r.readers[k] = o
            for w in o.writes:
                if w.readers:
                    w.writers = {k: o}
                    w.readers = {}
                else:
                    w.writers[k] = o
            o.deps = list(deps.values())
            for d in o.deps:
                d.sig = True
            if not o.dma:
                last[o.eng] = o
        for o in last.values():
            o.sig = True
        for o in self.ops:
            eng = self.engs[o.eng]
            need = {}
            for d in o.deps:
                if d.dma:
                    ent = d.sres.dsems[self._kind(d)]
                    need[("d", id(ent[0]))] = (ent[0], ent[1])
                else:
                    v = d.sigval
                    assert v > 0
                    if d.eng not in need or need[d.eng][1] < v:
                        need[d.eng] = (esem[d.eng], v)
            for key, (s, v) in need.items():
                wk = (o.eng, key)
                if waited.get(wk, 0) >= v:
                    continue
                waited[wk] = v
                eng.wait_ge(s, v)
                self.nwait += 1
            ins = o.fn(eng)
            self.ninst[o.eng] += 1
            if o.dma:
                s = o.sres
                kd = self._kind(o)
                if kd not in s.dsems:
                    if self.free_dsems[kd]:
                        s.dsems[kd] = self.free_dsems[kd].pop()
                    else:
                        s.dsems[kd] = [self.sem("D_" + kd + "_" + s.name), 0]
                ent = s.dsems[kd]
                ent[1] += 16
                ins.then_inc(ent[0], 16)
            elif o.sig:
                ecount[o.eng] += 1
                o.sigval = ecount[o.eng]
                ins.then_inc(esem[o.eng], 1)
        dsems = {}
        for r in self.all_res:
            for ent in r.dsems.values():
                dsems[id(ent[0])] = (ent[0], ent[1])
        for lst in self.free_dsems.values():
            for ent in lst:
                dsems[id(ent[0])] = (ent[0], ent[1])
        for en, eng in self.engs.items():
            for e2 in COMPUTE:
                if e2 == en or ecount[e2] == 0:
                    continue
                wk = (en, e2)
                if waited.get(wk, 0) < ecount[e2]:
                    waited[wk] = ecount[e2]
                    eng.wait_ge(esem[e2], ecount[e2])
            for key, (s, c) in dsems.items():
                wk = (en, ("d", key))
                if waited.get(wk, 0) < c:
                    waited[wk] = c
                    eng.wait_ge(s, c)
        for r in self.all_res:
            r.writers = {}
            r.readers = {}
        self.ops = []

    def end_phase(self):
        self.flush()
        for r in self.phase_res:
            for kd, ent in r.dsems.items():
                self.free_dsems[kd].append(ent)
            r.dsems = {}
        pr = set(id(r) for r in self.phase_res)
        self.all_res = [r for r in self.all_res if id(r) not in pr]
        self.phase_res = []

    @property
    def stats(self):
        return dict(ninst=self.ninst, nwait=self.nwait, nsem=self.nsem, sig=dict(self.ecount))


class Builder:
    def __init__(self):
        self.NPT = CFG["NPT"]
        self.NTT = self.NPT + NST
        self.T = self.NPT * 128
        self.nc = bass.Bass("TRN2", target_bir_lowering=False)
        self.es = ExitStack()
        self.P = Prog(self.nc, self.es)
        self.declare_dram()

    def dram(self, name, shape, dtype, kind):
        kw = {}
        if kind == "Internal":
            kw["addr_space"] = "Local"
        return self.nc.dram_tensor(name, list(shape), dtype, kind=kind, **kw).ap()

    def declare_dram(self):
        T, NTT = self.T, self.NTT
        I = lambda n, s: self.dram(n, s, F32, "ExternalInput")
        O = lambda n, s: self.dram(n, s, F32, "ExternalOutput")
        self.xp = I("xp", [T, 2048])
        self.xs = I("xs", [NST, DEC, 2048])
        self.w_att = I("w_att", [2048, 2048])
        self.w_rw = I("w_rw", [2048, 2176])
        self.w_out = I("w_out", [2048, 1024])
        self.xres_p = I("xres_p", [T, 1024])
        self.xres_s = I("xres_s", [NST, DEC, 1024])
        self.norm_g = I("norm_g", [2048])
        self.qg = I("qg", [64]); self.kg = I("kg", [64])
        self.lq1 = I("lq1", [64]); self.lk1 = I("lk1", [64]); self.lq2 = I("lq2", [64]); self.lk2 = I("lk2", [64])
        self.subln = I("subln", [128])
        self.mu = I("mu", [1664])
        self.w0 = I("w0", [512]); self.a0 = I("a0", [512])
        self.w_up = I("w_up", [64, 512]); self.a_up = I("a_up", [64, 512])
        self.k_k = I("k_k", [512]); self.k_a = I("k_a", [512]); self.r_k = I("r_k", [512])
        self.lnx_g = I("lnx_g", [512]); self.lnx_b = I("lnx_b", [512])
        self.cache_k = I("cache_k", [NST, PAST, 512])
        self.cache_v = I("cache_v", [NST, PAST, 512])
        self.wkv0 = I("wkv0", [NST, 8, 64, 64])
        self.shift0 = I("shift0", [NST, 1664])
        self.y_p = O("y_p", [T, 1024]); self.y_s = O("y_s", [NST, DEC, 1024])
        self.k_p = O("k_p", [T, 512]); self.v_p = O("v_p", [T, 512])
        self.wkv_p = O("wkv_p", [8, 64, 64]); self.shift_p = O("shift_p", [1664])
        self.k_s = O("k_s", [NST, DEC, 512]); self.v_s = O("v_s", [NST, DEC, 512])
        self.wkv_s = O("wkv_s", [NST, 8, 64, 64]); self.shift_s = O("shift_s", [NST, 1664])
        S = lambda n, s, d=BF16: self.dram(n, s, d, "Internal")
        self.QT = S("QT", [4, 128, NTT * 128])
        self.KT = S("KT", [4, 128, NTT * 128])
        self.Vs = S("Vs", [4, 128, NTT, 128])
        self.Gs = S("Gs", [NTT * 128, 512])
        dbg = CFG["debug"]
        self.Mx = self.dram("Mxs", [NTT * 128, 1024], BF16, "Internal")
        if dbg:
            self.Mx_dbg = self.dram("Mx", [NTT * 128, 1024], BF16, "ExternalOutput")
        self.CCH = 8
        self.Mall = [S(f"Mall{i}", [2 * self.CCH * 128, 1024]) for i in range((NTT + self.CCH - 1) // self.CCH)]
        D = self.P.dres
        self.dQT = D("dQT"); self.dKT = D("dKT"); self.dVs = D("dVs"); self.dGs = D("dGs")
        self.dMx = D("dMx"); self.dMall = D("dMall"); self.dOut = D("dOut")
        self.zero_pad()

    def zero_pad(self):
        pass

    def begin_phase(self):
        self.pes = ExitStack()
        self.P.es = self.pes

    def end_phase(self):
        self.P.end_phase()
        self.pes.close()
        self.P.es = self.P.root_es

    def bcast_load(self, name, src, n, q="sp"):
        t = self.P.sb(name, [128, n], F32)
        self.P.dma(q, t[:], src.partition_broadcast(128), w=[t], sres=t)
        return t

    def tile_rows(self, tt):
        return 128 if tt < self.NPT else DEC

    def setup(self):
        P, nc = self.P, self.nc
        NTT, NPT = self.NTT, self.NPT
        self.identf = P.sb("identf", [128, 128], F32)
        self.ident = P.sb("ident", [128, 128], BF16)
        self.maskf = P.sb("maskf", [128, 384], F32)
        self.maskb = P.sb("maskb", [128, 384], BF16)
        self.epst = P.sb("epst", [128, 1], F32)
        self.lnxeps = P.sb("lnxeps", [128, 1], F32)
        idf, mk = self.identf, self.maskf
        P.pool(lambda e: e.memset(self.epst[:], EPS), w=[self.epst])
        P.pool(lambda e: e.memset(self.lnxeps[:], LNX_EPS), w=[self.lnxeps])
        P.pool(lambda e: e.memset(idf[:], 1.0), w=[idf])
        P.pool(lambda e: e.affine_select(out=idf[:], in_=idf[:], compare_op=ALU.is_ge, fill=0.0, base=0,
                                         pattern=[[-1, 128]], channel_multiplier=1), r=[idf], w=[idf])
        P.pool(lambda e: e.affine_select(out=idf[:], in_=idf[:], compare_op=ALU.is_ge, fill=0.0, base=0,
                                         pattern=[[1, 128]], channel_multiplier=-1), r=[idf], w=[idf])
        P.dve(lambda e: e.tensor_copy(out=self.ident[:], in_=idf[:]), r=[idf], w=[self.ident])
        P.pool(lambda e: e.memset(mk[:], 1.0), w=[mk])
        P.pool(lambda e: e.affine_select(out=mk[:, 0:128], in_=mk[:, 0:128], compare_op=ALU.is_gt, fill=0.0, base=0,
                                         pattern=[[1, 128]], channel_multiplier=-1), r=[mk], w=[mk])
        P.pool(lambda e: e.affine_select(out=mk[:, 128:256], in_=mk[:, 128:256], compare_op=ALU.is_ge, fill=0.0, base=0,
                                         pattern=[[1, 128]], channel_multiplier=-1), r=[mk], w=[mk])
        P.pool(lambda e: e.affine_select(out=mk[:, 256:384], in_=mk[:, 256:384], compare_op=ALU.is_gt, fill=0.0, base=0,
                                         pattern=[[-1, 128]], channel_multiplier=1), r=[mk], w=[mk])
        P.dve(lambda e: e.tensor_copy(out=self.maskb[:], in_=mk[:]), r=[mk], w=[self.maskb])
        self.gb = self.bcast_load("gb", self.norm_g, 2048)
        self.gq = self.bcast_load("gq", self.qg, 64)
        self.gk = self.bcast_load("gk", self.kg, 64)
        self.ngq = P.sb("ngq", [128, 32], F32)
        self.ngk = P.sb("ngk", [128, 32], F32)
        for dst, src in ((self.ngq, self.gq), (self.ngk, self.gk)):
            P.dve(lambda e, dst=dst, src=src: e.tensor_scalar(out=dst[:], in0=src[:, 32:64], scalar1=-1.0, scalar2=None,
                                                             op0=ALU.mult), r=[src], w=[dst])
        self.sub_b = self.bcast_load("sub_b", self.subln, 128)
        P.dve(lambda e: e.tensor_scalar(out=self.sub_b[:], in0=self.sub_b[:], scalar1=1.0 - LAM_INIT, scalar2=None,
                                        op0=ALU.mult), r=[self.sub_b], w=[self.sub_b])
        lam_in = [self.bcast_load(n, s, 64) for n, s in (("lq1b", self.lq1), ("lk1b", self.lk1),
                                                         ("lq2b", self.lq2), ("lk2b", self.lk2))]
        lj = P.sb("lj", [128, 64], F32)
        ls = P.sb("ls", [128, 2], F32)
        self.neglam = P.sb("neglam", [128, 1], F32)
        for i in range(2):
            a, b = lam_in[2 * i], lam_in[2 * i + 1]
            P.dve(lambda e, a=a, b=b: e.tensor_tensor(out=lj[:], in0=a[:], in1=b[:], op=ALU.mult), r=[a, b], w=[lj])
            P.dve(lambda e, i=i: e.reduce_sum(out=ls[:, i:i + 1], in_=lj[:], axis=AX.X), r=[lj], w=[ls])
        P.act(lambda e: e.activation(out=ls[:], in_=ls[:], func=AF.Exp), r=[ls], w=[ls])
        P.dve(lambda e: e.tensor_tensor(out=self.neglam[:], in0=ls[:, 1:2], in1=ls[:, 0:1], op=ALU.subtract),
              r=[ls], w=[self.neglam])
        P.dve(lambda e: e.tensor_scalar(out=self.neglam[:], in0=self.neglam[:], scalar1=-LAM_INIT, scalar2=None,
                                        op0=ALU.add), r=[self.neglam], w=[self.neglam])
        self.mq = P.sb("mq", [128, 1], F32)
        self.mk8 = P.sb("mk8", [128, 1], F32)
        P.dve(lambda e: e.tensor_reduce(out=self.mq[:], in_=self.gq[:], axis=AX.X, op=ALU.max,
                                        apply_absolute_value=True), r=[self.gq], w=[self.mq])
        P.dve(lambda e: e.tensor_reduce(out=self.mk8[:], in_=self.gk[:], axis=AX.X, op=ALU.max,
                                        apply_absolute_value=True), r=[self.gk], w=[self.mk8])
        P.dve(lambda e: e.tensor_tensor(out=self.mk8[:], in0=self.mk8[:], in1=self.mk8[:], op=ALU.mult),
              r=[self.mk8], w=[self.mk8])
        P.dve(lambda e: e.tensor_scalar(out=self.mk8[:], in0=self.mk8[:], scalar1=64.0, scalar2=None, op0=ALU.mult),
              r=[self.mk8], w=[self.mk8])
        self.negC_p = P.sb("negC_p", [128, 1], F32)
        P.act(lambda e: e.activation(out=self.negC_p[:], in_=self.mk8[:], func=AF.Sqrt), r=[self.mk8], w=[self.negC_p])
        P.dve(lambda e: e.scalar_tensor_tensor(out=self.negC_p[:], in0=self.negC_p[:], scalar=-1.0, in1=self.mq[:],
                                               op0=ALU.mult, op1=ALU.mult), r=[self.negC_p, self.mq], w=[self.negC_p])

    def rope_tables(self):
        P = self.P
        NTT, NPT = self.NTT, self.NPT
        self.cosT = P.sb("cosT", [128, NTT, 32], F32)
        self.sinT = P.sb("sinT", [128, NTT, 32], F32)
        posi = P.sb("posi", [128, NTT], I32)
        posf = P.sb("posf", [128, NTT], F32)
        P.pool(lambda e: e.iota(posi[:, 0:NPT], pattern=[[128, NPT]], base=0, channel_multiplier=1), w=[posi])
        P.pool(lambda e: e.iota(posi[:, NPT:NTT], pattern=[[0, NST]], base=PAST, channel_multiplier=1), w=[posi])
        P.dve(lambda e: e.tensor_copy(out=posf[:], in_=posi[:]), r=[posi], w=[posf])
        invf = P.sb("invf", [128, 32], F32)
        for i in range(32):
            v = float(np.float32(10000.0) ** np.float32(-i / 32.0))
            P.pool(lambda e, i=i, v=v: e.memset(invf[:, i:i + 1], v), w=[invf])
        ang = P.sb("ang", [128, NTT, 32], F32)
        a2 = P.sb("ang2", [128, NTT, 32], F32)
        ki = P.sb("angk", [128, NTT, 32], I32)
        kf = P.sb("angkf", [128, NTT, 32], F32)
        P.dve(lambda e: e.tensor_tensor(out=ang[:], in0=posf[:, :].unsqueeze(2).broadcast_to([128, NTT, 32]),
                                        in1=invf[:, :].unsqueeze(1).broadcast_to([128, NTT, 32]), op=ALU.mult),
              r=[posf, invf], w=[ang])
        C1 = 6.28125
        C2 = 2.0 * math.pi - C1
        for which, dst, off in (("s", self.sinT, 0.0), ("c", self.cosT, math.pi / 2)):
            P.dve(lambda e, off=off: e.tensor_scalar(out=a2[:], in0=ang[:], scalar1=off, scalar2=1.0 / (2 * math.pi),
                                                     op0=ALU.add, op1=ALU.mult), r=[ang], w=[a2])
            P.dve(lambda e: e.tensor_copy(out=ki[:], in_=a2[:]), r=[a2], w=[ki])
            P.dve(lambda e: e.tensor_copy(out=kf[:], in_=ki[:]), r=[ki], w=[kf])
            P.dve(lambda e, off=off: e.tensor_scalar(out=a2[:], in0=ang[:], scalar1=off, scalar2=None, op0=ALU.add),
                  r=[ang], w=[a2])
            P.dve(lambda e: e.scalar_tensor_tensor(out=a2[:], in0=kf[:], scalar=-C1, in1=a2[:], op0=ALU.mult, op1=ALU.add),
                  r=[kf, a2], w=[a2])
            P.dve(lambda e: e.scalar_tensor_tensor(out=a2[:], in0=kf[:], scalar=-C2, in1=a2[:], op0=ALU.mult, op1=ALU.add),
                  r=[kf, a2], w=[a2])
            P.dve(lambda e: e.tensor_scalar(out=kf[:], in0=a2[:], scalar1=math.pi, scalar2=-2 * math.pi,
                                            op0=ALU.is_gt, op1=ALU.mult), r=[a2], w=[kf])
            P.dve(lambda e: e.tensor_tensor(out=a2[:], in0=a2[:], in1=kf[:], op=ALU.add), r=[a2, kf], w=[a2])
            P.dve(lambda e: e.tensor_scalar(out=kf[:], in0=a2[:], scalar1=-math.pi, scalar2=2 * math.pi,
                                            op0=ALU.is_lt, op1=ALU.mult), r=[a2], w=[kf])
            P.dve(lambda e: e.tensor_tensor(out=a2[:], in0=a2[:], in1=kf[:], op=ALU.add), r=[a2, kf], w=[a2])
            P.dve(lambda e: e.tensor_scalar(out=a2[:], in0=a2[:], scalar1=3.1415925, scalar2=-3.1415925,
                                            op0=ALU.min, op1=ALU.max), r=[a2], w=[a2])
            P.act(lambda e, dst=dst: e.activation(out=dst[:], in_=a2[:], func=AF.Sin), r=[a2], w=[dst])


    def alloc_xnorm(self, nx=2):
        P = self.P
        self.xt = [P.sb(f"xt{i}", [128, 2048], F32) for i in range(nx)]
        self.ss = P.sb("ss", [128, 1], F32)
        self.rstd = P.sb("rstd", [128, 1], F32)
        self.xn = P.sb("xn", [128, 2048], BF16)
        self.junk = self.xn
        self.xnT = [P.sb(f"xnT{i}", [128, 16, 128], BF16) for i in range(nx)]
        self.pT = [P.ps(f"pT{i}", [128, 8, 128], BF16) for i in range(2)]
        self.xcnt = 0

    def xnorm_tile(self, tt):
        P = self.P
        X = self.xt[self.xcnt % len(self.xt)]
        XT = self.xnT[self.xcnt % len(self.xnT)]
        self.xcnt += 1
        if tt < self.NPT:
            P.dma("sp", X[:], self.xp[tt * 128:(tt + 1) * 128, :], w=[X], sres=X)
        else:
            s = tt - self.NPT
            P.pool(lambda e: e.memset(X[:, :], 0.0), w=[X])
            P.dma("sp", X[0:DEC, :], self.xs[s], w=[X], sres=X)
        ss, rstd, xn, junk = self.ss, self.rstd, self.xn, self.junk
        P.act(lambda e: e.activation(out=junk[:], in_=X[:], func=AF.Square, accum_out=ss[:]), r=[X], w=[xn, ss])
        P.act(lambda e: e.activation(out=rstd[:], in_=ss[:], func=AF.Sqrt, scale=1.0 / 2048, bias=self.epst[:, 0:1]),
              r=[ss, self.epst], w=[rstd])
        P.dve(lambda e: e.reciprocal(out=rstd[:], in_=rstd[:]), r=[rstd], w=[rstd])
        P.dve(lambda e: e.scalar_tensor_tensor(out=xn[:], in0=X[:], scalar=rstd[:, 0:1], in1=self.gb[:],
                                               op0=ALU.mult, op1=ALU.mult), r=[X, rstd, self.gb], w=[xn])
        for hf in range(2):
            pt = self.pT[hf]
            for j in range(8):
                k = hf * 8 + j
                P.pe(lambda e, pt=pt, j=j, k=k: e.transpose(out=pt[:, j, :], in_=xn[:, k * 128:(k + 1) * 128],
                                                           identity=self.ident[:]), r=[xn, self.ident], w=[pt])
            if hf == 0:
                P.act(lambda e, pt=pt: e.copy(out=XT[:, 0:8, :], in_=pt[:]), r=[pt], w=[XT])
            else:
                P.dve(lambda e, pt=pt: e.tensor_copy(out=XT[:, 8:16, :], in_=pt[:]), r=[pt], w=[XT])
        return XT

    def phase_A1(self):
        P = self.P
        NTT, NPT = self.NTT, self.NPT
        self.begin_phase()
        self.rope_tables()
        self.alloc_xnorm()
        W = P.sb("Wb", [128, 16, 2048], BF16)
        self.pY = [P.ps(f"pY{i}", [128, 512], F32) for i in range(2)]
        self.pTq = P.ps("pTq", [128, 8, 128], BF16)
        for k in range(16):
            P.dma("pool", W[:, k, :], self.w_att[k * 128:(k + 1) * 128, :], w=[W], sres=W)
        pY = self.pY
        pTq = self.pTq
        sq = P.sb("a1_sq", [128, 512], F32)
        ssq = P.sb("a1_ssq", [128, 8], F32)
        rq = P.sb("a1_rq", [128, 8], F32)
        qs = P.sb("a1_qs", [128, 8, 64], F32)
        tmp = P.sb("a1_tmp", [128, 8, 64], F32)
        of = [P.sb(f"a1_of{i}", [128, 8, 64], F32) for i in range(2)]
        ob = P.sb("a1_ob", [128, 512], BF16)
        qT = [P.sb(f"a1_qT{i}", [128, 4, 128], BF16) for i in range(2)]
        vf = [P.sb(f"a1_vf{i}", [128, 512], F32) for i in range(2)]
        vb = [P.sb(f"a1_vb{i}", [128, 512], BF16) for i in range(2)]
        gbf = [P.sb(f"a1_gb{i}", [128, 512], BF16) for i in range(2)]
        tabs = {}
        for nm in ("q", "k"):
            tabs[nm] = (P.sb(f"a1_CC{nm}", [128, 64], F32), P.sb(f"a1_S1{nm}", [128, 32], F32),
                        P.sb(f"a1_nS2{nm}", [128, 32], F32))
        cnt = 0
        STOP = CFG.get("stop", 99)
        tlist = list(CFG.get("tiles", range(NTT)))
        XT_next = self.xnorm_tile(tlist[0])
        for ti, tt in enumerate(tlist):
            XT = XT_next
            if STOP <= 1:
                continue
            for nm, g, ng in (("q", self.gq, self.ngq), ("k", self.gk, self.ngk)):
                CC, S1, nS2 = tabs[nm]
                cos_b = self.cosT[:, tt, :].unsqueeze(1).broadcast_to([128, 2, 32])
                P.pool(lambda e, CC=CC, g=g, cos_b=cos_b: e.tensor_tensor(
                    out=CC[:, :].rearrange("p (a b) -> p a b", a=2), in0=g[:, :].rearrange("p (a b) -> p a b", a=2),
                    in1=cos_b, op=ALU.mult), r=[g, self.cosT], w=[CC])
                P.pool(lambda e, S1=S1, g=g, tt=tt: e.tensor_tensor(out=S1[:], in0=g[:, 0:32], in1=self.sinT[:, tt, :],
                                                                   op=ALU.mult), r=[g, self.sinT], w=[S1])
                P.pool(lambda e, nS2=nS2, ng=ng, tt=tt: e.tensor_tensor(out=nS2[:], in0=ng[:], in1=self.sinT[:, tt, :],
                                                                        op=ALU.mult), r=[ng, self.sinT], w=[nS2])
            for cb, nm in enumerate(("q", "k", "v", "g")):
                py = pY[cnt % 2]
                cnt += 1
                for k in range(16):
                    P.pe(lambda e, py=py, k=k, cb=cb, XT=XT: e.matmul(
                        py[:], lhsT=XT[:, k, :], rhs=W[:, k, cb * 512:(cb + 1) * 512], start=(k == 0), stop=(k == 15)),
                        r=[XT, W], w=[py])
                if cb == 0 and ti + 1 < len(tlist):
                    XT_next = self.xnorm_tile(tlist[ti + 1])
                if STOP <= 2:
                    continue
                if nm in ("q", "k"):
                    CC, S1, nS2 = tabs[nm]
                    o_f = of[cnt % 2]
                    P.act(lambda e, py=py: e.activation(out=sq[:], in_=py[:], func=AF.Square), r=[py], w=[sq])
                    P.dve(lambda e: e.reduce_sum(out=ssq[:], in_=sq[:, :].rearrange("p (g d) -> p g d", g=8), axis=AX.X),
                          r=[sq], w=[ssq])
                    P.act(lambda e: e.activation(out=rq[:], in_=ssq[:], func=AF.Sqrt, scale=1.0 / 64,
                                                 bias=self.epst[:, 0:1]), r=[ssq, self.epst], w=[rq])
                    P.dve(lambda e: e.reciprocal(out=rq[:], in_=rq[:]), r=[rq], w=[rq])
                    if STOP <= 3:
                        continue
                    P.dve(lambda e, py=py: e.tensor_tensor(
                        out=qs[:], in0=py[:, :].rearrange("p (g d) -> p g d", g=8),
                        in1=rq[:, :].unsqueeze(2).broadcast_to([128, 8, 64]), op=ALU.mult), r=[py, rq], w=[qs])
                    if STOP <= 4:
                        continue
                    P.dve(lambda e, o_f=o_f, CC=CC: e.tensor_tensor(
                        out=o_f[:], in0=qs[:], in1=CC[:, :].unsqueeze(1).broadcast_to([128, 8, 64]), op=ALU.mult),
                        r=[qs, CC], w=[o_f])
                    P.pool(lambda e, nS2=nS2: e.tensor_tensor(
                        out=tmp[:, :, 0:32], in0=qs[:, :, 32:64],
                        in1=nS2[:, :].unsqueeze(1).broadcast_to([128, 8, 32]), op=ALU.mult), r=[qs, nS2], w=[tmp])
                    P.pool(lambda e, S1=S1: e.tensor_tensor(
                        out=tmp[:, :, 32:64], in0=qs[:, :, 0:32],
                        in1=S1[:, :].unsqueeze(1).broadcast_to([128, 8, 32]), op=ALU.mult), r=[qs, S1], w=[tmp])
                    P.dve(lambda e, o_f=o_f: e.tensor_tensor(out=o_f[:], in0=o_f[:], in1=tmp[:], op=ALU.add),
                          r=[o_f, tmp], w=[o_f])
                    if STOP <= 5:
                        continue
                    P.act(lambda e, o_f=o_f: e.copy(out=ob[:], in_=o_f[:, :, :].rearrange("p g d -> p (g d)")),
                          r=[o_f], w=[ob])
                    if nm == "k":
                        if tt < NPT:
                            P.dma("pool", self.k_p[tt * 128:(tt + 1) * 128, :], o_f[:, :, :].rearrange("p g d -> p (g d)"),
                                  r=[o_f], w=[self.dOut], sres=o_f)
                        else:
                            P.dma("pool", self.k_s[tt - NPT], o_f[0:DEC, :, :].rearrange("p g d -> p (g d)"),
                                  r=[o_f], w=[self.dOut], sres=o_f)
                    qt = qT[cnt % 2]
                    for h in range(4):
                        P.pe(lambda e, h=h: e.transpose(out=pTq[:, h, :], in_=ob[:, h * 128:(h + 1) * 128],
                                                        identity=self.ident[:]), r=[ob, self.ident], w=[pTq])
                    P.dve(lambda e, qt=qt: e.tensor_copy(out=qt[:], in_=pTq[:, 0:4, :]), r=[pTq], w=[qt])
                    dst, dres = (self.QT, self.dQT) if nm == "q" else (self.KT, self.dKT)
                    P.dma("pool", dst[:, :, tt * 128:(tt + 1) * 128].rearrange("h p c -> p h c"), qt[:],
                          r=[qt], w=[dres], sres=qt)
                elif STOP <= 6:
                    continue
                elif nm == "v" and CFG.get("nov"):
                    continue
                elif nm == "v":
                    v_f = vf[cnt % 2]
                    v_b = vb[cnt % 2]
                    P.act(lambda e, py=py, v_f=v_f: e.copy(out=v_f[:], in_=py[:]), r=[py], w=[v_f])
                    P.pool(lambda e, v_f=v_f, v_b=v_b: e.tensor_copy(out=v_b[:], in_=v_f[:]), r=[v_f], w=[v_b])
                    if tt < NPT:
                        P.dma("pool", self.v_p[tt * 128:(tt + 1) * 128, :], v_f[:], r=[v_f], w=[self.dOut], sres=v_f)
                    else:
                        P.dma("pool", self.v_s[tt - NPT], v_f[0:DEC, :], r=[v_f], w=[self.dOut], sres=v_f)
                    P.dma("pool", self.Vs[:, :, tt, :].rearrange("h p d -> p h d"),
                          v_b[:, :].rearrange("p (h d) -> p h d", h=4), r=[v_b], w=[self.dVs], sres=v_b)
                else:
                    g_b = gbf[cnt % 2]
                    P.act(lambda e, py=py, g_b=g_b: e.activation(out=g_b[:], in_=py[:], func=(AF.Copy if CFG.get("nosilu") else AF.Silu)), r=[py], w=[g_b])
                    P.dma("pool", self.Gs[tt * 128:(tt + 1) * 128, :], g_b[:], r=[g_b], w=[self.dGs], sres=g_b)
        self.end_phase()

    def phase_B(self):
        P = self.P
        NTT, NPT, T = self.NTT, self.NPT, self.T
        self.begin_phase()
        self.pTq = P.ps("pTq", [128, 8, 128], BF16)
        self.pM = P.ps("pM", [128, 512], F32)
        NKT = NPT
        Kb = [P.sb(f"b_K{i}", [128, max(T, PAST + 128)], BF16) for i in range(2)]
        Vb = [P.sb(f"b_V{i}", [128, max(NKT, 17), 130], BF16) for i in range(2)]
        for v in Vb:
            P.pool(lambda e, v=v: e.memset(v[:, :, 128:129], 1.0), w=[v])
            P.pool(lambda e, v=v: e.memset(v[:, :, 129:130], 0.0), w=[v])
        Kc = P.sb("b_Kc", [128, 16, 128], BF16)
        Qt = [P.sb(f"b_Q{i}", [128, 512], BF16) for i in range(2)]
        Pt = [P.sb(f"b_P{i}", [128, 512], BF16) for i in range(3)]
        Gt = [P.sb(f"b_G{i}", [128, 4, 128], BF16) for i in range(2)]
        pS = [P.ps(f"b_pS{i}", [128, 512], F32) for i in range(2)]
        pO = [[P.ps(f"b_pO{c}{i}", [128, 2, 256], F32) for i in range(2)] for c in range(2)]
        pTk = self.pTq
        rl = P.sb("b_rl", [128, 2], F32)
        o1 = P.sb("b_o1", [128, 128], F32)
        o2 = P.sb("b_o2", [128, 128], F32)
        jk = P.sb("b_jk", [128, 128], BF16)
        oss = P.sb("b_oss", [128, 1], F32)
        ors = P.sb("b_ors", [128, 1], F32)
        ob = [P.sb(f"b_ob{i}", [128, 4, 128], BF16) for i in range(2)]
        kss = P.sb("b_kss", [128, 8], F32)
        kmx = P.sb("b_kmx", [128, 1], F32)
        kmr = P.sb("b_kmr", [1, 128], F32)
        km1 = P.sb("b_km1", [1, 1], BF16)
        kmxb = P.sb("b_kmxb", [128, 1], BF16)
        onesr = P.sb("b_ones", [1, 128], BF16)
        P.pool(lambda e: e.memset(onesr[:], 1.0), w=[onesr])
        negC_s = P.sb("b_negCs", [128, 1], F32)
        Kcf = P.sb("b_Kcf", [128, 512], F32)
        pM = self.pM
        hc = 0
        qc = 0
        pc = 0
        sc = 0
        oc = 0
        seqs = [("p", 0)] + [("s", s) for s in range(NST)]
        for kind, s in seqs:
            if kind == "s":
                for kt in range(16):
                    P.dma("sp", Kcf[:], self.cache_k[s, kt * 128:(kt + 1) * 128, :], w=[Kcf], sres=Kcf)
                    P.dve(lambda e: e.tensor_tensor(out=Kcf[:], in0=Kcf[:], in1=Kcf[:], op=ALU.mult), r=[Kcf], w=[Kcf])
                    P.dve(lambda e: e.reduce_sum(out=kss[:], in_=Kcf[:, :].rearrange("p (g d) -> p g d", g=8), axis=AX.X),
                          r=[Kcf], w=[kss])
                    if kt == 0:
                        P.dve(lambda e: e.tensor_reduce(out=kmx[:], in_=kss[:], axis=AX.X, op=ALU.max), r=[kss], w=[kmx])
                    else:
                        P.dve(lambda e: e.tensor_reduce(out=kss[:, 0:1], in_=kss[:], axis=AX.X, op=ALU.max),
                              r=[kss], w=[kss])
                        P.dve(lambda e: e.tensor_tensor(out=kmx[:], in0=kmx[:], in1=kss[:, 0:1], op=ALU.max),
                              r=[kmx, kss], w=[kmx])
                P.dve(lambda e: e.tensor_tensor(out=kmx[:], in0=kmx[:], in1=self.mk8[:], op=ALU.max),
                      r=[kmx, self.mk8], w=[kmx])
                P.dve(lambda e: e.tensor_scalar(out=kmxb[:], in0=kmx[:], scalar1=1.01, scalar2=None, op0=ALU.mult),
                      r=[kmx], w=[kmxb])
                P.pe(lambda e: e.transpose(out=pM[0:1, 0:64].bitcast(BF16), in_=kmxb[:, 0:1], identity=self.ident[:]),
                     r=[kmxb, self.ident], w=[pM])
                P.dve(lambda e: e.tensor_copy(out=kmr[:], in_=pM[0:1, 0:64].bitcast(BF16)), r=[pM], w=[kmr])
                P.dve(lambda e: e.tensor_reduce(out=km1[:], in_=kmr[:], axis=AX.X, op=ALU.max), r=[kmr], w=[km1])
                P.pe(lambda e: e.matmul(pM[:, 0:1], lhsT=onesr[:], rhs=km1[:], start=True, stop=True),
                     r=[onesr, km1], w=[pM])
                P.act(lambda e: e.activation(out=negC_s[:], in_=pM[:, 0:1], func=AF.Sqrt), r=[pM], w=[negC_s])
                P.dve(lambda e: e.scalar_tensor_tensor(out=negC_s[:], in0=negC_s[:], scalar=-1.0, in1=self.mq[:],
                                                       op0=ALU.mult, op1=ALU.mult), r=[negC_s, self.mq], w=[negC_s])
                negC = negC_s
            else:
                negC = self.negC_p
            for h in range(4):
                K = Kb[hc % 2]
                V = Vb[hc % 2]
                hc += 1
                if kind == "p":
                    nq_tiles = NPT // 4 if NPT >= 4 else 1
                    QW = min(512, T)
                    P.dma("sp", K[:, 0:T], self.KT[h, :, 0:T], r=[self.dKT], w=[K], sres=K)
                    P.dma("sp", V[:, 0:NKT, 0:128], self.Vs[h, :, 0:NKT, :], r=[self.dVs], w=[V], sres=V)
                    tok0 = 0
                else:
                    nq_tiles = 1
                    QW = DEC
                    tt = NPT + s
                    tok0 = tt * 128
                    P.dma("pool", Kc[:], self.cache_k[s, :, h * 128:(h + 1) * 128].rearrange("(kt p) d -> p kt d", p=128),
                          w=[Kc], sres=Kc)
                    for g4 in range(4):
                        for j in range(4):
                            kt = g4 * 4 + j
                            P.pe(lambda e, kt=kt, j=j: e.transpose(out=pTk[:, j, :], in_=Kc[:, kt, :], identity=self.ident[:]),
                                 r=[Kc, self.ident], w=[pTk])
                        P.dve(lambda e, K=K, g4=g4: e.tensor_copy(
                            out=K[:, g4 * 512:(g4 + 1) * 512], in_=pTk[:, 0:4, :].rearrange("p a b -> p (a b)")),
                            r=[pTk], w=[K])
                    P.dma("sp", K[:, PAST:PAST + DEC], self.KT[h, :, tok0:tok0 + DEC], r=[self.dKT], w=[K], sres=K)
                    P.dma("pool", V[:, 0:16, 0:128],
                          self.cache_v[s, :, h * 128:(h + 1) * 128].rearrange("(kt p) d -> p kt d", p=128), w=[V], sres=V)
                    P.dma("sp", V[0:DEC, 16, 0:128], self.Vs[h, 0:DEC, tt, :], r=[self.dVs], w=[V], sres=V)
                for qi in range(nq_tiles):
                    Q = Qt[qc % 2]
                    G = Gt[qc % 2]
                    qc += 1
                    q0 = qi * QW
                    nsub = (QW + 127) // 128
                    P.dma("sp", Q[:, 0:QW], self.QT[h, :, tok0 + q0:tok0 + q0 + QW], r=[self.dQT], w=[Q], sres=Q)
                    if kind == "p":
                        P.dma("sp", G[:, 0:nsub, :],
                              self.Gs[q0:q0 + QW, h * 128:(h + 1) * 128].rearrange("(a p) d -> p a d", p=128),
                              r=[self.dGs], w=[G], sres=G)
                    else:
                        P.dma("sp", G[0:DEC, 0, :], self.Gs[tok0:tok0 + DEC, h * 128:(h + 1) * 128],
                              r=[self.dGs], w=[G], sres=G)
                    if kind == "p":
                        blocks = [(j * 128, 128, j, 0, None) for j in range(q0 // 128)]
                        for j in range(nsub):
                            blocks.append((q0 + j * 128, 128, q0 // 128 + j, j, j))
                    else:
                        blocks = [(j * 128, 128, j, 0, None) for j in range(16)] + [(PAST, DEC, 16, 0, None)]
                    steps = [(c, bi) + blk for c in range(2) for bi, blk in enumerate(blocks)]

                    def emit_qk(step):
                        nonlocal sc, pc
                        c, bi, kc0, nk, vt, s0, dg = step
                        ps = pS[sc % 2]
                        sc += 1
                        pt = Pt[pc % 3]
                        pc += 1
                        qa = s0 * 128
                        P.pe(lambda e, ps=ps, K=K, Q=Q, c=c, kc0=kc0, nk=nk, qa=qa, QW=QW: e.matmul(
                            ps[0:nk, qa:QW], lhsT=K[64 * c:64 * c + 64, kc0:kc0 + nk], rhs=Q[64 * c:64 * c + 64, qa:QW],
                            start=True, stop=True), r=[K, Q], w=[ps])
                        P.act(lambda e, ps=ps, pt=pt, nk=nk, qa=qa, QW=QW, negC=negC: e.activation(
                            out=pt[0:nk, qa:QW], in_=ps[0:nk, qa:QW], func=AF.Exp, scale=0.125, bias=negC[0:nk, 0:1]),
                            r=[ps, negC], w=[pt])
                        if dg is not None:
                            P.pool(lambda e, pt=pt, dg=dg: e.memset(pt[64:128, dg * 128:dg * 128 + 64], 0.0), w=[pt])
                        return pt

                    def emit_pv(step, pt):
                        c, bi, kc0, nk, vt, s0, dg = step
                        for sub in range(s0, nsub):
                            nqs = min(128, QW - sub * 128)
                            po = pO[c][sub // 2]
                            first = (bi == 0) and (sub % 2 == 0)
                            last = (dg == sub) if kind == "p" else (bi == len(blocks) - 1)
                            P.pe(lambda e, po=po, pt=pt, V=V, sub=sub, nqs=nqs, nk=nk, vt=vt, first=first, last=last:
                                 e.matmul(po[0:nqs, sub % 2, 0:130], lhsT=pt[0:nk, sub * 128:sub * 128 + nqs],
                                          rhs=V[0:nk, vt, :], start=first, stop=last, skip_group_check=True), r=[pt, V], w=[po])
                    pend = None
                    for step in steps:
                        pt_new = emit_qk(step)
                        if pend is not None:
                            emit_pv(*pend)
                        pend = (step, pt_new)
                    emit_pv(*pend)
                    obt = ob[oc % 2]
                    oc += 1
                    for sub in range(nsub):
                        nqs = min(128, QW - sub * 128)
                        p0 = pO[0][sub // 2]
                        p1 = pO[1][sub // 2]
                        si = sub % 2
                        P.dve(lambda e, p0=p0, si=si, nqs=nqs: e.reciprocal(out=rl[0:nqs, 0:1], in_=p0[0:nqs, si, 128:129]),
                              r=[p0], w=[rl])
                        P.dve(lambda e, p1=p1, si=si, nqs=nqs: e.reciprocal(out=rl[0:nqs, 1:2], in_=p1[0:nqs, si, 128:129]),
                              r=[p1], w=[rl])
                        P.dve(lambda e, nqs=nqs: e.tensor_tensor(out=rl[0:nqs, 1:2], in0=rl[0:nqs, 1:2],
                                                                 in1=self.neglam[0:nqs, :], op=ALU.mult),
                              r=[rl, self.neglam], w=[rl])
                        P.dve(lambda e, p0=p0, si=si, nqs=nqs: e.tensor_scalar(
                            out=o1[0:nqs, :], in0=p0[0:nqs, si, 0:128], scalar1=rl[0:nqs, 0:1], scalar2=None, op0=ALU.mult),
                            r=[p0, rl], w=[o1])
                        P.dve(lambda e, p1=p1, si=si, nqs=nqs: e.scalar_tensor_tensor(
                            out=o2[0:nqs, :], in0=p1[0:nqs, si, 0:128], scalar=rl[0:nqs, 1:2], in1=o1[0:nqs, :],
                            op0=ALU.mult, op1=ALU.add), r=[p1, rl, o1], w=[o2])
                        P.act(lambda e, nqs=nqs: e.activation(out=jk[0:nqs, :], in_=o2[0:nqs, :], func=AF.Square,
                                                              accum_out=oss[0:nqs, :]), r=[o2], w=[jk, oss])
                        P.act(lambda e, nqs=nqs: e.activation(out=ors[0:nqs, :], in_=oss[0:nqs, :], func=AF.Sqrt,
                                                              scale=1.0 / 128, bias=self.epst[0:nqs, 0:1]),
                              r=[oss, self.epst], w=[ors])
                        P.dve(lambda e, nqs=nqs: e.reciprocal(out=ors[0:nqs, :], in_=ors[0:nqs, :]), r=[ors], w=[ors])
                        P.dve(lambda e, nqs=nqs: e.scalar_tensor_tensor(
                            out=o1[0:nqs, :], in0=o2[0:nqs, :], scalar=ors[0:nqs, 0:1], in1=self.sub_b[0:nqs, :],
                            op0=ALU.mult, op1=ALU.mult), r=[o2, ors, self.sub_b], w=[o1])
                        P.dve(lambda e, nqs=nqs, sub=sub, G=G, obt=obt: e.tensor_tensor(
                            out=obt[0:nqs, sub, :], in0=o1[0:nqs, :], in1=G[0:nqs, sub, :], op=ALU.mult),
                            r=[o1, G], w=[obt])
                    if kind == "p":
                        P.dma("pool", self.Mx[q0:q0 + QW, h * 128:(h + 1) * 128].rearrange("(a p) d -> p a d", p=128),
                              obt[:, 0:nsub, :], r=[obt], w=[self.dMx], sres=obt)
                    else:
                        P.dma("pool", self.Mx[tok0:tok0 + DEC, h * 128:(h + 1) * 128], obt[0:DEC, 0, :],
                              r=[obt], w=[self.dMx], sres=obt)
        self.end_phase()

    def phase_A2(self):
        P = self.P
        NTT, NPT = self.NTT, self.NPT
        self.begin_phase()
        self.alloc_xnorm(nx=1)
        C0 = math.exp(-0.5)
        W = P.sb("W2", [128, 16, 2176], BF16)
        for k in range(16):
            P.dma("pool", W[:, k, :], self.w_rw[k * 128:(k + 1) * 128, :], w=[W], sres=W)
        banks = [P.ps(f"bk{i}", [128, 512], F32) for i in range(6)]
        self.bkc = 0

        def nb():
            b = banks[self.bkc % len(banks)]
            self.bkc += 1
            return b
        bl = self.bcast_load
        mu_b = bl("mu_b", self.mu, 1664)
        w0_b = bl("w0_b", self.w0, 512); a0_b = bl("a0_b", self.a0, 512)
        kk_b = bl("kk_b", self.k_k, 512); ka_b = bl("ka_b", self.k_a, 512)
        rk_b = bl("rk_b", self.r_k, 512)
        lg_b = bl("lg_b", self.lnx_g, 512); lb_b = bl("lb_b", self.lnx_b, 512)
        omka = P.sb("omka", [128, 512], F32)
        P.dve(lambda e: e.tensor_scalar(out=omka[:], in0=ka_b[:], scalar1=-1.0, scalar2=1.0, op0=ALU.mult, op1=ALU.add),
              r=[ka_b], w=[omka])
        wup = P.sb("wup", [128, 512], BF16)
        P.dma("pool", wup[0:64, :], self.w_up, w=[wup], sres=wup)
        P.dma("pool", wup[64:128, :], self.a_up, w=[wup], sres=wup)
        onesc = P.sb("onesc", [128, 1], BF16)
        shi = P.sb("shi", [128, 512], BF16)
        slo = P.sb("slo", [128, 512], BF16)
        P.pool(lambda e: e.memset(onesc[:], 1.0), w=[onesc])
        tiny = P.sb("tiny", [128, 1], F32)
        identb3 = self.ident
        Pp = P.sb("Pp", [128, 1664], F32)
        prev = P.sb("prev", [128, 1664], F32)
        lastrow = P.sb("lastrow", [1, 1664], F32)
        gs = P.sb("gs", [128, 512], BF16)
        tl = P.sb("tl", [128, 128], BF16)
        tlT = P.sb("tlT", [128, 128], BF16)
        sigw = P.sb("sigw", [128, 512], F32)
        asig = P.sb("asig", [128, 512], F32)
        tA = P.sb("tA", [128, 512], F32)
        tB = P.sb("tB", [128, 512], F32)
        kkn = P.sb("kkn", [128, 512], F32)
        kmod = P.sb("kmod", [128, 512], F32)
        bvec = P.sb("bvec", [128, 512], F32)
        s8 = P.sb("s8", [128, 8], F32)
        bs8 = P.sb("bs8", [128, 8], F32)
        Ec = P.sb("Ec", [128, 512], F32); Ex = Ec
        En = P.sb("En", [128, 512], F32); Er = En
        rt = P.sb("rt", [128, 512], BF16); at = P.sb("at", [128, 512], BF16)
        kt = P.sb("kt", [128, 512], BF16); bt = P.sb("bt", [128, 512], BF16)
        kh = P.sb("kh", [128, 512], BF16); bh = P.sb("bh", [128, 512], BF16)
        vb = P.sb("vb", [128, 512], BF16)
        AR = P.sb("AR", [128, 4, 2, 128], BF16)
        BKz = P.sb("BKz", [128, 8, 2, 128], BF16)
        Hbz = P.sb("Hbz", [128, 8, 64], BF16)
        XL = P.sb("XL", [128, 8, 384], BF16)
        AK = P.sb("AK", [128, 8, 256], BF16)
        ML = [P.sb(f"ML{i}", [128, 8, 256], BF16) for i in range(2)]
        Q = [P.sb(f"Q{i}", [128, 8, 128], BF16) for i in range(2)]
        XLr = P.sub(XL, 8); AKr = P.sub(AK, 4)
        MLr = [P.sub(ML[0], 4), P.sub(ML[1], 4)]
        Qr = [P.sub(Q[0], 2), P.sub(Q[1], 2)]
        Wb = P.sb("Wb_", [128, 512], BF16)
        P.pool(lambda e: e.memset(BKz[:], 0.0), w=[BKz])
        P.pool(lambda e: e.memset(Hbz[:], 0.0), w=[Hbz])
        Uv = P.sb("Uv", [128, 512], F32)
        AhT = P.sb("AhT", [128, 4, 128], BF16)
        AhTo = P.sb("AhTo", [128, 4, 128], BF16)
        GC = P.sb("GC", [128, 4], F32)
        Hs = P.sb("Hs", [128, 4, 64], F32)
        Hb = P.sb("Hb", [128, 4, 64], BF16)
        Hlo = P.sb("Hlo", [128, 4, 64], BF16)
        Ub = P.sb("Ub", [128, 512], BF16)
        yf = tA
        ysq = tB
        m8 = P.sb("m8", [128, 8], F32); v8 = P.sb("v8", [128, 8], F32)
        yo = [Wb]
        S0v = Uv[0:64, :].rearrange("i (h j) -> i h j", h=8)
        Sov = Ec[0:64, :].rearrange("i (h j) -> i h j", h=8)
        mask3 = self.maskf
        mask2b = self.maskf[:, 0:256].unsqueeze(1).broadcast_to([128, 2, 256])
        P.pool(lambda e: e.memset(tiny[:], 1e-12), w=[tiny])

        def g8(t):
            return t[:, :].rearrange("p (g d) -> p g d", g=8)

        def b8(t):
            return t[:, :].unsqueeze(2).broadcast_to([128, 8, 64])

        def refresh_hbz(extra_r=()):
            P.act(lambda e: e.copy(out=Hbz[0:64, 0:8:2, :], in_=Hs[0:64, :, :]), r=[Hs] + list(extra_r), w=[Hbz])
            P.dve(lambda e: e.tensor_copy(out=Hbz[64:128, 1:8:2, :], in_=Hs[64:128, :, :]), r=[Hs] + list(extra_r), w=[Hbz])

        def store_state(dst, soi):
            P.act(lambda e: e.copy(out=Hb[:], in_=Hs[:]), r=[Hs], w=[Hb])
            P.dve(lambda e: e.tensor_tensor(out=Hlo[:], in0=Hs[:], in1=Hb[:], op=ALU.subtract), r=[Hs, Hb], w=[Hlo])
            bk = nb()
            bkb = bk[:, :].bitcast(BF16)
            for j, src in enumerate((Hb, Hlo)):
                for p in range(4):
                    P.pe(lambda e, bkb=bkb, p=p, j=j, src=src: e.transpose(
                        out=bkb[0:64, j * 512 + p * 128:j * 512 + (p + 1) * 128], in_=src[:, p, :], identity=self.ident[:]),
                        r=[src, self.ident], w=[bk])
            P.act(lambda e, bkb=bkb: e.copy(out=Ec[0:64, :], in_=bkb[0:64, 0:512]), r=[bk], w=[Ec])
            P.dve(lambda e, bkb=bkb: e.tensor_tensor(out=Ec[0:64, :], in0=Ec[0:64, :], in1=bkb[0:64, 512:1024], op=ALU.add),
                  r=[bk, Ec], w=[Ec])
            P.dma("pool", dst.rearrange("h i j -> i h j"), Sov, r=[Ec], w=[self.dOut], sres=Ec)

        yc = 0
        soi = 0
        A2STOP = CFG.get('a2stop', 99)
        for tt in CFG.get('tiles', range(NTT)):
            sample = tt >= NPT
            s = tt - NPT
            if tt == 0:
                P.pool(lambda e: e.memset(Hs[:], 0.0), w=[Hs])
                P.pool(lambda e: e.memset(lastrow[:], 0.0), w=[lastrow])
            if sample:
                P.dma("sp", S0v, self.wkv0[s].rearrange("h i j -> i h j"), w=[Uv], sres=Uv)
                P.act(lambda e: e.copy(out=shi[0:64, :], in_=Uv[0:64, :]), r=[Uv], w=[shi])
                P.dve(lambda e: e.tensor_tensor(out=slo[0:64, :], in0=Uv[0:64, :], in1=shi[0:64, :], op=ALU.subtract),
                      r=[Uv, shi], w=[slo])
                bk = nb()
                bkb = bk[:, :].bitcast(BF16)
                for j, src in enumerate((shi, slo)):
                    for p in range(4):
                        P.pe(lambda e, bkb=bkb, p=p, j=j, src=src: e.transpose(
                            out=bkb[:, j * 256 + p * 64:j * 256 + (p + 1) * 64], in_=src[0:64, p * 128:(p + 1) * 128],
                            identity=self.ident[0:64, 0:64]), r=[src, self.ident], w=[bk])
                P.act(lambda e, bkb=bkb: e.copy(out=Hs[:, :, :].rearrange("p a b -> p (a b)"), in_=bkb[:, 0:256]),
                      r=[bk], w=[Hs])
                P.dve(lambda e, bkb=bkb: e.tensor_tensor(out=Hs[:, :, :].rearrange("p a b -> p (a b)"),
                                                         in0=Hs[:, :, :].rearrange("p a b -> p (a b)"), in1=bkb[:, 256:512],
                                                         op=ALU.add), r=[bk, Hs], w=[Hs])
                refresh_hbz()
                P.dma("sp", lastrow[:], self.shift0[s:s + 1, :], w=[lastrow], sres=lastrow)
            XT = self.xnorm_tile(tt)
            for cb, (c0, cw) in enumerate(((0, 512), (512, 512), (1024, 512), (1536, 128), (1664, 512))):
                py = nb()
                for k in range(16):
                    P.pe(lambda e, py=py, k=k, c0=c0, cw=cw, XT=XT: e.matmul(
                        py[:, 0:cw], lhsT=XT[:, k, :], rhs=W[:, k, c0:c0 + cw], start=(k == 0), stop=(k == 15)),
                        r=[XT, W], w=[py])
                if cb < 4:
                    P.act(lambda e, py=py, c0=c0, cw=cw: e.copy(out=Pp[:, c0:c0 + cw], in_=py[:, 0:cw]), r=[py], w=[Pp])
                else:
                    P.act(lambda e, py=py: e.activation(out=gs[:], in_=py[:], func=AF.Silu), r=[py], w=[gs])
            if A2STOP <= 1:
                continue
            P.dma("sp", prev[1:128, :], Pp[0:127, :], r=[Pp], w=[prev], sres=prev)
            P.dma("sp", prev[0:1, :], lastrow[:], r=[lastrow], w=[prev], sres=prev)
            if not sample:
                if tt == NPT - 1:
                    P.dma("pool", self.shift_p.rearrange("(a c) -> a c", a=1), Pp[127:128, :], r=[Pp], w=[self.dOut], sres=Pp)
                else:
                    P.dma("sp", lastrow[:], Pp[127:128, :], r=[Pp], w=[lastrow], sres=lastrow)
            else:
                P.dma("pool", self.shift_s[s:s + 1, :], Pp[DEC - 1:DEC, :], r=[Pp], w=[self.dOut], sres=Pp)
            P.pool(lambda e: e.tensor_tensor(out=prev[:], in0=prev[:], in1=Pp[:], op=ALU.subtract), r=[prev, Pp], w=[prev])
            P.pool(lambda e: e.tensor_tensor(out=prev[:], in0=prev[:], in1=mu_b[:], op=ALU.mult), r=[prev, mu_b], w=[prev])
            P.pool(lambda e: e.tensor_tensor(out=prev[:], in0=prev[:], in1=Pp[:], op=ALU.add), r=[prev, Pp], w=[prev])
            xr = prev[:, 0:512]; xk = prev[:, 512:1024]; xv = prev[:, 1024:1536]
            if A2STOP <= 2:
                continue
            P.act(lambda e: e.activation(out=tl[:, 0:64], in_=prev[:, 1536:1600], func=AF.Tanh), r=[prev], w=[tl])
            P.act(lambda e: e.copy(out=tl[:, 64:128], in_=prev[:, 1600:1664]), r=[prev], w=[tl])
            bk = nb()
            P.pe(lambda e, bk=bk: e.transpose(out=bk[:, 0:64].bitcast(BF16), in_=tl[:], identity=self.ident[:]),
                 r=[tl, self.ident], w=[bk])
            P.act(lambda e, bk=bk: e.copy(out=tlT[:], in_=bk[:, 0:64].bitcast(BF16)), r=[bk], w=[tlT])
            bz = nb()
            P.pe(lambda e, bz=bz: e.matmul(bz[:], lhsT=tlT[0:64, :], rhs=wup[0:64, :], start=True, stop=True),
                 r=[tlT, wup], w=[bz])
            P.dve(lambda e, bz=bz: e.tensor_tensor(out=tA[:], in0=bz[:], in1=w0_b[:], op=ALU.add), r=[bz, w0_b], w=[tA])
            P.act(lambda e: e.activation(out=sigw[:], in_=tA[:], func=AF.Sigmoid), r=[tA], w=[sigw])
            bz2 = nb()
            P.pe(lambda e, bz2=bz2: e.matmul(bz2[:], lhsT=tlT[64:128, :], rhs=wup[64:128, :], start=True, stop=True),
                 r=[tlT, wup], w=[bz2])
            P.dve(lambda e, bz2=bz2: e.tensor_tensor(out=tB[:], in0=bz2[:], in1=a0_b[:], op=ALU.add), r=[bz2, a0_b], w=[tB])
            P.act(lambda e: e.activation(out=asig[:], in_=tB[:], func=AF.Sigmoid), r=[tB], w=[asig])
            if sample:
                P.pool(lambda e: e.memset(sigw[32:64, :], 0.0), w=[sigw])
                P.pool(lambda e: e.memset(sigw[64:128, :], 0.0), w=[sigw])
            if A2STOP <= 3:
                continue
            P.act(lambda e: e.copy(out=shi[:], in_=sigw[:]), r=[sigw], w=[shi])
            P.dve(lambda e: e.tensor_tensor(out=slo[:], in0=sigw[:], in1=shi[:], op=ALU.subtract), r=[sigw, shi], w=[slo])
            bc = nb(); bx = nb(); br = nb()
            for bnk, mc in ((bc, 128), (bx, 0), (br, 256)):
                P.pe(lambda e, bnk=bnk, mc=mc: e.matmul(bnk[:], lhsT=self.maskb[:, mc:mc + 128], rhs=shi[:], start=True, stop=False),
                     r=[self.maskb, shi], w=[bnk])
                P.pe(lambda e, bnk=bnk, mc=mc: e.matmul(bnk[:], lhsT=self.maskb[:, mc:mc + 128], rhs=slo[:], start=False, stop=True),
                     r=[self.maskb, slo], w=[bnk])
            P.act(lambda e, bc=bc: e.activation(out=Ec[:], in_=bc[:], func=AF.Exp, scale=-C0), r=[bc], w=[Ec])
            P.act(lambda e, bc=bc: e.activation(out=En[:], in_=bc[:], func=AF.Exp, scale=C0), r=[bc], w=[En])
            bg = nb()
            for p in range(4):
                P.pe(lambda e, bg=bg, p=p: e.matmul(bg[:, p:p + 1], lhsT=shi[:, p * 128:(p + 1) * 128], rhs=onesc[:],
                                                    start=(p == 0), stop=False, skip_group_check=True), r=[shi, onesc], w=[bg])
                P.pe(lambda e, bg=bg, p=p: e.matmul(bg[:, p:p + 1], lhsT=slo[:, p * 128:(p + 1) * 128], rhs=onesc[:],
                                                    start=False, stop=True, skip_group_check=True), r=[slo, onesc], w=[bg])
            P.act(lambda e, bg=bg: e.activation(out=GC[:], in_=bg[:, 0:4], func=AF.Exp, scale=-C0), r=[bg], w=[GC])
            if A2STOP <= 4:
                continue
            P.dve(lambda e: e.tensor_tensor(out=tA[:], in0=xk, in1=kk_b[:], op=ALU.mult), r=[prev, kk_b], w=[tA])
            P.act(lambda e: e.activation(out=tB[:], in_=tA[:], func=AF.Square), r=[tA], w=[tB])
            P.dve(lambda e: e.reduce_sum(out=s8[:], in_=g8(tB), axis=AX.X), r=[tB], w=[s8])
            P.act(lambda e: e.activation(out=s8[:], in_=s8[:], func=AF.Sqrt), r=[s8], w=[s8])
            P.dve(lambda e: e.tensor_scalar(out=s8[:], in0=s8[:], scalar1=tiny[:, 0:1], scalar2=None, op0=ALU.max),
                  r=[s8, tiny], w=[s8])
            P.dve(lambda e: e.reciprocal(out=s8[:], in_=s8[:]), r=[s8], w=[s8])
            P.dve(lambda e: e.tensor_tensor(out=g8(kkn), in0=g8(tA), in1=b8(s8), op=ALU.mult), r=[tA, s8], w=[kkn])
            P.pool(lambda e: e.tensor_tensor(out=tB[:], in0=asig[:], in1=ka_b[:], op=ALU.mult), r=[asig, ka_b], w=[tB])
            P.pool(lambda e: e.tensor_tensor(out=tB[:], in0=tB[:], in1=omka[:], op=ALU.add), r=[tB, omka], w=[tB])
            P.pool(lambda e: e.tensor_tensor(out=kmod[:], in0=xk, in1=tB[:], op=ALU.mult), r=[prev, tB], w=[kmod])
            P.dve(lambda e: e.tensor_tensor(out=bvec[:], in0=kkn[:], in1=asig[:], op=ALU.mult), r=[kkn, asig], w=[bvec])
            P.pool(lambda e: e.tensor_tensor(out=tA[:], in0=xr, in1=kmod[:], op=ALU.mult), r=[prev, kmod, kkn], w=[tA])
            P.pool(lambda e: e.tensor_tensor(out=tA[:], in0=tA[:], in1=rk_b[:], op=ALU.mult), r=[tA, rk_b], w=[tA])
            P.dve(lambda e: e.reduce_sum(out=bs8[:], in_=g8(tA), axis=AX.X), r=[tA], w=[bs8])
            P.dve(lambda e: e.tensor_tensor(out=rt[:], in0=xr, in1=Ec[:], op=ALU.mult), r=[prev, Ec], w=[rt])
            P.act(lambda e, bx=bx: e.activation(out=Ex[:], in_=bx[:], func=AF.Exp, scale=-C0), r=[bx], w=[Ex])
            P.dve(lambda e: e.scalar_tensor_tensor(out=at[:], in0=kkn[:], scalar=-1.0, in1=Ex[:], op0=ALU.mult, op1=ALU.mult),
                  r=[kkn, Ex], w=[at])
            P.dve(lambda e: e.tensor_tensor(out=kt[:], in0=kmod[:], in1=En[:], op=ALU.mult), r=[kmod, En], w=[kt])
            P.pool(lambda e: e.tensor_tensor(out=bt[:], in0=bvec[:], in1=En[:], op=ALU.mult), r=[bvec, En], w=[bt])
            P.act(lambda e, br=br: e.activation(out=Er[:], in_=br[:], func=AF.Exp, scale=-C0), r=[br], w=[Er])
            P.pool(lambda e: e.tensor_tensor(out=kh[:], in0=kmod[:], in1=Er[:], op=ALU.mult), r=[kmod, Er], w=[kh])
            P.pool(lambda e: e.tensor_tensor(out=bh[:], in0=bvec[:], in1=Er[:], op=ALU.mult), r=[bvec, Er], w=[bh])
            P.act(lambda e: e.copy(out=vb[:], in_=xv), r=[prev], w=[vb])
            if sample:
                for t_ in (kt, bt, kh, bh):
                    P.pool(lambda e, t_=t_: e.memset(t_[32:64, :], 0.0), w=[t_])
                    P.pool(lambda e, t_=t_: e.memset(t_[64:128, :], 0.0), w=[t_])
            if A2STOP <= 5:
                continue
            for dst, (o0, o1) in ((AR, (at, rt)), (BKz, (bt, kt))):
                bk = nb()
                bkb = bk[:, :].bitcast(BF16).rearrange("p (a b c) -> p a b c", a=4, b=2)
                for p in range(4):
                    for j, o in enumerate((o0, o1)):
                        P.pe(lambda e, bkb=bkb, p=p, j=j, o=o: e.transpose(
                            out=bkb[:, p, j, :], in_=o[:, p * 128:(p + 1) * 128], identity=self.ident[:]),
                            r=[o, self.ident], w=[bk])
                if dst is AR:
                    P.act(lambda e, dst=dst, bkb=bkb: e.copy(out=dst[:], in_=bkb), r=[bk], w=[dst])
                else:
                    P.act(lambda e, bkb=bkb: e.copy(out=BKz[0:64, 0:8:2, :, :], in_=bkb[0:64, :, :, :]), r=[bk], w=[BKz])
                    P.act(lambda e, bkb=bkb: e.copy(out=BKz[64:128, 1:8:2, :, :], in_=bkb[64:128, :, :, :]), r=[bk], w=[BKz])
            if A2STOP <= 6:
                continue
            for h in range(8):
                p, hp = h // 2, h % 2
                rows = slice(64 * hp, 64 * hp + 64)
                ba = nb()
                P.pe(lambda e, ba=ba, p=p, h=h: e.matmul(
                    ba[:, 0:256], lhsT=BKz[:, h, 0, :], rhs=AR[:, p, :, :].rearrange("k a t -> k (a t)"),
                    start=True, stop=True), r=[BKz, AR], w=[ba])
                P.pe(lambda e, ba=ba, p=p, h=h: e.matmul(
                    ba[:, 256:384], lhsT=AR[:, p, 0, :], rhs=BKz[:, h, 0, :], start=True, stop=True),
                    r=[BKz, AR], w=[ba])
                P.dve(lambda e, ba=ba, h=h: e.tensor_tensor(out=XL[:, h, :], in0=ba[:, 0:384], in1=mask3[:, :], op=ALU.mult),
                      r=[ba, mask3], w=[XLr[h]])
                if hp == 0:
                    bb = nb()
                P.pe(lambda e, bb=bb, p=p, h=h, hp=hp: e.matmul(
                    bb[:, hp * 256:(hp + 1) * 256], lhsT=BKz[:, h, 1, :],
                    rhs=AR[:, p, :, :].rearrange("k a t -> k (a t)"), start=True, stop=True), r=[BKz, AR], w=[bb])
                if hp == 1:
                    P.dve(lambda e, bb=bb, h=h: e.tensor_tensor(
                        out=AK[:, h - 1:h + 1, :], in0=bb[:, :].rearrange("p (a c) -> p a c", a=2), in1=mask2b, op=ALU.mult),
                        r=[bb, mask3], w=[AKr[p]])
            if A2STOP <= 7:
                continue
            P.dve(lambda e: e.tensor_tensor(out=Q[0][:], in0=XL[:, :, 0:128],
                                            in1=self.ident[:, :].unsqueeze(1).broadcast_to([128, 8, 128]), op=ALU.add),
                  r=XLr + [self.ident], w=Qr[0])
            qi = 0
            INV = CFG.get("invstop", 99)
            for lev in range(1, 7):
                if INV <= 0 or lev > CFG.get("invlev", 6):
                    break
                mlo = ML[lev % 2]
                mli = ML[(lev - 1) % 2]

                def Mprev(h):
                    return XL[:, h, 0:128] if lev == 1 else mli[:, h, 0:128]

                def Lprev(h):
                    return XL[:, h, 256:384] if lev == 1 else mli[:, h, 128:256]
                srcr = (lambda h: XLr[h]) if lev == 1 else (lambda h, lv=lev: MLr[(lv - 1) % 2][h // 2])
                mlor = MLr[lev % 2]
                if lev < 6:
                    for h2 in range(4):
                        bm = nb()
                        for j in range(2):
                            h = 2 * h2 + j
                            P.pe(lambda e, bm=bm, j=j, h=h, Mp=Mprev(h), Lp=Lprev(h): e.matmul(
                                bm[:, j * 256:j * 256 + 128], lhsT=Lp, rhs=Mp, start=True, stop=True), r=[srcr(h)], w=[bm])
                            P.pe(lambda e, bm=bm, j=j, h=h, Mp=Mprev(h), Lp=Lprev(h): e.matmul(
                                bm[:, j * 256 + 128:j * 256 + 256], lhsT=Mp, rhs=Lp, start=True, stop=True), r=[srcr(h)], w=[bm])
                        P.act(lambda e, bm=bm, h2=h2, mlo=mlo: e.copy(
                            out=mlo[:, 2 * h2:2 * h2 + 2, :].rearrange("p a c -> p (a c)"), in_=bm[:]), r=[bm], w=[mlor[h2]])
                    Lcur = lambda h: mlo[:, h, 128:256]
                else:
                    for h4 in range(2):
                        bm = nb()
                        for j in range(4):
                            h = 4 * h4 + j
                            P.pe(lambda e, bm=bm, j=j, Mp=Mprev(h), Lp=Lprev(h): e.matmul(
                                bm[:, j * 128:(j + 1) * 128], lhsT=Mp, rhs=Lp, start=True, stop=True), r=[srcr(h)], w=[bm])
                        P.act(lambda e, bm=bm, h4=h4, mlo=mlo: e.copy(
                            out=mlo[:, 4 * h4:4 * h4 + 4, 0:128], in_=bm[:, :].rearrange("p (a c) -> p a c", a=4)),
                            r=[bm], w=[mlor[2 * h4], mlor[2 * h4 + 1]])
                    Lcur = lambda h: mlo[:, h, 0:128]
                if INV <= 1:
                    continue
                qo, qn = Q[qi % 2], Q[(qi + 1) % 2]
                qor, qnr = Qr[qi % 2], Qr[(qi + 1) % 2]
                qi += 1
                for h4 in range(2):
                    bq = nb()
                    for j in range(4):
                        h = 4 * h4 + j
                        P.pe(lambda e, bq=bq, j=j, h=h, Lc=Lcur(h), qo=qo: e.matmul(
                            bq[:, j * 128:(j + 1) * 128], lhsT=Lc, rhs=qo[:, h, :], start=True, stop=True),
                            r=[mlor[h // 2], qor[h4]], w=[bq])
                    P.dve(lambda e, bq=bq, h4=h4, qo=qo, qn=qn: e.tensor_tensor(
                        out=qn[:, 4 * h4:4 * h4 + 4, :].rearrange("p a c -> p (a c)"), in0=bq[:],
                        in1=qo[:, 4 * h4:4 * h4 + 4, :].rearrange("p a c -> p (a c)"), op=ALU.add), r=[bq, qor[h4]], w=[qnr[h4]])
            PT = Q[qi % 2]
            PTr = Qr[qi % 2]
            if A2STOP <= 8:
                continue
            bw = nb()
            for h in range(8):
                P.pe(lambda e, bw=bw, h=h: e.matmul(bw[:, h * 64:(h + 1) * 64], lhsT=AK[:, h, 0:128],
                                                    rhs=vb[:, h * 64:(h + 1) * 64], start=True, stop=True),
                     r=[AKr[h // 2], vb], w=[bw])
            P.act(lambda e, bw=bw: e.copy(out=Wb[:], in_=bw[:]), r=[bw], w=[Wb])
            if CFG.get("s9", 99) <= 1:
                continue
            bu = nb()
            for h in range(8):
                P.pe(lambda e, bu=bu, h=h, PT=PT: e.matmul(bu[:, h * 64:(h + 1) * 64], lhsT=PT[:, h, :],
                                                           rhs=Wb[:, h * 64:(h + 1) * 64], start=True, stop=True),
                     r=[PTr[h // 4], Wb], w=[bu])
            P.act(lambda e, bu=bu: e.copy(out=Uv[:], in_=bu[:]), r=[bu], w=[Uv])
            if CFG.get("s9", 99) <= 2:
                continue
            bhe = nb(); bho = nb()
            for h in range(8):
                p, hp = h // 2, h % 2
                bb_ = bhe if hp == 0 else bho
                P.pe(lambda e, bb_=bb_, h=h, p=p, PT=PT: e.matmul(
                    bb_[:, p * 128:(p + 1) * 128], lhsT=(kt if CFG.get("va") else at)[:, p * 128:(p + 1) * 128], rhs=PT[:, h, :],
                    start=True, stop=True), r=[at, PTr[h // 4]], w=[bb_])
            P.act(lambda e, bhe=bhe: e.copy(out=AhT[:, :, :].rearrange("p a t -> p (a t)"), in_=bhe[:, :]),
                  r=[bhe], w=[AhT])
            P.act(lambda e, bho=bho: e.copy(out=AhTo[:, :, :].rearrange("p a t -> p (a t)"), in_=bho[:, :]),
                  r=[bho], w=[AhTo])
            if A2STOP <= 9:
                continue
            bU = nb()
            for h in range(8):
                p, hp = h // 2, h % 2
                rows = slice(64 * hp, 64 * hp + 64)
                Ah_ = AhT if hp == 0 else AhTo
                P.pe(lambda e, bU=bU, h=h, p=p, Ah_=Ah_: e.matmul(
                    bU[:, h * 64:(h + 1) * 64], lhsT=Ah_[:, p, :], rhs=Hbz[:, h, :], start=True, stop=True),
                    r=[Ah_, Hbz], w=[bU])
            P.dve(lambda e, bU=bU: e.tensor_tensor(out=Ub[:], in0=bU[:], in1=Uv[:], op=ALU.add), r=[bU, Uv], w=[Ub])
            bY = nb()
            for h in range(8):
                p, hp = h // 2, h % 2
                rows = slice(64 * hp, 64 * hp + 64)
                cs = slice(h * 64, (h + 1) * 64)
                P.pe(lambda e, bY=bY, h=h, p=p, rows=rows, cs=cs: e.matmul(
                    bY[:, cs], lhsT=AR[:, p, 1, :], rhs=Hbz[:, h, :], start=(h == 0), stop=False, skip_group_check=True),
                    r=[AR, Hbz], w=[bY])
                P.pe(lambda e, bY=bY, h=h, cs=cs: e.matmul(
                    bY[:, cs], lhsT=XL[:, h, 128:256], rhs=Ub[:, cs], start=False, stop=False, skip_group_check=True),
                    r=[XLr[h], Ub], w=[bY])
                P.pe(lambda e, bY=bY, h=h, cs=cs: e.matmul(
                    bY[:, cs], lhsT=AK[:, h, 128:256], rhs=vb[:, cs], start=False, stop=True, skip_group_check=True),
                    r=[AKr[h // 2], vb], w=[bY])
            if A2STOP <= 10:
                continue
            bHe = nb(); bHo = nb()
            for h in range(8):
                p, hp = h // 2, h % 2
                cs = slice(h * 64, (h + 1) * 64)
                pc = slice(p * 128, (p + 1) * 128)
                bb_ = bHe if hp == 0 else bHo
                o_ = bb_[:, p * 64:(p + 1) * 64]
                P.pe(lambda e, o_=o_, cs=cs, pc=pc, h=h: e.matmul(o_, lhsT=kh[:, pc], rhs=vb[:, cs], start=(h < 2), stop=False,
                                                                  skip_group_check=True), r=[kh, vb], w=[bb_])
                P.pe(lambda e, o_=o_, cs=cs, pc=pc: e.matmul(o_, lhsT=bh[:, pc], rhs=Ub[:, cs], start=False, stop=True,
                                                             skip_group_check=True), r=[bh, Ub], w=[bb_])
            P.dve(lambda e: e.tensor_tensor(out=Hs[:], in0=Hs[:], in1=GC[:, :].unsqueeze(2).broadcast_to([128, 4, 64]),
                                            op=ALU.mult), r=[Hs, GC], w=[Hs])
            P.dve(lambda e, bHe=bHe: e.tensor_tensor(out=Hs[0:64, :, :], in0=Hs[0:64, :, :],
                                                     in1=bHe[0:64, 0:256].rearrange("p (a b) -> p a b", a=4), op=ALU.add),
                  r=[Hs, bHe, bY], w=[Hs])
            P.dve(lambda e, bHo=bHo: e.tensor_tensor(out=Hs[64:128, :, :], in0=Hs[64:128, :, :],
                                                     in1=bHo[64:128, 0:256].rearrange("p (a b) -> p a b", a=4), op=ALU.add),
                  r=[Hs, bHo], w=[Hs])
            refresh_hbz()
            if A2STOP <= 11:
                continue
            P.act(lambda e, bY=bY: e.copy(out=yf[:], in_=bY[:]), r=[bY], w=[yf])
            P.dve(lambda e: e.reduce_sum(out=m8[:], in_=g8(yf), axis=AX.X), r=[yf], w=[m8])
            P.act(lambda e: e.activation(out=ysq[:], in_=yf[:], func=AF.Square), r=[yf], w=[ysq])
            P.dve(lambda e: e.reduce_sum(out=v8[:], in_=g8(ysq), axis=AX.X), r=[ysq], w=[v8])
            P.dve(lambda e: e.tensor_scalar(out=m8[:], in0=m8[:], scalar1=1.0 / 64, scalar2=None, op0=ALU.mult), r=[m8], w=[m8])
            P.dve(lambda e: e.tensor_tensor(out=s8[:], in0=m8[:], in1=m8[:], op=ALU.mult), r=[m8], w=[s8])
            P.dve(lambda e: e.scalar_tensor_tensor(out=v8[:], in0=v8[:], scalar=1.0 / 64, in1=s8[:], op0=ALU.mult,
                                                   op1=ALU.subtract), r=[v8, s8], w=[v8])
            P.act(lambda e: e.activation(out=v8[:], in_=v8[:], func=AF.Sqrt, bias=self.lnxeps[:, 0:1]),
                  r=[v8, self.lnxeps], w=[v8])
            P.dve(lambda e: e.reciprocal(out=v8[:], in_=v8[:]), r=[v8], w=[v8])
            P.dve(lambda e: e.tensor_tensor(out=g8(yf), in0=g8(yf), in1=b8(m8), op=ALU.subtract), r=[yf, m8], w=[yf])
            P.dve(lambda e: e.tensor_tensor(out=g8(yf), in0=g8(yf), in1=b8(v8), op=ALU.mult), r=[yf, v8], w=[yf])
            P.pool(lambda e: e.tensor_tensor(out=yf[:], in0=yf[:], in1=lg_b[:], op=ALU.mult), r=[yf, lg_b], w=[yf])
            P.pool(lambda e: e.tensor_tensor(out=yf[:], in0=yf[:], in1=lb_b[:], op=ALU.add), r=[yf, lb_b], w=[yf])
            P.pool(lambda e: e.tensor_tensor(out=g8(ysq), in0=g8(prev[:, 1024:1536]), in1=b8(bs8), op=ALU.mult),
                   r=[prev, bs8, ysq], w=[ysq])
            P.pool(lambda e: e.tensor_tensor(out=yf[:], in0=yf[:], in1=ysq[:], op=ALU.add), r=[yf, ysq], w=[yf])
            yot = yo[0]
            yc += 1
            P.dve(lambda e, yot=yot: e.tensor_tensor(out=yot[:], in0=yf[:], in1=gs[:], op=ALU.mult), r=[yf, gs], w=[yot])
            nr = DEC if sample else 128
            P.dma("pool", self.Mx[tt * 128:tt * 128 + nr, 512:1024], yot[0:nr, :], r=[yot], w=[self.dMx], sres=yot)
            if A2STOP <= 12:
                continue
            if tt == NPT - 1:
                store_state(self.wkv_p, soi); soi += 1
            if sample:
                store_state(self.wkv_s[s], soi); soi += 1
        self.end_phase()

    def phase_C(self):
        P = self.P
        P.flush()
        sem = P.sem("cc")
        CH = self.CCH
        n = 0
        for ci, t0 in enumerate(range(0, self.NTT, CH)):
            nt = min(CH, self.NTT - t0)
            ins = self.nc.gpsimd.collective_compute(
                "AllGather", ALU.bypass, replica_groups=[[0, 1], [2, 3], [4, 5], [6, 7]],
                ins=[self.Mx[t0 * 128:(t0 + nt) * 128, :]], outs=[self.Mall[ci][0:2 * nt * 128, :]])
            ins.then_inc(sem, 1)
            n += 1
        for eng in P.engs.values():
            eng.wait_ge(sem, n)

    def phase_D(self):
        P = self.P
        NTT, NPT = self.NTT, self.NPT
        R = NTT * 128
        self.begin_phase()
        Wo = P.sb("Wo", [128, 16, 1024], BF16)
        for k in range(16):
            P.dma("pool", Wo[:, k, :], self.w_out[k * 128:(k + 1) * 128, :], w=[Wo], sres=Wo)
        Mt = [P.sb(f"Mt{i}", [128, 2, 1024], BF16) for i in range(2)]
        MT = [P.sb(f"MT{i}", [128, 16, 128], BF16) for i in range(2)]
        xr = [P.sb(f"xr{i}", [128, 1024], F32) for i in range(2)]
        yo = [P.sb(f"yo{i}", [128, 1024], F32) for i in range(2)]
        pT = [P.ps(f"dpT{i}", [128, 8, 128], BF16) for i in range(2)]
        pY = [P.ps(f"dpY{i}", [128, 512], F32) for i in range(2)]
        yc = 0
        for tt in range(NTT):
            sample = tt >= NPT
            nr = DEC if sample else 128
            M = Mt[tt % 2]; T_ = MT[tt % 2]; X = xr[tt % 2]; Y = yo[tt % 2]
            if sample:
                P.pool(lambda e, M=M: e.memset(M[:], 0.0), w=[M])
            ci, tl_ = tt // self.CCH, tt % self.CCH
            ntc = min(self.CCH, NTT - ci * self.CCH)
            for rk in range(2):
                r0 = rk * ntc * 128 + tl_ * 128
                P.dma("sp", M[0:nr, rk, :], self.Mall[ci][r0:r0 + nr, :], r=[self.dMall], w=[M], sres=M)
            if sample:
                P.dma("sp", X[0:nr, :], self.xres_s[tt - NPT], w=[X], sres=X)
            else:
                P.dma("sp", X[:], self.xres_p[tt * 128:(tt + 1) * 128, :], w=[X], sres=X)
            for hf in range(2):
                pt = pT[hf]
                for j in range(8):
                    k = hf * 8 + j
                    P.pe(lambda e, pt=pt, j=j, k=k, M=M: e.transpose(
                        out=pt[:, j, :], in_=M[:, k // 8, (k % 8) * 128:(k % 8 + 1) * 128], identity=self.ident[:]),
                        r=[M, self.ident], w=[pt])
                if hf == 0:
                    P.act(lambda e, pt=pt, T_=T_: e.copy(out=T_[:, 0:8, :], in_=pt[:]), r=[pt], w=[T_])
                else:
                    P.dve(lambda e, pt=pt, T_=T_: e.tensor_copy(out=T_[:, 8:16, :], in_=pt[:]), r=[pt], w=[T_])
            for cb in range(2):
                py = pY[yc % 2]
                yc += 1
                for k in range(16):
                    P.pe(lambda e, py=py, k=k, cb=cb, T_=T_: e.matmul(
                        py[:], lhsT=T_[:, k, :], rhs=Wo[:, k, cb * 512:(cb + 1) * 512], start=(k == 0), stop=(k == 15)),
                        r=[T_, Wo], w=[py])
                P.dve(lambda e, py=py, cb=cb, X=X, Y=Y, nr=nr: e.tensor_tensor(
                    out=Y[0:nr, cb * 512:(cb + 1) * 512], in0=py[0:nr, :], in1=X[0:nr, cb * 512:(cb + 1) * 512], op=ALU.add),
                    r=[py, X], w=[Y])
            if sample:
                P.dma("pool", self.y_s[tt - NPT], Y[0:nr, :], r=[Y], w=[self.dOut], sres=Y)
            else:
                P.dma("pool", self.y_p[tt * 128:(tt + 1) * 128, :], Y[:], r=[Y], w=[self.dOut], sres=Y)
        self.end_phase()

    def phase_DBG(self):
        P = self.P
        t = P.sb("dbgt", [128, 1024], BF16)
        for tt in range(self.NTT):
            P.dma("sp", t[:], self.Mx[tt * 128:(tt + 1) * 128, :], r=[self.dMx], w=[t], sres=t)
            P.dma("sp", self.Mx_dbg[tt * 128:(tt + 1) * 128, :], t[:], r=[t], w=[self.dOut], sres=t)
        P.flush()

    def build(self):
        P = self.P
        self.setup()
        P.flush()
        phases = CFG["phases"].split(",")
        for ph in phases:
            if hasattr(self, "phase_" + ph):
                getattr(self, "phase_" + ph)()
        P.flush()
        self.stats = P.stats
        return self.nc


def _core_inputs(c, inp, NPT):
    b, hh = c // 2, c % 2
    T = NPT * 128
    f = lambda a: np.ascontiguousarray(a, dtype=np.float32)
    w_in = inp["w_in"][0]
    a = slice(hh * 512, hh * 512 + 512)
    att_cols = np.concatenate([np.arange(i * 1024 + hh * 512, i * 1024 + hh * 512 + 512) for i in range(4)])
    pb = 4096
    rw_cols = np.concatenate([np.arange(pb + i * 1024 + hh * 512, pb + i * 1024 + hh * 512 + 512) for i in range(3)]
                             + [np.arange(pb + 3072, pb + 3200)]
                             + [np.arange(pb + 3200 + hh * 512, pb + 3200 + hh * 512 + 512)])
    sh_cols = np.concatenate([np.arange(i * 1024 + hh * 512, i * 1024 + hh * 512 + 512) for i in range(3)]
                             + [np.arange(3072, 3200)])
    w_out = inp["w_out"][0]
    rows = np.concatenate([np.arange(0, 512), np.arange(1024, 1536), np.arange(512, 1024), np.arange(1536, 2048)])
    oc = slice(hh * 1024, hh * 1024 + 1024)
    sb = slice(4 * b, 4 * b + 4)
    d = {
        "xp": f(inp["x_prompt"][b, :T]),
        "xs": f(inp["x_sample"][sb]),
        "w_att": f(w_in[:, att_cols]),
        "w_rw": f(w_in[:, rw_cols]),
        "w_out": f(w_out[rows][:, oc]),
        "xres_p": f(inp["x_prompt"][b, :T, oc]),
        "xres_s": f(inp["x_sample"][sb, :, oc]),
        "norm_g": f(inp["norm_g"][0]),
        "qg": f(inp["q_norm_g"][0]), "kg": f(inp["k_norm_g"][0]),
        "lq1": f(inp["lambda_q1"][0]), "lk1": f(inp["lambda_k1"][0]),
        "lq2": f(inp["lambda_q2"][0]), "lk2": f(inp["lambda_k2"][0]),
        "subln": f(inp["subln_g"][0]),
        "mu": f(inp["shift_mu"][0][sh_cols]),
        "w0": f(inp["w0"][0][a]), "a0": f(inp["a0"][0][a]),
        "w_up": f(inp["w_up"][0][:, a]), "a_up": f(inp["a_up"][0][:, a]),
        "k_k": f(inp["k_k"][0][a]), "k_a": f(inp["k_a"][0][a]),
        "r_k": f(inp["r_k"][0][8 * hh:8 * hh + 8].reshape(512)),
        "lnx_g": f(inp["lnx_g"][0][a]), "lnx_b": f(inp["lnx_b"][0][a]),
        "cache_k": f(inp["cache_attn_k"][0, sb, :, 4 * hh:4 * hh + 4].reshape(4, PAST, 512)),
        "cache_v": f(inp["cache_attn_v"][0, sb, :, 4 * hh:4 * hh + 4].reshape(4, PAST, 512)),
        "wkv0": f(inp["state_rwkv_wkv"][0, sb, 8 * hh:8 * hh + 8]),
        "shift0": f(inp["state_rwkv_shift"][0, sb, 0][:, sh_cols]),
    }
    return d, sh_cols


_CACHE = {}


def run_device(inputs, NPT=None):
    NPT = CFG["NPT"] if NPT is None else NPT
    CFG["NPT"] = NPT
    key = (NPT, CFG["debug"], CFG["phases"])
    if key not in _CACHE:
        bld = Builder()
        nc = bld.build()
        _CACHE[key] = (nc, bld.stats)
    nc, stats = _CACHE[key]
    if CFG.get("debug"):
        print("CFG", {k: (v if not isinstance(v, (list, range)) else list(v)) for k, v in CFG.items()}, stats, flush=True)
    inp = {k: np.asarray(v) for k, v in inputs.items()}
    in_maps = []
    sh_cols = None
    for c in range(8):
        d, sh_cols = _core_inputs(c, inp, NPT)
        in_maps.append(d)
    ncores = CFG.get("ncores", 8)
    res = run_bass_kernel_spmd(nc, in_maps[:ncores], core_ids=list(range(ncores)))
    return res.results, sh_cols


def kernel(**inputs):
    NPT = CFG["NPT"]
    T = NPT * 128
    R, sh_cols = run_device(inputs, NPT)
    B = 4
    y_p = np.zeros((B, T, 2048), np.float32)
    y_s = np.zeros((16, DEC, 2048), np.float32)
    k_p = np.zeros((1, B, T, 8, 2, 64), np.float32)
    v_p = np.zeros((1, B, T, 8, 128), np.float32)
    wkv_p = np.zeros((1, B, 16, 64, 64), np.float32)
    sh_p = np.zeros((1, B, 1, 3200), np.float32)
    k_s = np.zeros((1, 16, DEC, 8, 2, 64), np.float32)
    v_s = np.zeros((1, 16, DEC, 8, 128), np.float32)
    wkv_s = np.zeros((1, 16, 16, 64, 64), np.float32)
    sh_s = np.zeros((1, 16, 1, 3200), np.float32)
    for c in range(8):
        b, hh = c // 2, c % 2
        r = R[c]
        sh_cols = np.concatenate([np.arange(i * 1024 + hh * 512, i * 1024 + hh * 512 + 512) for i in range(3)]
                                 + [np.arange(3072, 3200)])
        oc = slice(hh * 1024, hh * 1024 + 1024)
        sb = slice(4 * b, 4 * b + 4)
        y_p[b, :, oc] = r["y_p"]
        y_s[sb, :, oc] = r["y_s"]
        k_p[0, b, :, 4 * hh:4 * hh + 4] = r["k_p"].reshape(T, 4, 2, 64)
        v_p[0, b, :, 4 * hh:4 * hh + 4] = r["v_p"].reshape(T, 4, 128)
        wkv_p[0, b, 8 * hh:8 * hh + 8] = r["wkv_p"]
        sh_p[0, b, 0, sh_cols] = r["shift_p"]
        k_s[0, sb, :, 4 * hh:4 * hh + 4] = r["k_s"].reshape(4, DEC, 4, 2, 64)
        v_s[0, sb, :, 4 * hh:4 * hh + 4] = r["v_s"].reshape(4, DEC, 4, 128)
        wkv_s[0, sb, 8 * hh:8 * hh + 8] = r["wkv_s"]
        sh_s[0, sb, 0][:, sh_cols] = r["shift_s"]
    return (y_p, y_s, k_p, v_p, wkv_p, sh_p, k_s, v_s, wkv_s, sh_s)
```
